# Optimizing a Trainium2 kernel written in Bass

```python
import jax, jax.numpy as jnp
from jax import lax
import numpy as np

D_MODEL = 1024
BATCH = 4
SEQ = 8192
DEPTH = 2

CHUNK = 64
Q_BLOCK = 128
EPS = 1e-6

POOL_WIDTH = D_MODEL // 4
POOL_WINDOWS = (2, 4, 8, 16)
POOL_GROUPS = len(POOL_WINDOWS)
POOL_GROUP = POOL_WIDTH // POOL_GROUPS
GLA_WIDTH = D_MODEL // 4
GLA_HEADS = 4
GLA_HEAD_DIM = GLA_WIDTH // GLA_HEADS
GLA_GATE_RANK = 16
GLA_TAU = 16.0
FOX_WIDTH = D_MODEL - POOL_WIDTH - GLA_WIDTH
FOX_HEAD_DIM = 64
FOX_HEADS = FOX_WIDTH // FOX_HEAD_DIM
D_FF = ((-(-8 * D_MODEL // 3)) + 255) // 256 * 256

IN_SIZES = (POOL_WIDTH,
            GLA_WIDTH, GLA_WIDTH, GLA_WIDTH, GLA_WIDTH,
            GLA_GATE_RANK,
            FOX_WIDTH, FOX_WIDTH, FOX_WIDTH,
            FOX_HEADS)
IN_WIDTH = sum(IN_SIZES)

kernel_name = "hybrid_pool_gla_fox_block"


def _split_points():
    pts, acc = [], 0
    for s in IN_SIZES[:-1]:
        acc += s
        pts.append(acc)
    return pts


def rmsnorm(x, g):
    x32 = x.astype(jnp.float32)
    y = x32 * lax.rsqrt(jnp.mean(x32 * x32, axis=-1, keepdims=True) + EPS) * g.astype(jnp.float32)
    return y.astype(x.dtype)


def pool_mixer(u, w_pool, pool_scale):
    B, T, _ = u.shape
    u32 = u.astype(jnp.float32)
    cs = jnp.cumsum(u32, axis=1)
    cs_pad = jnp.concatenate([jnp.zeros((B, 1, POOL_WIDTH), jnp.float32), cs], axis=1)
    t1 = jnp.arange(1, T + 1)
    means = []
    for gi, w in enumerate(POOL_WINDOWS):
        lo, hi = gi * POOL_GROUP, (gi + 1) * POOL_GROUP
        lagged = jnp.concatenate([jnp.zeros((B, w - 1, POOL_GROUP), jnp.float32),
                                  cs_pad[:, :T - w + 1, lo:hi]], axis=1)
        cnt = jnp.minimum(t1, w).astype(jnp.float32)
        means.append((cs[:, :, lo:hi] - lagged) / cnt[None, :, None])
    d = (jnp.concatenate(means, axis=-1) - u32).astype(u.dtype)
    d = d.reshape(B, T, POOL_GROUPS, POOL_GROUP)
    y = jnp.einsum('btgc,gcd->btgd', d, w_pool).reshape(B, T, POOL_WIDTH)
    return y * pool_scale


def gla_mixer(q, k, v, g, a_low, w_a_up, b_a, gla_gn):
    B, T, _ = q.shape
    H, Dh, C = GLA_HEADS, GLA_HEAD_DIM, CHUNK
    NC = T // C

    def heads(z):
        return z.astype(jnp.float32).reshape(B, NC, C, H, Dh).transpose(0, 3, 1, 2, 4)

    log_a = jax.nn.log_sigmoid(jnp.einsum('btr,rd->btd', a_low.astype(jnp.float32),
                                          w_a_up.astype(jnp.float32)) + b_a.astype(jnp.float32)) / GLA_TAU
    qh = heads(q) * (Dh ** -0.5)
    kh, vh, la = heads(k), heads(v), heads(log_a)
    bcum = jnp.cumsum(la, axis=3)
    b_last = bcum[:, :, :, -1:, :]
    ref = bcum[:, :, :, C // 2 - 1:C // 2, :]

    q_in = qh * jnp.exp(bcum - ref)
    k_in = kh * jnp.exp(ref - bcum)
    causal = jnp.tril(jnp.ones((C, C), dtype=bool))
    att = jnp.where(causal, jnp.einsum('bhncd,bhnsd->bhncs', q_in, k_in), 0.0)
    o_intra = jnp.einsum('bhncs,bhnse->bhnce', att, vh)

    kv = jnp.einsum('bhncd,bhnce->bhnde', kh * jnp.exp(b_last - bcum), vh)
    dec = jnp.exp(b_last[:, :, :, 0, :])

    def step(S, inp):
        kv_n, dec_n = inp
        return dec_n[..., None] * S + kv_n, S

    S0 = jnp.zeros((B, H, Dh, Dh), jnp.float32)
    _, S_prev = lax.scan(step, S0, (kv.transpose(2, 0, 1, 3, 4), dec.transpose(2, 0, 1, 3)))
    S_prev = S_prev.transpose(1, 2, 0, 3, 4)
    o_inter = jnp.einsum('bhncd,bhnde->bhnce', qh * jnp.exp(bcum), S_prev)

    o = o_intra + o_inter
    o = o * lax.rsqrt(jnp.mean(o * o, axis=-1, keepdims=True) + EPS)
    o = o.transpose(0, 2, 3, 1, 4).reshape(B, T, GLA_WIDTH) * gla_gn.astype(jnp.float32)
    return (o * jax.nn.silu(g.astype(jnp.float32))).astype(q.dtype)


def fox_mixer(q, k, v, f_logit, b_f):
    B, T, _ = q.shape
    H, Dh = FOX_HEADS, FOX_HEAD_DIM
    nb = T // Q_BLOCK

    def heads(z):
        return z.astype(jnp.float32).reshape(B, T, H, Dh).transpose(0, 2, 1, 3)

    qh = heads(q) * (Dh ** -0.5)
    kh, vh = heads(k), heads(v)
    logf = jax.nn.log_sigmoid(f_logit.astype(jnp.float32) + b_f.astype(jnp.float32))
    F = jnp.cumsum(logf, axis=1).transpose(0, 2, 1)
    q_blocks = qh.reshape(B, H, nb, Q_BLOCK, Dh).transpose(2, 0, 1, 3, 4)
    F_blocks = F.reshape(B, H, nb, Q_BLOCK).transpose(2, 0, 1, 3)
    kpos = jnp.arange(T)

    def block(args):
        qb, Fb, i = args
        s = jnp.einsum('bhqd,bhkd->bhqk', qb, kh) + Fb[..., None] - F[:, :, None, :]
        qpos = i * Q_BLOCK + jnp.arange(Q_BLOCK)
        s = jnp.where(kpos[None, :] <= qpos[:, None], s, -jnp.inf)
        p = jax.nn.softmax(s, axis=-1)
        return jnp.einsum('bhqk,bhkd->bhqd', p, vh)

    o = lax.map(block, (q_blocks, F_blocks, jnp.arange(nb)))
    return o.transpose(1, 0, 3, 2, 4).reshape(B, T, FOX_WIDTH).astype(q.dtype)


def setup_inputs(seed: int = 0) -> dict:
    key = jax.random.key(seed)
    ks = jax.random.split(key, 16)
    f32 = jnp.float32
    nrm = lambda k, shape: jax.random.normal(k, shape, f32)
    return {
        "x": nrm(ks[0], (BATCH, SEQ, D_MODEL)),
        "ln1": 1.0 + 0.02 * nrm(ks[1], (DEPTH, D_MODEL)),
        "w_in": nrm(ks[2], (DEPTH, D_MODEL, IN_WIDTH)) * D_MODEL ** -0.5,
        "w_pool": nrm(ks[3], (DEPTH, POOL_GROUPS, POOL_GROUP, POOL_GROUP)) * POOL_GROUP ** -0.5,
        "pool_scale": 0.5 + 0.1 * nrm(ks[4], (DEPTH, POOL_WIDTH)),
        "w_a_up": nrm(ks[5], (DEPTH, GLA_GATE_RANK, GLA_WIDTH)) * GLA_GATE_RANK ** -0.5,
        "b_a": 0.1 * nrm(ks[6], (DEPTH, GLA_WIDTH)),
        "gla_gn": 1.0 + 0.02 * nrm(ks[7], (DEPTH, GLA_WIDTH)),
        "b_f": 3.0 + 0.5 * nrm(ks[8], (DEPTH, FOX_HEADS)),
        "w_o": nrm(ks[9], (DEPTH, D_MODEL, D_MODEL)) * D_MODEL ** -0.5,
        "ln2": 1.0 + 0.02 * nrm(ks[10], (DEPTH, D_MODEL)),
        "w_gu": nrm(ks[11], (DEPTH, D_MODEL, 2 * D_FF)) * D_MODEL ** -0.5,
        "w_down": nrm(ks[12], (DEPTH, D_FF, D_MODEL)) * D_FF ** -0.5,
        "ln_f": 1.0 + 0.02 * nrm(ks[13], (D_MODEL,)),
    }


def reference(x, ln1, w_in, w_pool, pool_scale, w_a_up, b_a, gla_gn, b_f, w_o, ln2, w_gu, w_down, ln_f):
    pts = _split_points()
    for l in range(DEPTH):
        h = rmsnorm(x, ln1[l])
        z = jnp.einsum('btd,de->bte', h, w_in[l])
        u_pool, gq, gk, gv, gg, ga, fq, fk, fv, ff = jnp.split(z, pts, axis=-1)
        y_pool = pool_mixer(u_pool, w_pool[l], pool_scale[l])
        y_gla = gla_mixer(gq, gk, gv, gg, ga, w_a_up[l], b_a[l], gla_gn[l])
        y_fox = fox_mixer(fq, fk, fv, ff, b_f[l])
        mix = jnp.concatenate([y_pool.astype(x.dtype), y_gla.astype(x.dtype), y_fox.astype(x.dtype)], axis=-1)
        x = x + jnp.einsum('btd,de->bte', mix, w_o[l])
        h = rmsnorm(x, ln2[l])
        gate, up = jnp.split(jnp.einsum('btd,df->btf', h, w_gu[l]), 2, axis=-1)
        x = x + jnp.einsum('btf,fd->btd', jax.nn.silu(gate) * up, w_down[l])
    return rmsnorm(x, ln_f)
```

```python
import contextlib
import numpy as np
import concourse.bass as bass
import concourse.mybir as mybir
from concourse.bass_utils import run_bass_kernel_spmd

F32 = mybir.dt.float32
BF16 = mybir.dt.bfloat16
AF = mybir.ActivationFunctionType
ALU = mybir.AluOpType

ENGS = ("pe", "act", "dve", "pool", "sp")

D = 1024
KC = 8
DEPTH = 2
TT = 512
NBLK = 4
DFF = 2816
NFC = 22
EPS = 1e-6
NEG = -30000.0
ST_ENG = "sp"
LN8 = float(np.log(8.0))

G_IN = [("pool", 256), ("g1", 512), ("g2", 512), ("gs", 24), ("fq", 512), ("fk", 512), ("fv", 512)]
GU_SIZES = [512, 512, 512, 512, 512, 256]


def _offsets():
    off = {}
    o = 0
    for nm, n in G_IN:
        off[nm] = (o, 8 * n, n)
        o += 8 * n
    for og in range(2):
        off["wo%d" % og] = (o, 8 * 512, 512)
        o += 8 * 512
    for g, n in enumerate(GU_SIZES):
        off["gate%d" % g] = (o, 8 * n, n)
        o += 8 * n
    for g, n in enumerate(GU_SIZES):
        off["up%d" % g] = (o, 8 * n, n)
        o += 8 * n
    for j in range(8):
        off["dn%d" % j] = (o, NFC * 128, 128)
        o += NFC * 128
    return off, o


WOFF, WTOT = _offsets()
PIECE = 2052
assert WTOT % PIECE == 0

NVL = 22
C_M0 = 0
C_MASK2 = 512
C_SEG = 640
C_IDENT = 1152
C_TRIU = 1280
C_SEL127 = 1408
C_BONES = 1536
C_INVW = 1664
C_INVCNT = 1666
NCONST = 1698


class Op:
    __slots__ = ("eng", "fn", "deps", "signal", "semval", "dma_sem", "dma_val", "is_dma", "tag")

    def __init__(self, eng, fn, is_dma):
        self.eng = eng
        self.fn = fn
        self.deps = []
        self.signal = False
        self.semval = None
        self.is_dma = is_dma
        self.dma_sem = None
        self.dma_val = None


class Prog:
    def __init__(self, nc):
        self.nc = nc
        self.ops = {e: [] for e in ENGS}
        self.last_writer = {}
        self.readers = {}
        self.dma_slots = {}
        self.n_dma_sems = 0
        self.stack = contextlib.ExitStack()
        self.rot = {}
        self.excl = set()
        self.tag = "setup"

    def sb(self, name, shape, dtype):
        return self.stack.enter_context(self.nc.sbuf_tensor(name, list(shape), dtype))

    def ps(self, name, shape, dtype=F32):
        return self.stack.enter_context(self.nc.psum_tensor(name, list(shape), dtype))

    def add(self, eng, fn, reads=(), writes=(), dma_slot=None):
        op = Op(eng, fn, dma_slot is not None)
        op.tag = self.tag
        deps = {}
        ex = [r for r in reads if r in self.excl]
        if ex:
            reads = [r for r in reads if r not in self.excl]
            writes = list(writes) + ex
        for r in reads:
            w = self.last_writer.get(r)
            if w is not None:
                deps[id(w)] = w
        for k in writes:
            w = self.last_writer.get(k)
            if w is not None:
                deps[id(w)] = w
            for rd in self.readers.get(k, ()):
                deps[id(rd)] = rd
        op.deps = list(deps.values())
        for r in reads:
            rl = self.readers.setdefault(r, [])
            if not op.is_dma:
                for idx in range(len(rl)):
                    if (not rl[idx].is_dma) and rl[idx].eng == eng:
                        rl.pop(idx)
                        break
            rl.append(op)
        for k in writes:
            self.last_writer[k] = op
            self.readers[k] = []
        if dma_slot is not None:
            if dma_slot not in self.dma_slots:
                self.dma_slots[dma_slot] = [self.n_dma_sems, 0]
                self.n_dma_sems += 1
            s = self.dma_slots[dma_slot]
            s[1] += 1
            op.dma_sem = s[0]
            op.dma_val = 16 * s[1]
        self.ops[eng].append(op)
        return op

    def emit(self):
        nc = self.nc
        for e in ENGS:
            for op in self.ops[e]:
                for d in op.deps:
                    if d.is_dma:
                        continue
                    if d.eng == "pe" and op.eng == "pe":
                        continue
                    d.signal = True
        for e in ENGS:
            c = 0
            for op in self.ops[e]:
                if op.signal and not op.is_dma:
                    c += 1
                    op.semval = c
        st = self.stack
        esem = {e: st.enter_context(nc.semaphore("s_" + e)) for e in ENGS if e != "sp"}
        dsem = [st.enter_context(nc.semaphore("d%d" % i)) for i in range(self.n_dma_sems)]
        block = st.enter_context(nc.Block())

        def run(ename, eng):
            waited = {}
            for op in self.ops[ename]:
                need = {}
                for d in op.deps:
                    if d.is_dma:
                        key = ("d", d.dma_sem)
                        val = d.dma_val
                    else:
                        if d.eng == "pe" and ename == "pe":
                            continue
                        key = ("e", d.eng)
                        val = d.semval
                    if val > need.get(key, 0):
                        need[key] = val
                for key, val in need.items():
                    if waited.get(key, 0) >= val:
                        continue
                    waited[key] = val
                    sem = dsem[key[1]] if key[0] == "d" else esem[key[1]]
                    eng.wait_ge(sem, val)
                ins = op.fn(eng)
                if op.is_dma:
                    ins.then_inc(dsem[op.dma_sem], 16)
                elif op.signal:
                    ins.then_inc(esem[ename], 1)
            fin = {}
            for op in self.ops[ename]:
                if op.is_dma:
                    fin[op.dma_sem] = max(fin.get(op.dma_sem, 0), op.dma_val)
            for s, v in fin.items():
                if waited.get(("d", s), 0) < v:
                    eng.wait_ge(dsem[s], v)

        @block.tensor
        def _(pe):
            run("pe", pe)

        @block.scalar
        def _(act):
            run("act", act)

        @block.vector
        def _(dve):
            run("dve", dve)

        @block.gpsimd
        def _(pool):
            run("pool", pool)

        @block.sync
        def _(sp):
            run("sp", sp)

    def close(self):
        self.stack.close()


class _Stop(Exception):
    pass


def build_program(T, depth=DEPTH, own_from=0, stop=None):
    NT = T // TT
    NBT = T // 128
    nc = bass.Bass("TRN2", target_bir_lowering=False)
    P = Prog(nc)
    add = P.add

    xT_d = nc.dram_tensor("xT", [D, T], F32, kind="ExternalInput").ap()
    wall_d = nc.dram_tensor("wall", [depth, 128, WTOT + 4], F32, kind="ExternalInput").ap()
    vecs_d = nc.dram_tensor("vecs", [128, NVL * depth + 8], F32, kind="ExternalInput").ap()
    wpool_d = nc.dram_tensor("wpool", [depth, 4, 64, 64], F32, kind="ExternalInput").ap()
    waup_d = nc.dram_tensor("waup", [depth, 16, 256], F32, kind="ExternalInput").ap()
    bf_d = nc.dram_tensor("bfb", [depth, 128, 32], F32, kind="ExternalInput").ap()
    const_d = nc.dram_tensor("consts", [128, NCONST], F32, kind="ExternalInput").ap()
    outT_d = nc.dram_tensor("outT", [D, T], F32, kind="ExternalOutput").ap()

    wbf_d = nc.dram_tensor("wbf", [depth, 128, WTOT], BF16).ap()
    kc_d = nc.dram_tensor("kcache", [4, 128, T], BF16).ap()
    vc_d = nc.dram_tensor("vcache", [4, 128, NBT, 130], BF16).ap()
    x1_d = nc.dram_tensor("x1T", [D, T], F32).ap()

    xT = P.sb("xTs", [128, KC, TT], F32)
    hT = P.sb("hT", [128, KC, TT], BF16)
    rstd = P.sb("rstd", [128, TT], F32)
    sq = [P.sb("sq%d" % i, [128, TT], BF16) for i in range(2)]
    wbuf = [P.sb("wbuf%d" % i, [128, 4096], BF16) for i in range(3)]
    mixT = P.sb("mixT", [128, KC, TT], BF16)
    actT = P.sb("actT", [128, NFC, TT], BF16)
    qT = P.sb("qT", [128, 4, TT], BF16)
    kT = P.sb("kT", [128, 4, TT], BF16)
    vtm = P.sb("vtm", [128, 4, NBLK, 130], BF16)
    kseg = [P.sb("kseg%d" % i, [128, 2048], BF16) for i in range(2)]
    vseg = [P.sb("vseg%d" % i, [128, 16, 130], BF16) for i in range(2)]
    PT = [P.sb("PT%d" % i, [128, TT], BF16) for i in range(3)]
    osb = [P.sb("osb%d" % i, [128, TT], F32) for i in range(2)]
    negF = P.sb("negF", [128, NBT * 8], F32)
    fbias = [P.sb("fbias%d" % i, [128, NBT], F32) for i in range(2)]
    nref = P.sb("nref", [128, 8], F32)
    fft = P.sb("fft", [128, 32], F32)
    spf = P.sb("spf", [128, 32], F32)
    carry = P.sb("carry", [1, 8], F32)
    bfb = P.sb("bfbs", [128, depth, 32], F32)
    gqT = P.sb("gqT", [128, 2, TT], F32)
    gkT = P.sb("gkT", [128, 2, TT], F32)
    sg = P.sb("sg", [128, 2, TT], BF16)
    gvraw = P.sb("gvraw", [128, NBLK, 256], BF16)
    gvpad = P.sb("gvpad", [128, NBLK, 2, 2, 128], BF16)
    gaT = P.sb("gaT", [16, TT], BF16)
    bp = [P.sb("bp%d" % i, [128, TT], F32) for i in range(2)]
    d1 = P.sb("d1", [128, TT], F32)
    d3 = P.sb("d3", [128, TT], F32)
    Et = [P.sb("Et%d" % i, [128, TT], F32) for i in range(2)]
    esb, spg = Et[0], Et[1]
    qin = [P.sb("qin%d" % i, [128, TT], BF16) for i in range(2)]
    kin = [P.sb("kin%d" % i, [128, 2, TT], BF16) for i in range(2)]
    kkv = [P.sb("kkv%d" % i, [128, TT], F32) for i in range(2)]
    qo = [P.sb("qo%d" % i, [128, TT], BF16) for i in range(2)]
    dec = [P.sb("dec%d" % i, [128, 8], F32) for i in range(2)]
    kktm = [P.sb("kktm%d" % i, [128, 2, NBLK, 128], BF16) for i in range(2)]
    attm = P.sb("attm", [128, 2, 128], BF16)
    S32 = P.sb("S32", [128, 2, 128], F32)
    Sbf = P.sb("Sbf", [128, 2, 2, 8, 128], BF16)
    osq = P.sb("osq", [128, TT], BF16)
    t1 = P.sb("t1", [128, TT], F32)
    rr = P.sb("rr", [128, TT], F32)
    sgate = Et
    wa_f = P.sb("wa_f", [16, depth, 256], F32)
    wa_b = P.sb("wa_b", [16, depth, 256], BF16)
    U = P.sb("U", [128, 2, 16 + TT], F32)
    s2 = P.sb("s2", [128, 16 + TT], F32)
    s4 = P.sb("s4", [128, 16 + TT], F32)
    s8 = s2
    s16 = s4
    dT = P.sb("dT", [128, 2, TT], BF16)
    wp_f = P.sb("wp_f", [128, depth, 2, 128], F32)
    wp_b = P.sb("wp_b", [128, depth, 2, 128], BF16)
    cst = P.sb("cst", [128, NCONST], F32)
    vec = P.sb("vec", [128, NVL * depth + 8], F32)
    negba = P.sb("negba", [128, depth, 2], F32)
    m0b = P.sb("m0b", [128, TT], BF16)
    mask2b = P.sb("mask2b", [128, 128], BF16)
    identb = P.sb("identb", [128, 128], BF16)
    onesb = P.sb("onesb", [128, 128], BF16)
    bonesb = P.sb("bonesb", [128, 128], BF16)
    onesf = P.sb("onesf", [128, 128], F32)

    banks = {}
    for nm in ["m0", "m1", "m2", "s0", "s1", "o0", "o1", "aux"]:
        banks[nm] = P.ps("ps_" + nm, [128, 512], F32)
    rot = {"m": 0, "s": 0, "o": 0, "w": 0, "pt": 0, "sq": 0, "sg": 0, "osb": 0, "et": 0, "kv": 0, "fb": 0}

    def mbank():
        rot["m"] = (rot["m"] + 1) % 3
        k = "m%d" % rot["m"]
        return banks[k], k

    def sbank():
        rot["s"] = (rot["s"] + 1) % 2
        k = "s%d" % rot["s"]
        return banks[k], k

    def obank():
        rot["o"] = (rot["o"] + 1) % 2
        k = "o%d" % rot["o"]
        return banks[k], k

    aux = banks["aux"]
    P.excl = set(banks.keys())

    def mm(out, lhsT, rhs, start, stop, reads, writes):
        add("pe", lambda e: e.matmul(out, lhsT=lhsT, rhs=rhs, start=start, stop=stop), reads, writes)

    def cvec(col):
        return vec[:, col:col + 1]

    add("sp", lambda e: e.dma_start(out=cst[:], in_=const_d[:, :]), writes=["cst"], dma_slot="cst")
    add("sp", lambda e: e.dma_start(out=vec[:], in_=vecs_d[:, :]), writes=["vec"], dma_slot="vec")
    add("sp", lambda e: e.dma_start(out=bfb[:], in_=bf_d.rearrange("l p c -> p l c")), writes=["bfb"], dma_slot="bfb")
    add("sp", lambda e: e.dma_start(out=wa_f[:], in_=waup_d.rearrange("l r c -> r l c")), writes=["wa_f"], dma_slot="wa_f")
    add("pool", lambda e: e.memset(wp_f[:], 0.0), writes=["wp_f"])
    for l in range(depth):
        for g in range(4):
            j, hh = g // 2, g % 2
            add("sp", lambda e, l=l, g=g, j=j, hh=hh: e.dma_start(
                out=wp_f[hh * 64:(hh + 1) * 64, l, j, hh * 64:(hh + 1) * 64], in_=wpool_d[l, g, :, :]),
                reads=[], writes=["wp_f"], dma_slot="wp%d_%d" % (l, g))
    add("dve", lambda e: e.tensor_copy(out=wp_b[:], in_=wp_f[:]), reads=["wp_f"], writes=["wp_b"])
    add("dve", lambda e: e.tensor_copy(out=wa_b[:], in_=wa_f[:]), reads=["wa_f"], writes=["wa_b"])
    add("dve", lambda e: e.tensor_copy(out=m0b[:], in_=cst[:, C_M0:C_M0 + 512]), reads=["cst"], writes=["m0b"])
    add("dve", lambda e: e.tensor_copy(out=mask2b[:], in_=cst[:, C_MASK2:C_MASK2 + 128]), reads=["cst"], writes=["mask2b"])
    add("dve", lambda e: e.tensor_copy(out=identb[:], in_=cst[:, C_IDENT:C_IDENT + 128]), reads=["cst"], writes=["identb"])
    add("dve", lambda e: e.tensor_copy(out=bonesb[:], in_=cst[:, C_BONES:C_BONES + 128]), reads=["cst"], writes=["bonesb"])
    add("pool", lambda e: e.memset(onesb[:], 1.0), writes=["onesb"])
    add("pool", lambda e: e.memset(onesf[:], 1.0), writes=["onesf"])
    add("pool", lambda e: e.memset(vtm[:], 1.0), writes=["vtm"])
    add("pool", lambda e: e.memset(gvpad[:], 0.0), writes=["gvpad"])
    for pc_ in range(2):
        add("pool", lambda e, pc_=pc_: e.memset(kin[pc_][:], 0.0), writes=["kin_%d" % pc_])
        add("pool", lambda e, pc_=pc_: e.memset(kktm[pc_][:], 0.0), writes=["kktm_%d" % pc_])
    for l in range(depth):
        add("dve", lambda e, l=l: e.tensor_scalar(out=negba[:, l, :], in0=vec[:, NVL * l + 18:NVL * l + 20],
                                                  scalar1=-1.0, scalar2=None, op0=ALU.mult),
            reads=["vec"], writes=["negba"])
    identf = cst[:, C_IDENT:C_IDENT + 128]
    triU = cst[:, C_TRIU:C_TRIU + 128]
    sel127 = cst[:, C_SEL127:C_SEL127 + 128]
    segm = cst[:, C_SEG:C_SEG + 512]

    actT_f = actT[:].rearrange("p a b -> p (a b)").bitcast(F32)
    yout = actT_f[:, 0:KC * TT].rearrange("p (c t) -> p c t", c=KC)
    npiece = WTOT // PIECE
    cv = 0
    for l in range(depth):
        for pi in range(npiece):
            s = cv % 2
            stg = actT_f[:, s * PIECE:(s + 1) * PIECE]
            ob = wbuf[s][:, 0:PIECE]
            add("sp", lambda e, l=l, pi=pi, stg=stg: e.dma_start(out=stg, in_=wall_d[l, :, pi * PIECE:(pi + 1) * PIECE]),
                writes=["stg%d" % s], dma_slot="stg%d" % s)
            eng = ("dve", "act", "pool")[cv % 3]
            if eng == "act":
                add("act", lambda e, stg=stg, ob=ob: e.copy(out=ob, in_=stg), reads=["stg%d" % s], writes=["wbuf%d" % s])
            else:
                add(eng, lambda e, stg=stg, ob=ob: e.tensor_copy(out=ob, in_=stg), reads=["stg%d" % s], writes=["wbuf%d" % s])
            add(ST_ENG, lambda e, l=l, pi=pi, ob=ob: e.dma_start(out=wbf_d[l, :, pi * PIECE:(pi + 1) * PIECE], in_=ob),
                reads=["wbuf%d" % s], writes=["wbf%d_%d" % (l, pi)], dma_slot="cvo%d" % s)
            cv += 1
    stg_keys = ["stg0", "stg1"]

    def wload(l, name):
        off, n, ncol = WOFF[name]
        rot["w"] = (rot["w"] + 1) % 3
        s = rot["w"]
        key = "wbuf%d" % s
        buf = wbuf[s]
        pkeys = ["wbf%d_%d" % (l, pi) for pi in range(off // PIECE, (off + n - 1) // PIECE + 1)]
        add("sp", lambda e: e.dma_start(out=buf[:, 0:n], in_=wbf_d[l, :, off:off + n]),
            reads=pkeys, writes=[key], dma_slot=key)
        if name.startswith("dn"):
            view = buf[:, 0:n].rearrange("p (k c) -> p k c", k=NFC)
        else:
            view = buf[:, 0:n].rearrange("p (k c) -> p k c", k=KC)
        return view, key

    def norm(gcol0, out_is_h=True, final=False):
        for c in range(KC):
            rot["sq"] = (rot["sq"] + 1) % 2
            s = rot["sq"]
            if c % 2 == 0:
                add("act", lambda e, c=c, s=s: e.activation(out=sq[s][:], in_=xT[:, c, :], func=AF.Square),
                    reads=["xT%d" % c], writes=["sq%d" % s])
            else:
                add("dve", lambda e, c=c, s=s: e.tensor_tensor(out=sq[s][:], in0=xT[:, c, :], in1=xT[:, c, :], op=ALU.mult),
                    reads=["xT%d" % c], writes=["sq%d" % s])
            mm(aux[:, :], onesb[:], sq[s][:], c == 0, c == KC - 1, ["onesb", "sq%d" % s], ["aux"])
        add("act", lambda e: e.activation(out=rstd[:], in_=aux[:, :], func=AF.Ln, bias=EPS, scale=1.0 / D),
            reads=["aux"], writes=["rstd"])
        add("act", lambda e: e.activation(out=rstd[:], in_=rstd[:], func=AF.Exp, scale=-0.5),
            reads=["rstd"], writes=["rstd"])
        for c in range(KC):
            if final:
                add("dve", lambda e, c=c: e.scalar_tensor_tensor(out=yout[:, c, :], in0=xT[:, c, :], scalar=cvec(gcol0 + c),
                                                                 in1=rstd[:], op0=ALU.mult, op1=ALU.mult),
                    reads=["xT%d" % c, "rstd", "vec"], writes=["act%d" % (2 * c), "act%d" % (2 * c + 1)])
            else:
                add("dve", lambda e, c=c: e.scalar_tensor_tensor(out=hT[:, c, :], in0=xT[:, c, :], scalar=cvec(gcol0 + c),
                                                                 in1=rstd[:], op0=ALU.mult, op1=ALU.mult),
                    reads=["xT%d" % c, "rstd", "vec"], writes=["hT%d" % c])

    hkeys = ["hT%d" % c for c in range(KC)]

    def proj_fm(W, wkey, j, out_ap_fn, ncols=TT):
        bk, bkey = mbank()
        for kc in range(KC):
            mm(bk[:, 0:ncols], W[:, kc, j * 128:(j + 1) * 128], hT[:, kc, :], kc == 0, kc == KC - 1,
               [wkey, "hT%d" % kc], [bkey])
        return bk, bkey

    def tile(l, i):
        vb = NVL * l
        par = i % 2
        first_layer = (l == 0)
        last_layer = (l == depth - 1)
        src = xT_d if first_layer else x1_d
        c0 = i * TT
        L_ = 16 + TT
        add("sp", lambda e: e.dma_start(out=xT[:], in_=src.rearrange("(c p) t -> p c t", p=128)[:, :, c0:c0 + TT]),
            reads=["x1d%d" % i] if not first_layer else [], writes=["xT%d" % c for c in range(KC)], dma_slot="xT")
        P.tag = "norm1"
        norm(vb + 0)

        def ip_pool():
            W, wk = wload(l, "pool")
            for j in range(2):
                bk, bkey = proj_fm(W, wk, j, None)
                add("act", lambda e, j=j, bk=bk: e.copy(out=U[:, j, 16:16 + TT], in_=bk[:, :]), reads=[bkey], writes=["U%d" % j])

        def ip_g1():
            W, wk = wload(l, "g1")
            for j in range(4):
                bk, bkey = proj_fm(W, wk, j, None)
                dst = gqT if j < 2 else gkT
                dkey = ("gq%d" if j < 2 else "gk%d") % (j % 2)
                add("dve", lambda e, j=j, bk=bk, dst=dst: e.tensor_copy(out=dst[:, j % 2, :], in_=bk[:, :]), reads=[bkey], writes=[dkey])

        def ip_g2():
            W, wk = wload(l, "g2")
            for blk in range(NBLK):
                bk, bkey = mbank()
                for kc in range(KC):
                    mm(bk[:, 0:256], hT[:, kc, blk * 128:(blk + 1) * 128], W[:, kc, 0:256], kc == 0, kc == KC - 1,
                       [wk, "hT%d" % kc], [bkey])
                add("dve", lambda e, blk=blk, bk=bk: e.tensor_copy(out=gvraw[:, blk, :], in_=bk[:, 0:256]), reads=[bkey], writes=["gvraw"])
                for hh in range(2):
                    add("act", lambda e, blk=blk, bk=bk, hh=hh: e.copy(
                        out=gvpad[:, blk, :, hh, hh * 64:(hh + 1) * 64],
                        in_=bk[:, 0:256].rearrange("p (a h d) -> p a h d", a=2, h=2)[:, :, hh, :]),
                        reads=[bkey], writes=["gvpad"])
            for j in range(2):
                bk, bkey = proj_fm(W, wk, 2 + j, None)
                add("act", lambda e, j=j, bk=bk: e.activation(out=sg[:, j, :], in_=bk[:, :], func=AF.Silu), reads=[bkey], writes=["sg%d" % j])

        def ip_gs():
            W, wk = wload(l, "gs")
            bk, bkey = mbank()
            for kc in range(KC):
                mm(bk[0:16, :], W[:, kc, 0:16], hT[:, kc, :], kc == 0, kc == KC - 1, [wk, "hT%d" % kc], [bkey])
            add("dve", lambda e, bk=bk: e.tensor_copy(out=gaT[:], in_=bk[0:16, :]), reads=[bkey], writes=["gaT"])
            bk, bkey = mbank()
            for blk in range(NBLK):
                for kc in range(KC):
                    mm(bk[:, blk * 8:(blk + 1) * 8], hT[:, kc, blk * 128:(blk + 1) * 128], W[:, kc, 16:24], kc == 0, kc == KC - 1,
                       [wk, "hT%d" % kc], [bkey])
            add("dve", lambda e, bk=bk: e.tensor_tensor(out=fft[:], in0=bk[:, 0:32], in1=bfb[:, l, :], op=ALU.add),
                reads=[bkey, "bfb"], writes=["fft"])

        def ip_fq():
            W, wk = wload(l, "fq")
            for j in range(4):
                bk, bkey = proj_fm(W, wk, j, None)
                add("act", lambda e, j=j, bk=bk: e.mul(out=qT[:, j, :], in_=bk[:, :], mul=0.125), reads=[bkey], writes=["qT%d" % j])

        def ip_fk():
            W, wk = wload(l, "fk")
            for j in range(4):
                bk, bkey = proj_fm(W, wk, j, None)
                add("dve", lambda e, j=j, bk=bk: e.tensor_copy(out=kT[:, j, :], in_=bk[:, :]), reads=[bkey], writes=["kT%d" % j])
                add(ST_ENG, lambda e, j=j: e.dma_start(out=kc_d[j, :, c0:c0 + TT], in_=kT[:, j, :]),
                    reads=["kT%d" % j], writes=["kc%d_%d" % (j, i)], dma_slot="kst%d" % j)

        def ip_fv():
            W, wk = wload(l, "fv")
            for blk in range(NBLK):
                bk, bkey = mbank()
                for kc in range(KC):
                    mm(bk[:, :], hT[:, kc, blk * 128:(blk + 1) * 128], W[:, kc, :], kc == 0, kc == KC - 1, [wk, "hT%d" % kc], [bkey])
                add("dve", lambda e, blk=blk, bk=bk: e.tensor_copy(
                    out=vtm[:, :, blk, :].rearrange("p a (h d) -> p a h d", h=2)[:, :, :, 0:64],
                    in_=bk[:, :].rearrange("p (a h d) -> p a h d", a=4, h=2)), reads=[bkey], writes=["vtm"])
            for pr in range(4):
                add(ST_ENG, lambda e, pr=pr: e.dma_start(out=vc_d[pr, :, i * NBLK:(i + 1) * NBLK, :], in_=vtm[:, pr, :, :]),
                    reads=["vtm"], writes=["vc%d_%d" % (pr, i)], dma_slot="vst%d" % pr)

        def f_chain():
            add("act", lambda e: e.activation(out=spf[:], in_=fft[:], func=AF.Exp, scale=-1.0), reads=["fft"], writes=["spf"])
            add("act", lambda e: e.activation(out=spf[:], in_=spf[:], func=AF.Ln, bias=1.0, scale=1.0), reads=["spf"], writes=["spf"])
            if i == 0:
                add("dve", lambda e: e.memset(carry[:], 0.0), writes=["carry"])
            for blk in range(NBLK):
                o = aux[:, blk * 8:(blk + 1) * 8]
                mm(o, triU, spf[:, blk * 8:(blk + 1) * 8], True, False, ["cst", "spf"], ["aux"])
                for b2 in range(blk):
                    mm(o, onesf[:], spf[:, b2 * 8:(b2 + 1) * 8], False, False, ["onesf", "spf"], ["aux"])
                mm(o, onesf[0:1, :], carry[0:1, :], False, True, ["onesf", "carry"], ["aux"])
            add("dve", lambda e: e.tensor_copy(out=negF[:, i * 32:(i + 1) * 32], in_=aux[:, 0:32]), reads=["aux"], writes=["negF"])
            mm(aux[:, 64:72], sel127, negF[:, i * 32 + 8:i * 32 + 16], True, True, ["cst", "negF"], ["aux"])
            mm(aux[0:1, 80:88], sel127[:, 0:1], negF[:, i * 32 + 24:i * 32 + 32], True, True, ["cst", "negF"], ["aux"])
            add("dve", lambda e: e.tensor_copy(out=nref[:], in_=aux[:, 64:72]), reads=["aux"], writes=["nref"])
            add("dve", lambda e: e.tensor_copy(out=carry[:], in_=aux[0:1, 80:88]), reads=["aux"], writes=["carry"])

        def pool_elem(j):
            if i == 0:
                add("pool", lambda e, j=j: e.memset(U[:, j, 0:16], 0.0), writes=["Uh%d" % j])
            Uj = U[:, j, :]
            add("pool", lambda e, Uj=Uj: e.tensor_tensor(out=s2[:, 1:L_], in0=Uj[:, 1:L_], in1=Uj[:, 0:L_ - 1], op=ALU.add),
                reads=["U%d" % j, "Uh%d" % j], writes=["s2"])
            add("pool", lambda e: e.tensor_tensor(out=s4[:, 3:L_], in0=s2[:, 3:L_], in1=s2[:, 1:L_ - 2], op=ALU.add),
                reads=["s2"], writes=["s4"])
            if j == 1:
                add("pool", lambda e: e.tensor_tensor(out=s8[:, 7:L_], in0=s4[:, 7:L_], in1=s4[:, 3:L_ - 4], op=ALU.add),
                    reads=["s4"], writes=["s2"])
                add("pool", lambda e: e.tensor_tensor(out=s16[:, 15:L_], in0=s8[:, 15:L_], in1=s8[:, 7:L_ - 8], op=ALU.add),
                    reads=["s2"], writes=["s4"])
                wins = [(s8, "s2", 0.125), (s16, "s4", 0.0625)]
            else:
                wins = [(s2, "s2", 0.5), (s4, "s4", 0.25)]
            for hh in range(2):
                wt, wkey_, invw = wins[hh]
                r0, r1 = hh * 64, (hh + 1) * 64
                add("dve", lambda e, j=j, wt=wt, invw=invw, r0=r0, r1=r1, Uj=Uj: e.scalar_tensor_tensor(
                    out=dT[r0:r1, j, :], in0=wt[r0:r1, 16:L_], scalar=invw, in1=Uj[r0:r1, 16:L_],
                    op0=ALU.mult, op1=ALU.subtract),
                    reads=[wkey_, "U%d" % j], writes=["dT%d" % j])
                if i == 0:
                    add("dve", lambda e, j=j, wt=wt, r0=r0, r1=r1: e.tensor_tensor(
                        out=t1[r0:r1, 0:16], in0=wt[r0:r1, 16:32], in1=cst[r0:r1, C_INVCNT + 16 * j:C_INVCNT + 16 * j + 16],
                        op=ALU.mult), reads=[wkey_, "cst"], writes=["t1"])
                    add("dve", lambda e, j=j, r0=r0, r1=r1, Uj=Uj: e.tensor_tensor(
                        out=dT[r0:r1, j, 0:16], in0=t1[r0:r1, 0:16], in1=Uj[r0:r1, 16:32], op=ALU.subtract),
                        reads=["t1", "U%d" % j, "dT%d" % j], writes=["dT%d" % j])
            add("pool", lambda e, j=j: e.tensor_copy(out=U[:, j, 0:16], in_=U[:, j, TT:TT + 16]),
                reads=["U%d" % j, "s2"], writes=["Uh%d" % j])

        def pool_mm(j):
            bk, bkey = mbank()
            mm(bk[:, :], wp_b[:, l, j, :], dT[:, j, :], True, True, ["wp_b", "dT%d" % j], [bkey])
            add("dve", lambda e, j=j, bk=bk: e.tensor_scalar(out=mixT[:, j, :], in0=bk[:, :], scalar1=cvec(vb + 16 + j), scalar2=None,
                                                              op0=ALU.mult), reads=[bkey, "vec"], writes=["mix%d" % j])

        def gla_a(pc):
            bpc, qinc, kinc, kkvc, qoc, decc = bp[pc], qin[pc], kin[pc], kkv[pc], qo[pc], dec[pc]
            sfx = "_%d" % pc
            bk, bkey = mbank()
            mm(bk[:, :], wa_b[0:16, l, pc * 128:(pc + 1) * 128], gaT[0:16, :], True, True, ["wa_b", "gaT"], [bkey])
            add("act", lambda e, bk=bk: e.activation(out=esb[:], in_=bk[:, :], func=AF.Exp, bias=negba[:, l, pc:pc + 1], scale=-1.0),
                reads=[bkey, "negba"], writes=["Et0"])
            add("act", lambda e: e.activation(out=spg[:], in_=esb[:], func=AF.Ln, bias=1.0, scale=1.0), reads=["Et0"], writes=["Et1"])
            add("dve", lambda e: e.tensor_tensor_scan(out=bpc[:], data0=segm, data1=spg[:], initial=0.0, op0=ALU.mult, op1=ALU.add),
                reads=["cst", "Et1"], writes=["bp" + sfx])
            bpv = bpc[:].rearrange("p (n c) -> p n c", c=64)
            add("dve", lambda e: e.tensor_tensor(out=d1[:].rearrange("p (n c) -> p n c", c=64), in0=bpv,
                                                 in1=bpv[:, :, 31:32].to_broadcast([128, 8, 64]), op=ALU.subtract),
                reads=["bp" + sfx], writes=["d1"])
            add("dve", lambda e: e.tensor_tensor(out=d3[:].rearrange("p (n c) -> p n c", c=64), in0=bpv,
                                                 in1=bpv[:, :, 63:64].to_broadcast([128, 8, 64]), op=ALU.subtract),
                reads=["bp" + sfx], writes=["d3"])
            specs = [(d1, "d1", -1.0 / 16, -LN8, gqT, "gq%d" % pc, qinc, "qin" + sfx),
                     (d1, "d1", 1.0 / 16, 0.0, gkT, "gk%d" % pc, None, "kin" + sfx),
                     (d3, "d3", 1.0 / 16, 0.0, gkT, "gk%d" % pc, kkvc, "kkv" + sfx),
                     (bpc, "bp" + sfx, -1.0 / 16, -LN8, gqT, "gq%d" % pc, qoc, "qo" + sfx)]
            for (srcT, skey, scl, bia, opT, okey, dst, dkey) in specs:
                rot["et"] = (rot["et"] + 1) % 2
                et = Et[rot["et"]]
                ekey = "Et%d" % rot["et"]
                add("act", lambda e, et=et, srcT=srcT, scl=scl, bia=bia: e.activation(out=et[:], in_=srcT[:], func=AF.Exp, bias=bia, scale=scl),
                    reads=[skey], writes=[ekey])
                if dst is None:
                    for hh in range(2):
                        r0, r1 = hh * 64, (hh + 1) * 64
                        add("dve", lambda e, et=et, opT=opT, hh=hh, r0=r0, r1=r1: e.tensor_tensor(
                            out=kinc[r0:r1, hh, :], in0=opT[r0:r1, pc, :], in1=et[r0:r1, :], op=ALU.mult),
                            reads=[ekey, okey], writes=[dkey])
                else:
                    add("dve", lambda e, et=et, opT=opT, dst=dst: e.tensor_tensor(out=dst[:], in0=opT[:, pc, :], in1=et[:], op=ALU.mult),
                        reads=[ekey, okey], writes=[dkey])
            add("act", lambda e: e.activation(out=decc[:], in_=bpv[:, :, 63], func=AF.Exp, scale=-1.0 / 16),
                reads=["bp" + sfx], writes=["dec" + sfx])

        def gla_b(pc):
            kkvc, decc, kktmc = kkv[pc], dec[pc], kktm[pc]
            sfx = "_%d" % pc
            bk, bkey = mbank()
            for blk in range(NBLK):
                add("pe", lambda e, bk=bk, blk=blk: e.transpose(bk[:, blk * 128:(blk + 1) * 128], kkvc[:, blk * 128:(blk + 1) * 128], identf),
                    reads=["kkv" + sfx, "cst"], writes=[bkey])
            for ch in range(2):
                t0, t1_ = ch * 64, (ch + 1) * 64
                add("dve", lambda e, bk=bk, ch=ch, t0=t0, t1_=t1_: e.tensor_copy(
                    out=kktmc[t0:t1_, ch, :, :], in_=bk[t0:t1_, :].rearrange("p (b c) -> p b c", b=NBLK)), reads=[bkey], writes=["kktm" + sfx])
            if i == 0:
                add("dve", lambda e: e.memset(S32[:, pc, :], 0.0), writes=["S32_%d" % pc])
                add("dve", lambda e: e.memset(Sbf[:, pc, 0, 0, :], 0.0), writes=["Sbf%d_0_0" % pc])
            kvb = []
            for half in range(2):
                kb_, kbkey = sbank()
                kvb.append((kb_, kbkey))
                for q4 in range(4):
                    n = half * 4 + q4
                    blk, ch = n // 2, n % 2
                    mm(kb_[:, q4 * 128:(q4 + 1) * 128], kktmc[:, ch, blk, :], gvraw[:, blk, pc * 128:(pc + 1) * 128], True, True,
                       ["kktm" + sfx, "gvraw"], [kbkey])
            for n in range(8):
                kb_, kbkey = kvb[n // 4]
                q4 = n % 4
                for hh in range(2):
                    r0, r1 = hh * 64, (hh + 1) * 64
                    add("dve", lambda e, n=n, hh=hh, r0=r0, r1=r1, kb_=kb_, q4=q4: e.scalar_tensor_tensor(
                        out=S32[r0:r1, pc, hh * 64:(hh + 1) * 64], in0=S32[r0:r1, pc, hh * 64:(hh + 1) * 64],
                        scalar=decc[r0:r1, n:n + 1], in1=kb_[r0:r1, q4 * 128 + hh * 64:q4 * 128 + (hh + 1) * 64],
                        op0=ALU.mult, op1=ALU.add), reads=["S32_%d" % pc, "dec" + sfx, kbkey], writes=["S32_%d" % pc])
                pn, vn = (par, n + 1) if n < 7 else (1 - par, 0)
                add("act", lambda e, pn=pn, vn=vn: e.copy(out=Sbf[:, pc, pn, vn, :], in_=S32[:, pc, :]),
                    reads=["S32_%d" % pc], writes=["Sbf%d_%d_%d" % (pc, pn, vn)])

        def gla_c(pc):
            qinc, kinc, qoc = qin[pc], kin[pc], qo[pc]
            sfx = "_%d" % pc
            ob, okey = obank()
            for blk in range(NBLK):
                ab, abkey = sbank()
                for hh in range(2):
                    mm(ab[:, hh * 128:(hh + 1) * 128], kinc[:, hh, blk * 128:(blk + 1) * 128], qinc[:, blk * 128:(blk + 1) * 128],
                       True, True, ["kin" + sfx, "qin" + sfx], [abkey])
                add("dve", lambda e, ab=ab: e.tensor_tensor(out=attm[:], in0=ab[:, 0:256].rearrange("p (h c) -> p h c", h=2),
                                                            in1=mask2b[:].unsqueeze(1).to_broadcast([128, 2, 128]), op=ALU.mult),
                    reads=[abkey, "mask2b"], writes=["attm"])
                oc = ob[:, blk * 128:(blk + 1) * 128]
                mm(oc, gvpad[:, blk, pc, 0, :], attm[:, 0, :], True, False, ["gvpad", "attm"], [okey])
                mm(oc, gvpad[:, blk, pc, 1, :], attm[:, 1, :], False, False, ["gvpad", "attm"], [okey])
                for ch in range(2):
                    n = blk * 2 + ch
                    occ = ob[:, n * 64:(n + 1) * 64]
                    mm(occ, Sbf[:, pc, par, n, :], qoc[:, n * 64:(n + 1) * 64], False, ch == 1,
                       ["Sbf%d_%d_%d" % (pc, par, n), "qo" + sfx], [okey])
            add("act", lambda e, ob=ob: e.activation(out=osq[:], in_=ob[:, :], func=AF.Square), reads=[okey], writes=["osq"])
            mm(aux[:, :], bonesb[:], osq[:], True, True, ["bonesb", "osq"], ["aux"])
            add("act", lambda e: e.activation(out=rr[:], in_=aux[:, :], func=AF.Ln, bias=EPS, scale=1.0 / 64), reads=["aux"], writes=["rr"])
            add("act", lambda e: e.activation(out=rr[:], in_=rr[:], func=AF.Exp, scale=-0.5), reads=["rr"], writes=["rr"])
            add("dve", lambda e, ob=ob: e.scalar_tensor_tensor(out=t1[:], in0=ob[:, :], scalar=cvec(vb + 20 + pc), in1=rr[:],
                                                               op0=ALU.mult, op1=ALU.mult), reads=[okey, "rr", "vec"], writes=["t1"])
            add("dve", lambda e: e.tensor_tensor(out=mixT[:, 2 + pc, :], in0=t1[:], in1=sg[:, pc, :], op=ALU.mult),
                reads=["t1", "sg%d" % pc], writes=["mix%d" % (2 + pc)])

        nkb = NBLK * (i + 1)
        nseg = (nkb + 15) // 16

        def fox(pr):
            fbs, oaccs = [], []
            for hh in range(2):
                h = 2 * pr + hh
                rot["fb"] = (rot["fb"] + 1) % 2
                fb = fbias[rot["fb"]]
                fbkey = "fbias%d" % rot["fb"]
                add("dve", lambda e, fb=fb, h=h: e.tensor_scalar(
                    out=fb[:, 0:nkb], in0=negF[:, 0:nkb * 8].rearrange("p (b h) -> p b h", h=8)[:, :, h],
                    scalar1=nref[:, h:h + 1], scalar2=None, op0=ALU.subtract),
                    reads=["negF", "nref"], writes=[fbkey])
                fbs.append((fb, fbkey))
                oaccs.append(obank())
            for s_ in range(nseg):
                nb = min(16, nkb - s_ * 16)
                rot["kv"] = (rot["kv"] + 1) % 2
                sl = rot["kv"]
                tiles_in = list(range(s_ * 4, min(s_ * 4 + 4, i + 1)))
                add("sp", lambda e, pr=pr, s_=s_, nb=nb, sl=sl: e.dma_start(out=kseg[sl][:, 0:nb * 128],
                                                                           in_=kc_d[pr, :, s_ * 2048:s_ * 2048 + nb * 128]),
                    reads=["kc%d_%d" % (pr, t) for t in tiles_in], writes=["kseg%d" % sl], dma_slot="kseg%d" % sl)
                add("sp", lambda e, pr=pr, s_=s_, nb=nb, sl=sl: e.dma_start(out=vseg[sl][:, 0:nb, :],
                                                                           in_=vc_d[pr, :, s_ * 16:s_ * 16 + nb, :]),
                    reads=["vc%d_%d" % (pr, t) for t in tiles_in], writes=["vseg%d" % sl], dma_slot="vseg%d" % sl)
                for hh in range(2):
                    r0, r1 = hh * 64, (hh + 1) * 64
                    fb, fbkey = fbs[hh]
                    oacc, oakey = oaccs[hh]
                    for kk in range(nb):
                        kb = s_ * 16 + kk
                        jd = kb - NBLK * i
                        cc0 = 128 * jd if jd > 0 else 0
                        sb_, sbkey = sbank()
                        mm(sb_[:, cc0:TT], kseg[sl][r0:r1, kk * 128:(kk + 1) * 128], qT[r0:r1, pr, cc0:TT], True, jd < 0,
                           ["kseg%d" % sl, "qT%d" % pr], [sbkey])
                        if jd >= 0:
                            mm(sb_[:, cc0:TT], identb[:], m0b[:, 0:TT - cc0], False, True, ["identb", "m0b"], [sbkey])
                        rot["pt"] = (rot["pt"] + 1) % 3
                        pt = PT[rot["pt"]]
                        ptkey = "PT%d" % rot["pt"]
                        add("act", lambda e, pt=pt, sb_=sb_, cc0=cc0, fb=fb, kb=kb: e.activation(
                            out=pt[:, cc0:TT], in_=sb_[:, cc0:TT], func=AF.Exp, bias=fb[:, kb:kb + 1], scale=1.0),
                            reads=[sbkey, fbkey], writes=[ptkey])
                        mm(oacc[0:65, cc0:TT], vseg[sl][:, kk, hh * 65:(hh + 1) * 65], pt[:, cc0:TT], kb == 0, kb == nkb - 1,
                           ["vseg%d" % sl, ptkey], [oakey])
            for hh in range(2):
                r0, r1 = hh * 64, (hh + 1) * 64
                oacc, oakey = oaccs[hh]
                rot["osb"] = (rot["osb"] + 1) % 2
                os_ = osb[rot["osb"]]
                oskey = "osb%d" % rot["osb"]
                add("dve", lambda e, os_=os_, oacc=oacc: e.tensor_copy(out=os_[0:65, :], in_=oacc[0:65, :]), reads=[oakey], writes=[oskey])
                add("dve", lambda e, os_=os_: e.reciprocal(out=os_[64:65, :], in_=os_[64:65, :]), reads=[oskey], writes=[oskey])
                mm(aux[0:64, :], onesf[64:65, 0:64], os_[64:65, :], True, True, ["onesf", oskey], ["aux"])
                add("dve", lambda e, os_=os_, pr=pr, r0=r0, r1=r1: e.tensor_tensor(out=mixT[r0:r1, 4 + pr, :], in0=os_[0:64, :],
                                                                                  in1=aux[0:64, :], op=ALU.mult),
                    reads=[oskey, "aux"], writes=["mix%d" % (4 + pr)])


        for fn_, args_ in [(ip_gs, ()), (f_chain, ()), (ip_g1, ()), (gla_a, (0,)), (gla_a, (1,)), (ip_g2, ()), (ip_pool, ()),
                           (ip_fq, ()), (ip_fk, ()), (ip_fv, ()), (pool_elem, (0,)), (pool_elem, (1,)), (gla_b, (0,)), (gla_b, (1,)),
                           (fox, (0,)), (gla_c, (0,)), (fox, (1,)), (gla_c, (1,)), (fox, (2,)), (pool_mm, (0,)), (pool_mm, (1,)),
                           (fox, (3,))]:
            P.tag = fn_.__name__
            fn_(*args_)
        P.tag = "wo"

        for og in range(2):
            W, wk = wload(l, "wo%d" % og)
            for jj in range(4):
                j = og * 4 + jj
                bk, bkey = mbank()
                for kc in range(KC):
                    mm(bk[:, :], W[:, kc, jj * 128:(jj + 1) * 128], mixT[:, kc, :], kc == 0, kc == KC - 1, [wk, "mix%d" % kc], [bkey])
                add("dve", lambda e, j=j, bk=bk: e.tensor_tensor(out=xT[:, j, :], in0=xT[:, j, :], in1=bk[:, :], op=ALU.add),
                    reads=[bkey, "xT%d" % j], writes=["xT%d" % j])
        P.tag = "norm2"
        norm(vb + 8)
        P.tag = "ffn_gu"
        for g, ncol in enumerate(GU_SIZES):
            Wg, wkg = wload(l, "gate%d" % g)
            Wu, wku = wload(l, "up%d" % g)
            for jj in range(ncol // 128):
                f = g * 4 + jj
                bg, bgkey = mbank()
                for kc in range(KC):
                    mm(bg[:, :], Wg[:, kc, jj * 128:(jj + 1) * 128], hT[:, kc, :], kc == 0, kc == KC - 1, [wkg, "hT%d" % kc], [bgkey])
                bu, bukey = mbank()
                for kc in range(KC):
                    mm(bu[:, :], Wu[:, kc, jj * 128:(jj + 1) * 128], hT[:, kc, :], kc == 0, kc == KC - 1, [wku, "hT%d" % kc], [bukey])
                rot["sg"] = (rot["sg"] + 1) % 2
                sgt = sgate[rot["sg"]]
                sgkey = "Et%d" % rot["sg"]
                add("act", lambda e, sgt=sgt, bg=bg: e.activation(out=sgt[:], in_=bg[:, :], func=AF.Silu), reads=[bgkey], writes=[sgkey])
                add("dve", lambda e, sgt=sgt, bu=bu, f=f: e.tensor_tensor(out=actT[:, f, :], in0=sgt[:], in1=bu[:, :], op=ALU.mult),
                    reads=[sgkey, bukey], writes=["act%d" % f] + (stg_keys if f < 6 else []))
        P.tag = "ffn_dn"
        for j in range(KC):
            W, wk = wload(l, "dn%d" % j)
            bk, bkey = mbank()
            for fc in range(NFC):
                mm(bk[:, :], W[:, fc, :], actT[:, fc, :], fc == 0, fc == NFC - 1, [wk, "act%d" % fc], [bkey])
            add("dve", lambda e, j=j, bk=bk: e.tensor_tensor(out=xT[:, j, :], in0=xT[:, j, :], in1=bk[:, :], op=ALU.add),
                reads=[bkey, "xT%d" % j], writes=["xT%d" % j])
        P.tag = "out"
        if last_layer:
            norm(NVL * depth, final=True)
            add(ST_ENG, lambda e: e.dma_start(out=outT_d.rearrange("(c p) t -> p c t", p=128)[:, :, c0:c0 + TT], in_=yout),
                reads=["act%d" % c for c in range(2 * KC)], writes=["outd"], dma_slot="yout")
        else:
            add(ST_ENG, lambda e: e.dma_start(out=x1_d.rearrange("(c p) t -> p c t", p=128)[:, :, c0:c0 + TT], in_=xT[:]),
                reads=["xT%d" % c for c in range(KC)], writes=["x1d%d" % i], dma_slot="x1st")

    try:
        if stop == "conv":
            raise _Stop()
        for l in range(depth):
            for i in range(NT):
                tile(l, i)
    except _Stop:
        pass

    P.emit()
    P.close()
    return nc


def _regroup(w, c0, c1):
    K = w.shape[0]
    kc = K // 128
    return np.ascontiguousarray(w[:, c0:c1].reshape(kc, 128, c1 - c0).transpose(1, 0, 2)).reshape(128, kc * (c1 - c0))


def host_weights(w_in, w_o, w_gu, w_down):
    depth = w_in.shape[0]
    out = np.empty((depth, 128, WTOT), np.float32)
    for l in range(depth):
        wi = w_in[l]
        parts = [
            _regroup(wi, 0, 256),
            _regroup(wi, 256, 768),
            _regroup(wi, 768, 1280),
            _regroup(np.concatenate([wi[:, 1280:1296], wi[:, 2832:2840]], axis=1), 0, 24),
            _regroup(wi, 1296, 1808),
            _regroup(wi, 1808, 2320),
            _regroup(wi, 2320, 2832),
        ]
        for og in range(2):
            parts.append(_regroup(w_o[l], og * 512, (og + 1) * 512))
        o = 0
        for n in GU_SIZES:
            parts.append(_regroup(w_gu[l], o, o + n))
            o += n
        o = 0
        for n in GU_SIZES:
            parts.append(_regroup(w_gu[l], DFF + o, DFF + o + n))
            o += n
        for j in range(8):
            parts.append(_regroup(w_down[l], j * 128, (j + 1) * 128))
        out[l] = np.concatenate(parts, axis=1)
    return out


def host_consts():
    c = np.zeros((128, NCONST), np.float32)
    k = np.arange(128)[:, None]
    q = np.arange(512)[None, :]
    c[:, C_M0:C_M0 + 512] = np.where(q >= k, 0.0, NEG)
    s = np.arange(128)[:, None]
    cc = np.arange(128)[None, :]
    c[:, C_MASK2:C_MASK2 + 128] = ((s // 64 == cc // 64) & (s <= cc)).astype(np.float32)
    seg = np.ones(512, np.float32)
    seg[::64] = 0.0
    c[:, C_SEG:C_SEG + 512] = seg[None, :]
    c[:, C_IDENT:C_IDENT + 128] = np.eye(128, dtype=np.float32)
    c[:, C_TRIU:C_TRIU + 128] = (s <= cc).astype(np.float32)
    c[127, C_SEL127:C_SEL127 + 128] = 1.0
    c[:, C_BONES:C_BONES + 128] = (s // 64 == cc // 64).astype(np.float32)
    wins = [[2, 4], [8, 16]]
    for j in range(2):
        for hh in range(2):
            w = wins[j][hh]
            c[hh * 64:(hh + 1) * 64, C_INVW + j] = 1.0 / w
            t = np.arange(16)
            c[hh * 64:(hh + 1) * 64, C_INVCNT + 16 * j:C_INVCNT + 16 * j + 16] = (1.0 / np.minimum(t + 1, w))[None, :]
    return c


def host_vecs(ln1, ln2, pool_scale, b_a, gla_gn, ln_f):
    depth = ln1.shape[0]
    v = np.zeros((128, NVL * depth + 8), np.float32)
    for l in range(depth):
        b = NVL * l
        v[:, b:b + 8] = ln1[l].reshape(8, 128).T
        v[:, b + 8:b + 16] = ln2[l].reshape(8, 128).T
        v[:, b + 16:b + 18] = pool_scale[l].reshape(2, 128).T
        v[:, b + 18:b + 20] = b_a[l].reshape(2, 128).T
        v[:, b + 20:b + 22] = gla_gn[l].reshape(2, 128).T
    v[:, NVL * depth:] = ln_f.reshape(8, 128).T
    return v


_PROG_CACHE = {}


def kernel(x, ln1, w_in, w_pool, pool_scale, w_a_up, b_a, gla_gn, b_f, w_o, ln2, w_gu, w_down, ln_f):
    x = np.asarray(x, np.float32)
    B, T, _ = x.shape
    depth = np.asarray(w_in).shape[0]
    n = 8
    key = (T, depth)
    if key not in _PROG_CACHE:
        import os
        _PROG_CACHE[key] = build_program(T, depth, stop=os.environ.get("KSTOP"))
    nc = _PROG_CACHE[key]
    wall = host_weights(np.asarray(w_in, np.float32), np.asarray(w_o, np.float32), np.asarray(w_gu, np.float32),
                        np.asarray(w_down, np.float32))
    vecs = host_vecs(np.asarray(ln1, np.float32), np.asarray(ln2, np.float32), np.asarray(pool_scale, np.float32),
                     np.asarray(b_a, np.float32), np.asarray(gla_gn, np.float32), np.asarray(ln_f, np.float32))
    consts = host_consts()
    bfb = np.ascontiguousarray(np.broadcast_to(np.tile(np.asarray(b_f, np.float32), (1, 4))[:, None, :], (depth, 128, 32)))
    in_maps = []
    for c in range(n):
        b = c % B
        wall_c = np.concatenate([wall, np.full((depth, 128, 4), float(c), np.float32)], axis=2)
        in_maps.append({
            "xT": np.ascontiguousarray(x[b].T),
            "wall": wall_c,
            "vecs": vecs,
            "wpool": np.asarray(w_pool, np.float32),
            "waup": np.asarray(w_a_up, np.float32),
            "bfb": bfb,
            "consts": consts,
        })
    res = run_bass_kernel_spmd(nc, in_maps, core_ids=list(range(n)))
    out = np.empty((B, T, D), np.float32)
    for b in range(B):
        out[b] = np.asarray(res.results[b]["outT"], np.float32).T
    return out
```

```python
import contextlib
import numpy as np
import concourse.bass as bass
import concourse.mybir as mybir
from concourse.bass_utils import run_bass_kernel_spmd

F32 = mybir.dt.float32
BF16 = mybir.dt.bfloat16
AF = mybir.ActivationFunctionType
ALU = mybir.AluOpType

ENGS = ("pe", "act", "dve", "pool", "sp")

D = 1024
KC = 8
DEPTH = 2
TT = 512
NBLK = 4
DFF = 2816
NFC = 22
EPS = 1e-6
NEG = -30000.0
ST_ENG = "sp"
LN8 = float(np.log(8.0))

G_IN = [("pool", 256), ("g1", 512), ("g2", 512), ("gs", 24), ("fq", 512), ("fk", 512), ("fv", 512)]
GU_SIZES = [512, 512, 512, 512, 512, 256]


def _offsets():
    off = {}
    o = 0
    for nm, n in G_IN:
        off[nm] = (o, 8 * n, n)
        o += 8 * n
    for og in range(2):
        off["wo%d" % og] = (o, 8 * 512, 512)
        o += 8 * 512
    for g, n in enumerate(GU_SIZES):
        off["gate%d" % g] = (o, 8 * n, n)
        o += 8 * n
    for g, n in enumerate(GU_SIZES):
        off["up%d" % g] = (o, 8 * n, n)
        o += 8 * n
    for j in range(8):
        off["dn%d" % j] = (o, NFC * 128, 128)
        o += NFC * 128
    return off, o


WOFF, WTOT = _offsets()
PIECE = 2052
assert WTOT % PIECE == 0

NVL = 22
C_M0 = 0
C_MASK2 = 512
C_SEG = 640
C_IDENT = 1152
C_TRIU = 1280
C_SEL127 = 1408
C_BONES = 1536
C_INVW = 1664
C_INVCNT = 1666
NCONST = 1698


class Op:
    __slots__ = ("eng", "fn", "deps", "signal", "semval", "dma_sem", "dma_val", "is_dma", "tag")

    def __init__(self, eng, fn, is_dma):
        self.eng = eng
        self.fn = fn
        self.deps = []
        self.signal = False
        self.semval = None
        self.is_dma = is_dma
        self.dma_sem = None
        self.dma_val = None


class Prog:
    def __init__(self, nc):
        self.nc = nc
        self.ops = {e: [] for e in ENGS}
        self.last_writer = {}
        self.readers = {}
        self.dma_slots = {}
        self.n_dma_sems = 0
        self.stack = contextlib.ExitStack()
        self.rot = {}
        self.excl = set()
        self.tag = "setup"

    def sb(self, name, shape, dtype):
        return self.stack.enter_context(self.nc.sbuf_tensor(name, list(shape), dtype))

    def ps(self, name, shape, dtype=F32):
        return self.stack.enter_context(self.nc.psum_tensor(name, list(shape), dtype))

    def add(self, eng, fn, reads=(), writes=(), dma_slot=None):
        op = Op(eng, fn, dma_slot is not None)
        op.tag = self.tag
        deps = {}
        ex = [r for r in reads if r in self.excl]
        if ex:
            reads = [r for r in reads if r not in self.excl]
            writes = list(writes) + ex
        for r in reads:
            w = self.last_writer.get(r)
            if w is not None:
                deps[id(w)] = w
        for k in writes:
            w = self.last_writer.get(k)
            if w is not None:
                deps[id(w)] = w
            for rd in self.readers.get(k, ()):
                deps[id(rd)] = rd
        op.deps = list(deps.values())
        for r in reads:
            rl = self.readers.setdefault(r, [])
            if not op.is_dma:
                for idx in range(len(rl)):
                    if (not rl[idx].is_dma) and rl[idx].eng == eng:
                        rl.pop(idx)
                        break
            rl.append(op)
        for k in writes:
            self.last_writer[k] = op
            self.readers[k] = []
        if dma_slot is not None:
            if dma_slot not in self.dma_slots:
                self.dma_slots[dma_slot] = [self.n_dma_sems, 0]
                self.n_dma_sems += 1
            s = self.dma_slots[dma_slot]
            s[1] += 1
            op.dma_sem = s[0]
            op.dma_val = 16 * s[1]
        self.ops[eng].append(op)
        return op

    def emit(self):
        nc = self.nc
        for e in ENGS:
            for op in self.ops[e]:
                for d in op.deps:
                    if d.is_dma:
                        continue
                    if d.eng == "pe" and op.eng == "pe":
                        continue
                    d.signal = True
        for e in ENGS:
            c = 0
            for op in self.ops[e]:
                if op.signal and not op.is_dma:
                    c += 1
                    op.semval = c
        st = self.stack
        esem = {e: st.enter_context(nc.semaphore("s_" + e)) for e in ENGS if e != "sp"}
        dsem = [st.enter_context(nc.semaphore("d%d" % i)) for i in range(self.n_dma_sems)]
        block = st.enter_context(nc.Block())

        def run(ename, eng):
            waited = {}
            for op in self.ops[ename]:
                need = {}
                for d in op.deps:
                    if d.is_dma:
                        key = ("d", d.dma_sem)
                        val = d.dma_val
                    else:
                        if d.eng == "pe" and ename == "pe":
                            continue
                        key = ("e", d.eng)
                        val = d.semval
                    if val > need.get(key, 0):
                        need[key] = val
                for key, val in need.items():
                    if waited.get(key, 0) >= val:
                        continue
                    waited[key] = val
                    sem = dsem[key[1]] if key[0] == "d" else esem[key[1]]
                    eng.wait_ge(sem, val)
                ins = op.fn(eng)
                if op.is_dma:
                    ins.then_inc(dsem[op.dma_sem], 16)
                elif op.signal:
                    ins.then_inc(esem[ename], 1)
            fin = {}
            for op in self.ops[ename]:
                if op.is_dma:
                    fin[op.dma_sem] = max(fin.get(op.dma_sem, 0), op.dma_val)
            for s, v in fin.items():
                if waited.get(("d", s), 0) < v:
                    eng.wait_ge(dsem[s], v)

        @block.tensor
        def _(pe):
            run("pe", pe)

        @block.scalar
        def _(act):
            run("act", act)

        @block.vector
        def _(dve):
            run("dve", dve)

        @block.gpsimd
        def _(pool):
            run("pool", pool)

        @block.sync
        def _(sp):
            run("sp", sp)

    def close(self):
        self.stack.close()


class _Stop(Exception):
    pass


def build_program(T, depth=DEPTH, own_from=0, stop=None):
    NT = T // TT
    NBT = T // 128
    nc = bass.Bass("TRN2", target_bir_lowering=False)
    P = Prog(nc)
    add = P.add

    xT_d = nc.dram_tensor("xT", [D, T], F32, kind="ExternalInput").ap()
    wall_d = nc.dram_tensor("wall", [depth, 128, WTOT + 4], F32, kind="ExternalInput").ap()
    vecs_d = nc.dram_tensor("vecs", [128, NVL * depth + 8], F32, kind="ExternalInput").ap()
    wpool_d = nc.dram_tensor("wpool", [depth, 4, 64, 64], F32, kind="ExternalInput").ap()
    waup_d = nc.dram_tensor("waup", [depth, 16, 256], F32, kind="ExternalInput").ap()
    bf_d = nc.dram_tensor("bfb", [depth, 128, 32], F32, kind="ExternalInput").ap()
    const_d = nc.dram_tensor("consts", [128, NCONST], F32, kind="ExternalInput").ap()
    outT_d = nc.dram_tensor("outT", [D, T], F32, kind="ExternalOutput").ap()

    wbf_d = nc.dram_tensor("wbf", [depth, 128, WTOT], BF16).ap()
    kc_d = nc.dram_tensor("kcache", [4, 128, T], BF16).ap()
    vc_d = nc.dram_tensor("vcache", [4, 128, NBT, 130], BF16).ap()
    x1_d = nc.dram_tensor("x1T", [D, T], F32).ap()

    xT = P.sb("xTs", [128, KC, TT], F32)
    hT = P.sb("hT", [128, KC, TT], BF16)
    rstd = P.sb("rstd", [128, TT], F32)
    sq = [P.sb("sq%d" % i, [128, TT], BF16) for i in range(2)]
    wbuf = [P.sb("wbuf%d" % i, [128, 4096], BF16) for i in range(3)]
    mixT = P.sb("mixT", [128, KC, TT], BF16)
    actT = P.sb("actT", [128, NFC, TT], BF16)
    qT = P.sb("qT", [128, 4, TT], BF16)
    kT = P.sb("kT", [128, 4, TT], BF16)
    vtm = P.sb("vtm", [128, 4, NBLK, 130], BF16)
    kseg = [P.sb("kseg%d" % i, [128, 2048], BF16) for i in range(2)]
    vseg = [P.sb("vseg%d" % i, [128, 16, 130], BF16) for i in range(2)]
    PT = [P.sb("PT%d" % i, [128, TT], BF16) for i in range(3)]
    osb = [P.sb("osb%d" % i, [128, TT], F32) for i in range(2)]
    negF = P.sb("negF", [128, NBT * 8], F32)
    fbias = [P.sb("fbias%d" % i, [128, NBT], F32) for i in range(2)]
    nref = P.sb("nref", [128, 8], F32)
    fft = P.sb("fft", [128, 32], F32)
    spf = P.sb("spf", [128, 32], F32)
    carry = P.sb("carry", [1, 8], F32)
    bfb = P.sb("bfbs", [128, depth, 32], F32)
    gqT = P.sb("gqT", [128, 2, TT], F32)
    gkT = P.sb("gkT", [128, 2, TT], F32)
    sg = P.sb("sg", [128, 2, TT], BF16)
    gvraw = P.sb("gvraw", [128, NBLK, 256], BF16)
    gvpad = P.sb("gvpad", [128, NBLK, 2, 2, 128], BF16)
    gaT = P.sb("gaT", [16, TT], BF16)
    bp = [P.sb("bp%d" % i, [128, TT], F32) for i in range(2)]
    d1 = P.sb("d1", [128, TT], F32)
    d3 = P.sb("d3", [128, TT], F32)
    Et = [P.sb("Et%d" % i, [128, TT], F32) for i in range(2)]
    esb, spg = Et[0], Et[1]
    qin = [P.sb("qin%d" % i, [128, TT], BF16) for i in range(2)]
    kin = [P.sb("kin%d" % i, [128, 2, TT], BF16) for i in range(2)]
    kkv = [P.sb("kkv%d" % i, [128, TT], F32) for i in range(2)]
    qo = [P.sb("qo%d" % i, [128, TT], BF16) for i in range(2)]
    dec = [P.sb("dec%d" % i, [128, 8], F32) for i in range(2)]
    kktm = [P.sb("kktm%d" % i, [128, 2, NBLK, 128], BF16) for i in range(2)]
    attm = P.sb("attm", [128, 2, 128], BF16)
    S32 = P.sb("S32", [128, 2, 128], F32)
    Sbf = P.sb("Sbf", [128, 2, 2, 8, 128], BF16)
    osq = P.sb("osq", [128, TT], BF16)
    t1 = P.sb("t1", [128, TT], F32)
    rr = P.sb("rr", [128, TT], F32)
    sgate = Et
    wa_f = P.sb("wa_f", [16, depth, 256], F32)
    wa_b = P.sb("wa_b", [16, depth, 256], BF16)
    U = P.sb("U", [128, 2, 16 + TT], F32)
    s2 = P.sb("s2", [128, 16 + TT], F32)
    s4 = P.sb("s4", [128, 16 + TT], F32)
    s8 = s2
    s16 = s4
    dT = P.sb("dT", [128, 2, TT], BF16)
    wp_f = P.sb("wp_f", [128, depth, 2, 128], F32)
    wp_b = P.sb("wp_b", [128, depth, 2, 128], BF16)
    cst = P.sb("cst", [128, NCONST], F32)
    vec = P.sb("vec", [128, NVL * depth + 8], F32)
    negba = P.sb("negba", [128, depth, 2], F32)
    m0b = P.sb("m0b", [128, TT], BF16)
    mask2b = P.sb("mask2b", [128, 128], BF16)
    identb = P.sb("identb", [128, 128], BF16)
    onesb = P.sb("onesb", [128, 128], BF16)
    bonesb = P.sb("bonesb", [128, 128], BF16)
    onesf = P.sb("onesf", [128, 128], F32)

    banks = {}
    for nm in ["m0", "m1", "m2", "s0", "s1", "o0", "o1", "aux"]:
        banks[nm] = P.ps("ps_" + nm, [128, 512], F32)
    rot = {"m": 0, "s": 0, "o": 0, "w": 0, "pt": 0, "sq": 0, "sg": 0, "osb": 0, "et": 0, "kv": 0, "fb": 0}

    def mbank():
        rot["m"] = (rot["m"] + 1) % 3
        k = "m%d" % rot["m"]
        return banks[k], k

    def sbank():
        rot["s"] = (rot["s"] + 1) % 2
        k = "s%d" % rot["s"]
        return banks[k], k

    def obank():
        rot["o"] = (rot["o"] + 1) % 2
        k = "o%d" % rot["o"]
        return banks[k], k

    aux = banks["aux"]
    P.excl = set(banks.keys())

    def mm(out, lhsT, rhs, start, stop, reads, writes):
        add("pe", lambda e: e.matmul(out, lhsT=lhsT, rhs=rhs, start=start, stop=stop), reads, writes)

    def cvec(col):
        return vec[:, col:col + 1]

    add("sp", lambda e: e.dma_start(out=cst[:], in_=const_d[:, :]), writes=["cst"], dma_slot="cst")
    add("sp", lambda e: e.dma_start(out=vec[:], in_=vecs_d[:, :]), writes=["vec"], dma_slot="vec")
    add("sp", lambda e: e.dma_start(out=bfb[:], in_=bf_d.rearrange("l p c -> p l c")), writes=["bfb"], dma_slot="bfb")
    add("sp", lambda e: e.dma_start(out=wa_f[:], in_=waup_d.rearrange("l r c -> r l c")), writes=["wa_f"], dma_slot="wa_f")
    add("pool", lambda e: e.memset(wp_f[:], 0.0), writes=["wp_f"])
    for l in range(depth):
        for g in range(4):
            j, hh = g // 2, g % 2
            add("sp", lambda e, l=l, g=g, j=j, hh=hh: e.dma_start(
                out=wp_f[hh * 64:(hh + 1) * 64, l, j, hh * 64:(hh + 1) * 64], in_=wpool_d[l, g, :, :]),
                reads=[], writes=["wp_f"], dma_slot="wp%d_%d" % (l, g))
    add("dve", lambda e: e.tensor_copy(out=wp_b[:], in_=wp_f[:]), reads=["wp_f"], writes=["wp_b"])
    add("dve", lambda e: e.tensor_copy(out=wa_b[:], in_=wa_f[:]), reads=["wa_f"], writes=["wa_b"])
    add("dve", lambda e: e.tensor_copy(out=m0b[:], in_=cst[:, C_M0:C_M0 + 512]), reads=["cst"], writes=["m0b"])
    add("dve", lambda e: e.tensor_copy(out=mask2b[:], in_=cst[:, C_MASK2:C_MASK2 + 128]), reads=["cst"], writes=["mask2b"])
    add("dve", lambda e: e.tensor_copy(out=identb[:], in_=cst[:, C_IDENT:C_IDENT + 128]), reads=["cst"], writes=["identb"])
    add("dve", lambda e: e.tensor_copy(out=bonesb[:], in_=cst[:, C_BONES:C_BONES + 128]), reads=["cst"], writes=["bonesb"])
    add("pool", lambda e: e.memset(onesb[:], 1.0), writes=["onesb"])
    add("pool", lambda e: e.memset(onesf[:], 1.0), writes=["onesf"])
    add("pool", lambda e: e.memset(vtm[:], 1.0), writes=["vtm"])
    add("pool", lambda e: e.memset(gvpad[:], 0.0), writes=["gvpad"])
    for pc_ in range(2):
        add("pool", lambda e, pc_=pc_: e.memset(kin[pc_][:], 0.0), writes=["kin_%d" % pc_])
        add("pool", lambda e, pc_=pc_: e.memset(kktm[pc_][:], 0.0), writes=["kktm_%d" % pc_])
    for l in range(depth):
        add("dve", lambda e, l=l: e.tensor_scalar(out=negba[:, l, :], in0=vec[:, NVL * l + 18:NVL * l + 20],
                                                  scalar1=-1.0, scalar2=None, op0=ALU.mult),
            reads=["vec"], writes=["negba"])
    identf = cst[:, C_IDENT:C_IDENT + 128]
    triU = cst[:, C_TRIU:C_TRIU + 128]
    sel127 = cst[:, C_SEL127:C_SEL127 + 128]
    segm = cst[:, C_SEG:C_SEG + 512]

    actT_f = actT[:].rearrange("p a b -> p (a b)").bitcast(F32)
    yout = actT_f[:, 0:KC * TT].rearrange("p (c t) -> p c t", c=KC)
    npiece = WTOT // PIECE
    cv = 0
    for l in range(depth):
        for pi in range(npiece):
            s = cv % 2
            stg = actT_f[:, s * PIECE:(s + 1) * PIECE]
            ob = wbuf[s][:, 0:PIECE]
            add("sp", lambda e, l=l, pi=pi, stg=stg: e.dma_start(out=stg, in_=wall_d[l, :, pi * PIECE:(pi + 1) * PIECE]),
                writes=["stg%d" % s], dma_slot="stg%d" % s)
            eng = ("dve", "act", "pool")[cv % 3]
            if eng == "act":
                add("act", lambda e, stg=stg, ob=ob: e.copy(out=ob, in_=stg), reads=["stg%d" % s], writes=["wbuf%d" % s])
            else:
                add(eng, lambda e, stg=stg, ob=ob: e.tensor_copy(out=ob, in_=stg), reads=["stg%d" % s], writes=["wbuf%d" % s])
            add(ST_ENG, lambda e, l=l, pi=pi, ob=ob: e.dma_start(out=wbf_d[l, :, pi * PIECE:(pi + 1) * PIECE], in_=ob),
                reads=["wbuf%d" % s], writes=["wbf%d_%d" % (l, pi)], dma_slot="cvo%d" % s)
            cv += 1
    stg_keys = ["stg0", "stg1"]

    def wload(l, name):
        off, n, ncol = WOFF[name]
        rot["w"] = (rot["w"] + 1) % 3
        s = rot["w"]
        key = "wbuf%d" % s
        buf = wbuf[s]
        pkeys = ["wbf%d_%d" % (l, pi) for pi in range(off // PIECE, (off + n - 1) // PIECE + 1)]
        add("sp", lambda e: e.dma_start(out=buf[:, 0:n], in_=wbf_d[l, :, off:off + n]),
            reads=pkeys, writes=[key], dma_slot=key)
        if name.startswith("dn"):
            view = buf[:, 0:n].rearrange("p (k c) -> p k c", k=NFC)
        else:
            view = buf[:, 0:n].rearrange("p (k c) -> p k c", k=KC)
        return view, key

    def norm(gcol0, out_is_h=True, final=False):
        for c in range(KC):
            rot["sq"] = (rot["sq"] + 1) % 2
            s = rot["sq"]
            if c % 2 == 0:
                add("act", lambda e, c=c, s=s: e.activation(out=sq[s][:], in_=xT[:, c, :], func=AF.Square),
                    reads=["xT%d" % c], writes=["sq%d" % s])
            else:
                add("dve", lambda e, c=c, s=s: e.tensor_tensor(out=sq[s][:], in0=xT[:, c, :], in1=xT[:, c, :], op=ALU.mult),
                    reads=["xT%d" % c], writes=["sq%d" % s])
            mm(aux[:, :], onesb[:], sq[s][:], c == 0, c == KC - 1, ["onesb", "sq%d" % s], ["aux"])
        add("act", lambda e: e.activation(out=rstd[:], in_=aux[:, :], func=AF.Ln, bias=EPS, scale=1.0 / D),
            reads=["aux"], writes=["rstd"])
        add("act", lambda e: e.activation(out=rstd[:], in_=rstd[:], func=AF.Exp, scale=-0.5),
            reads=["rstd"], writes=["rstd"])
        for c in range(KC):
            if final:
                add("dve", lambda e, c=c: e.scalar_tensor_tensor(out=yout[:, c, :], in0=xT[:, c, :], scalar=cvec(gcol0 + c),
                                                                 in1=rstd[:], op0=ALU.mult, op1=ALU.mult),
                    reads=["xT%d" % c, "rstd", "vec"], writes=["act%d" % (2 * c), "act%d" % (2 * c + 1)])
            else:
                add("dve", lambda e, c=c: e.scalar_tensor_tensor(out=hT[:, c, :], in0=xT[:, c, :], scalar=cvec(gcol0 + c),
                                                                 in1=rstd[:], op0=ALU.mult, op1=ALU.mult),
                    reads=["xT%d" % c, "rstd", "vec"], writes=["hT%d" % c])

    hkeys = ["hT%d" % c for c in range(KC)]

    def proj_fm(W, wkey, j, out_ap_fn, ncols=TT):
        bk, bkey = mbank()
        for kc in range(KC):
            mm(bk[:, 0:ncols], W[:, kc, j * 128:(j + 1) * 128], hT[:, kc, :], kc == 0, kc == KC - 1,
               [wkey, "hT%d" % kc], [bkey])
        return bk, bkey

    def tile(l, i):
        vb = NVL * l
        par = i % 2
        first_layer = (l == 0)
        last_layer = (l == depth - 1)
        src = xT_d if first_layer else x1_d
        c0 = i * TT
        L_ = 16 + TT
        add("sp", lambda e: e.dma_start(out=xT[:], in_=src.rearrange("(c p) t -> p c t", p=128)[:, :, c0:c0 + TT]),
            reads=["x1d%d" % i] if not first_layer else [], writes=["xT%d" % c for c in range(KC)], dma_slot="xT")
        P.tag = "norm1"
        norm(vb + 0)

        def ip_pool():
            W, wk = wload(l, "pool")
            for j in range(2):
                bk, bkey = proj_fm(W, wk, j, None)
                add("act", lambda e, j=j, bk=bk: e.copy(out=U[:, j, 16:16 + TT], in_=bk[:, :]), reads=[bkey], writes=["U%d" % j])

        def ip_g1():
            W, wk = wload(l, "g1")
            for j in range(4):
                bk, bkey = proj_fm(W, wk, j, None)
                dst = gqT if j < 2 else gkT
                dkey = ("gq%d" if j < 2 else "gk%d") % (j % 2)
                add("dve", lambda e, j=j, bk=bk, dst=dst: e.tensor_copy(out=dst[:, j % 2, :], in_=bk[:, :]), reads=[bkey], writes=[dkey])

        def ip_g2():
            W, wk = wload(l, "g2")
            for blk in range(NBLK):
                bk, bkey = mbank()
                for kc in range(KC):
                    mm(bk[:, 0:256], hT[:, kc, blk * 128:(blk + 1) * 128], W[:, kc, 0:256], kc == 0, kc == KC - 1,
                       [wk, "hT%d" % kc], [bkey])
                add("dve", lambda e, blk=blk, bk=bk: e.tensor_copy(out=gvraw[:, blk, :], in_=bk[:, 0:256]), reads=[bkey], writes=["gvraw"])
                for hh in range(2):
                    add("act", lambda e, blk=blk, bk=bk, hh=hh: e.copy(
                        out=gvpad[:, blk, :, hh, hh * 64:(hh + 1) * 64],
                        in_=bk[:, 0:256].rearrange("p (a h d) -> p a h d", a=2, h=2)[:, :, hh, :]),
                        reads=[bkey], writes=["gvpad"])
            for j in range(2):
                bk, bkey = proj_fm(W, wk, 2 + j, None)
                add("act", lambda e, j=j, bk=bk: e.activation(out=sg[:, j, :], in_=bk[:, :], func=AF.Silu), reads=[bkey], writes=["sg%d" % j])

        def ip_gs():
            W, wk = wload(l, "gs")
            bk, bkey = mbank()
            for kc in range(KC):
                mm(bk[0:16, :], W[:, kc, 0:16], hT[:, kc, :], kc == 0, kc == KC - 1, [wk, "hT%d" % kc], [bkey])
            add("dve", lambda e, bk=bk: e.tensor_copy(out=gaT[:], in_=bk[0:16, :]), reads=[bkey], writes=["gaT"])
            bk, bkey = mbank()
            for blk in range(NBLK):
                for kc in range(KC):
                    mm(bk[:, blk * 8:(blk + 1) * 8], hT[:, kc, blk * 128:(blk + 1) * 128], W[:, kc, 16:24], kc == 0, kc == KC - 1,
                       [wk, "hT%d" % kc], [bkey])
            add("dve", lambda e, bk=bk: e.tensor_tensor(out=fft[:], in0=bk[:, 0:32], in1=bfb[:, l, :], op=ALU.add),
                reads=[bkey, "bfb"], writes=["fft"])

        def ip_fq():
            W, wk = wload(l, "fq")
            for j in range(4):
                bk, bkey = proj_fm(W, wk, j, None)
                add("act", lambda e, j=j, bk=bk: e.mul(out=qT[:, j, :], in_=bk[:, :], mul=0.125), reads=[bkey], writes=["qT%d" % j])

        def ip_fk():
            W, wk = wload(l, "fk")
            for j in range(4):
                bk, bkey = proj_fm(W, wk, j, None)
                add("dve", lambda e, j=j, bk=bk: e.tensor_copy(out=kT[:, j, :], in_=bk[:, :]), reads=[bkey], writes=["kT%d" % j])
                add(ST_ENG, lambda e, j=j: e.dma_start(out=kc_d[j, :, c0:c0 + TT], in_=kT[:, j, :]),
                    reads=["kT%d" % j], writes=["kc%d_%d" % (j, i)], dma_slot="kst%d" % j)

        def ip_fv():
            W, wk = wload(l, "fv")
            for blk in range(NBLK):
                bk, bkey = mbank()
                for kc in range(KC):
                    mm(bk[:, :], hT[:, kc, blk * 128:(blk + 1) * 128], W[:, kc, :], kc == 0, kc == KC - 1, [wk, "hT%d" % kc], [bkey])
                add("dve", lambda e, blk=blk, bk=bk: e.tensor_copy(
                    out=vtm[:, :, blk, :].rearrange("p a (h d) -> p a h d", h=2)[:, :, :, 0:64],
                    in_=bk[:, :].rearrange("p (a h d) -> p a h d", a=4, h=2)), reads=[bkey], writes=["vtm"])
            for pr in range(4):
                add(ST_ENG, lambda e, pr=pr: e.dma_start(out=vc_d[pr, :, i * NBLK:(i + 1) * NBLK, :], in_=vtm[:, pr, :, :]),
                    reads=["vtm"], writes=["vc%d_%d" % (pr, i)], dma_slot="vst%d" % pr)

        def f_chain():
            add("act", lambda e: e.activation(out=spf[:], in_=fft[:], func=AF.Exp, scale=-1.0), reads=["fft"], writes=["spf"])
            add("act", lambda e: e.activation(out=spf[:], in_=spf[:], func=AF.Ln, bias=1.0, scale=1.0), reads=["spf"], writes=["spf"])
            if i == 0:
                add("dve", lambda e: e.memset(carry[:], 0.0), writes=["carry"])
            for blk in range(NBLK):
                o = aux[:, blk * 8:(blk + 1) * 8]
                mm(o, triU, spf[:, blk * 8:(blk + 1) * 8], True, False, ["cst", "spf"], ["aux"])
                for b2 in range(blk):
                    mm(o, onesf[:], spf[:, b2 * 8:(b2 + 1) * 8], False, False, ["onesf", "spf"], ["aux"])
                mm(o, onesf[0:1, :], carry[0:1, :], False, True, ["onesf", "carry"], ["aux"])
            add("dve", lambda e: e.tensor_copy(out=negF[:, i * 32:(i + 1) * 32], in_=aux[:, 0:32]), reads=["aux"], writes=["negF"])
            mm(aux[:, 64:72], sel127, negF[:, i * 32 + 8:i * 32 + 16], True, True, ["cst", "negF"], ["aux"])
            mm(aux[0:1, 80:88], sel127[:, 0:1], negF[:, i * 32 + 24:i * 32 + 32], True, True, ["cst", "negF"], ["aux"])
            add("dve", lambda e: e.tensor_copy(out=nref[:], in_=aux[:, 64:72]), reads=["aux"], writes=["nref"])
            add("dve", lambda e: e.tensor_copy(out=carry[:], in_=aux[0:1, 80:88]), reads=["aux"], writes=["carry"])

        def pool_elem(j):
            if i == 0:
                add("pool", lambda e, j=j: e.memset(U[:, j, 0:16], 0.0), writes=["Uh%d" % j])
            Uj = U[:, j, :]
            add("pool", lambda e, Uj=Uj: e.tensor_tensor(out=s2[:, 1:L_], in0=Uj[:, 1:L_], in1=Uj[:, 0:L_ - 1], op=ALU.add),
                reads=["U%d" % j, "Uh%d" % j], writes=["s2"])
            add("pool", lambda e: e.tensor_tensor(out=s4[:, 3:L_], in0=s2[:, 3:L_], in1=s2[:, 1:L_ - 2], op=ALU.add),
                reads=["s2"], writes=["s4"])
            if j == 1:
                add("pool", lambda e: e.tensor_tensor(out=s8[:, 7:L_], in0=s4[:, 7:L_], in1=s4[:, 3:L_ - 4], op=ALU.add),
                    reads=["s4"], writes=["s2"])
                add("pool", lambda e: e.tensor_tensor(out=s16[:, 15:L_], in0=s8[:, 15:L_], in1=s8[:, 7:L_ - 8], op=ALU.add),
                    reads=["s2"], writes=["s4"])
                wins = [(s8, "s2", 0.125), (s16, "s4", 0.0625)]
            else:
                wins = [(s2, "s2", 0.5), (s4, "s4", 0.25)]
            for hh in range(2):
                wt, wkey_, invw = wins[hh]
                r0, r1 = hh * 64, (hh + 1) * 64
                add("dve", lambda e, j=j, wt=wt, invw=invw, r0=r0, r1=r1, Uj=Uj: e.scalar_tensor_tensor(
                    out=dT[r0:r1, j, :], in0=wt[r0:r1, 16:L_], scalar=invw, in1=Uj[r0:r1, 16:L_],
                    op0=ALU.mult, op1=ALU.subtract),
                    reads=[wkey_, "U%d" % j], writes=["dT%d" % j])
                if i == 0:
                    add("dve", lambda e, j=j, wt=wt, r0=r0, r1=r1: e.tensor_tensor(
                        out=t1[r0:r1, 0:16], in0=wt[r0:r1, 16:32], in1=cst[r0:r1, C_INVCNT + 16 * j:C_INVCNT + 16 * j + 16],
                        op=ALU.mult), reads=[wkey_, "cst"], writes=["t1"])
                    add("dve", lambda e, j=j, r0=r0, r1=r1, Uj=Uj: e.tensor_tensor(
                        out=dT[r0:r1, j, 0:16], in0=t1[r0:r1, 0:16], in1=Uj[r0:r1, 16:32], op=ALU.subtract),
                        reads=["t1", "U%d" % j, "dT%d" % j], writes=["dT%d" % j])
            add("pool", lambda e, j=j: e.tensor_copy(out=U[:, j, 0:16], in_=U[:, j, TT:TT + 16]),
                reads=["U%d" % j, "s2"], writes=["Uh%d" % j])

        def pool_mm(j):
            bk, bkey = mbank()
            mm(bk[:, :], wp_b[:, l, j, :], dT[:, j, :], True, True, ["wp_b", "dT%d" % j], [bkey])
            add("dve", lambda e, j=j, bk=bk: e.tensor_scalar(out=mixT[:, j, :], in0=bk[:, :], scalar1=cvec(vb + 16 + j), scalar2=None,
                                                              op0=ALU.mult), reads=[bkey, "vec"], writes=["mix%d" % j])

        def gla_a(pc):
            bpc, qinc, kinc, kkvc, qoc, decc = bp[pc], qin[pc], kin[pc], kkv[pc], qo[pc], dec[pc]
            sfx = "_%d" % pc
            bk, bkey = mbank()
            mm(bk[:, :], wa_b[0:16, l, pc * 128:(pc + 1) * 128], gaT[0:16, :], True, True, ["wa_b", "gaT"], [bkey])
            add("act", lambda e, bk=bk: e.activation(out=esb[:], in_=bk[:, :], func=AF.Exp, bias=negba[:, l, pc:pc + 1], scale=-1.0),
                reads=[bkey, "negba"], writes=["Et0"])
            add("act", lambda e: e.activation(out=spg[:], in_=esb[:], func=AF.Ln, bias=1.0, scale=1.0), reads=["Et0"], writes=["Et1"])
            add("dve", lambda e: e.tensor_tensor_scan(out=bpc[:], data0=segm, data1=spg[:], initial=0.0, op0=ALU.mult, op1=ALU.add),
                reads=["cst", "Et1"], writes=["bp" + sfx])
            bpv = bpc[:].rearrange("p (n c) -> p n c", c=64)
            add("dve", lambda e: e.tensor_tensor(out=d1[:].rearrange("p (n c) -> p n c", c=64), in0=bpv,
                                                 in1=bpv[:, :, 31:32].to_broadcast([128, 8, 64]), op=ALU.subtract),
                reads=["bp" + sfx], writes=["d1"])
            add("dve", lambda e: e.tensor_tensor(out=d3[:].rearrange("p (n c) -> p n c", c=64), in0=bpv,
                                                 in1=bpv[:, :, 63:64].to_broadcast([128, 8, 64]), op=ALU.subtract),
                reads=["bp" + sfx], writes=["d3"])
            specs = [(d1, "d1", -1.0 / 16, -LN8, gqT, "gq%d" % pc, qinc, "qin" + sfx),
                     (d1, "d1", 1.0 / 16, 0.0, gkT, "gk%d" % pc, None, "kin" + sfx),
                     (d3, "d3", 1.0 / 16, 0.0, gkT, "gk%d" % pc, kkvc, "kkv" + sfx),
                     (bpc, "bp" + sfx, -1.0 / 16, -LN8, gqT, "gq%d" % pc, qoc, "qo" + sfx)]
            for (srcT, skey, scl, bia, opT, okey, dst, dkey) in specs:
                rot["et"] = (rot["et"] + 1) % 2
                et = Et[rot["et"]]
                ekey = "Et%d" % rot["et"]
                add("act", lambda e, et=et, srcT=srcT, scl=scl, bia=bia: e.activation(out=et[:], in_=srcT[:], func=AF.Exp, bias=bia, scale=scl),
                    reads=[skey], writes=[ekey])
                if dst is None:
                    for hh in range(2):
                        r0, r1 = hh * 64, (hh + 1) * 64
                        add("dve", lambda e, et=et, opT=opT, hh=hh, r0=r0, r1=r1: e.tensor_tensor(
                            out=kinc[r0:r1, hh, :], in0=opT[r0:r1, pc, :], in1=et[r0:r1, :], op=ALU.mult),
                            reads=[ekey, okey], writes=[dkey])
                else:
                    add("dve", lambda e, et=et, opT=opT, dst=dst: e.tensor_tensor(out=dst[:], in0=opT[:, pc, :], in1=et[:], op=ALU.mult),
                        reads=[ekey, okey], writes=[dkey])
            add("act", lambda e: e.activation(out=decc[:], in_=bpv[:, :, 63], func=AF.Exp, scale=-1.0 / 16),
                reads=["bp" + sfx], writes=["dec" + sfx])

        def gla_b(pc):
            kkvc, decc, kktmc = kkv[pc], dec[pc], kktm[pc]
            sfx = "_%d" % pc
            bk, bkey = mbank()
            for blk in range(NBLK):
                add("pe", lambda e, bk=bk, blk=blk: e.transpose(bk[:, blk * 128:(blk + 1) * 128], kkvc[:, blk * 128:(blk + 1) * 128], identf),
                    reads=["kkv" + sfx, "cst"], writes=[bkey])
            for ch in range(2):
                t0, t1_ = ch * 64, (ch + 1) * 64
                add("dve", lambda e, bk=bk, ch=ch, t0=t0, t1_=t1_: e.tensor_copy(
                    out=kktmc[t0:t1_, ch, :, :], in_=bk[t0:t1_, :].rearrange("p (b c) -> p b c", b=NBLK)), reads=[bkey], writes=["kktm" + sfx])
            if i == 0:
                add("dve", lambda e: e.memset(S32[:, pc, :], 0.0), writes=["S32_%d" % pc])
                add("dve", lambda e: e.memset(Sbf[:, pc, 0, 0, :], 0.0), writes=["Sbf%d_0_0" % pc])
            kvb = []
            for half in range(2):
                kb_, kbkey = sbank()
                kvb.append((kb_, kbkey))
                for q4 in range(4):
                    n = half * 4 + q4
                    blk, ch = n // 2, n % 2
                    mm(kb_[:, q4 * 128:(q4 + 1) * 128], kktmc[:, ch, blk, :], gvraw[:, blk, pc * 128:(pc + 1) * 128], True, True,
                       ["kktm" + sfx, "gvraw"], [kbkey])
            for n in range(8):
                kb_, kbkey = kvb[n // 4]
                q4 = n % 4
                for hh in range(2):
                    r0, r1 = hh * 64, (hh + 1) * 64
                    add("dve", lambda e, n=n, hh=hh, r0=r0, r1=r1, kb_=kb_, q4=q4: e.scalar_tensor_tensor(
                        out=S32[r0:r1, pc, hh * 64:(hh + 1) * 64], in0=S32[r0:r1, pc, hh * 64:(hh + 1) * 64],
                        scalar=decc[r0:r1, n:n + 1], in1=kb_[r0:r1, q4 * 128 + hh * 64:q4 * 128 + (hh + 1) * 64],
                        op0=ALU.mult, op1=ALU.add), reads=["S32_%d" % pc, "dec" + sfx, kbkey], writes=["S32_%d" % pc])
                pn, vn = (par, n + 1) if n < 7 else (1 - par, 0)
                add("act", lambda e, pn=pn, vn=vn: e.copy(out=Sbf[:, pc, pn, vn, :], in_=S32[:, pc, :]),
                    reads=["S32_%d" % pc], writes=["Sbf%d_%d_%d" % (pc, pn, vn)])

        def gla_c(pc):
            qinc, kinc, qoc = qin[pc], kin[pc], qo[pc]
            sfx = "_%d" % pc
            ob, okey = obank()
            for blk in range(NBLK):
                ab, abkey = sbank()
                for hh in range(2):
                    mm(ab[:, hh * 128:(hh + 1) * 128], kinc[:, hh, blk * 128:(blk + 1) * 128], qinc[:, blk * 128:(blk + 1) * 128],
                       True, True, ["kin" + sfx, "qin" + sfx], [abkey])
                add("dve", lambda e, ab=ab: e.tensor_tensor(out=attm[:], in0=ab[:, 0:256].rearrange("p (h c) -> p h c", h=2),
                                                            in1=mask2b[:].unsqueeze(1).to_broadcast([128, 2, 128]), op=ALU.mult),
                    reads=[abkey, "mask2b"], writes=["attm"])
                oc = ob[:, blk * 128:(blk + 1) * 128]
                mm(oc, gvpad[:, blk, pc, 0, :], attm[:, 0, :], True, False, ["gvpad", "attm"], [okey])
                mm(oc, gvpad[:, blk, pc, 1, :], attm[:, 1, :], False, False, ["gvpad", "attm"], [okey])
                for ch in range(2):
                    n = blk * 2 + ch
                    occ = ob[:, n * 64:(n + 1) * 64]
                    mm(occ, Sbf[:, pc, par, n, :], qoc[:, n * 64:(n + 1) * 64], False, ch == 1,
                       ["Sbf%d_%d_%d" % (pc, par, n), "qo" + sfx], [okey])
            add("act", lambda e, ob=ob: e.activation(out=osq[:], in_=ob[:, :], func=AF.Square), reads=[okey], writes=["osq"])
            mm(aux[:, :], bonesb[:], osq[:], True, True, ["bonesb", "osq"], ["aux"])
            add("act", lambda e: e.activation(out=rr[:], in_=aux[:, :], func=AF.Ln, bias=EPS, scale=1.0 / 64), reads=["aux"], writes=["rr"])
            add("act", lambda e: e.activation(out=rr[:], in_=rr[:], func=AF.Exp, scale=-0.5), reads=["rr"], writes=["rr"])
            add("dve", lambda e, ob=ob: e.scalar_tensor_tensor(out=t1[:], in0=ob[:, :], scalar=cvec(vb + 20 + pc), in1=rr[:],
                                                               op0=ALU.mult, op1=ALU.mult), reads=[okey, "rr", "vec"], writes=["t1"])
            add("dve", lambda e: e.tensor_tensor(out=mixT[:, 2 + pc, :], in0=t1[:], in1=sg[:, pc, :], op=ALU.mult),
                reads=["t1", "sg%d" % pc], writes=["mix%d" % (2 + pc)])

        nkb = NBLK * (i + 1)
        nseg = (nkb + 15) // 16

        def fox(pr):
            fbs, oaccs = [], []
            for hh in range(2):
                h = 2 * pr + hh
                rot["fb"] = (rot["fb"] + 1) % 2
                fb = fbias[rot["fb"]]
                fbkey = "fbias%d" % rot["fb"]
                add("dve", lambda e, fb=fb, h=h: e.tensor_scalar(
                    out=fb[:, 0:nkb], in0=negF[:, 0:nkb * 8].rearrange("p (b h) -> p b h", h=8)[:, :, h],
                    scalar1=nref[:, h:h + 1], scalar2=None, op0=ALU.subtract),
                    reads=["negF", "nref"], writes=[fbkey])
                fbs.append((fb, fbkey))
                oaccs.append(obank())
            seg_slot = {}

            def seg_load(s_):
                if s_ >= nseg or s_ in seg_slot:
                    return
                nb = min(16, nkb - s_ * 16)
                rot["kv"] = (rot["kv"] + 1) % 2
                sl = rot["kv"]
                seg_slot[s_] = sl
                tiles_in = list(range(s_ * 4, min(s_ * 4 + 4, i + 1)))
                add("sp", lambda e, s_=s_, nb=nb, sl=sl: e.dma_start(out=kseg[sl][:, 0:nb * 128],
                                                                    in_=kc_d[pr, :, s_ * 2048:s_ * 2048 + nb * 128]),
                    reads=["kc%d_%d" % (pr, t) for t in tiles_in], writes=["kseg%d" % sl], dma_slot="kseg%d" % sl)
                add("sp", lambda e, s_=s_, nb=nb, sl=sl: e.dma_start(out=vseg[sl][:, 0:nb, :],
                                                                    in_=vc_d[pr, :, s_ * 16:s_ * 16 + nb, :]),
                    reads=["vc%d_%d" % (pr, t) for t in tiles_in], writes=["vseg%d" % sl], dma_slot="vseg%d" % sl)

            blks = []
            for s_ in range(nseg):
                nb = min(16, nkb - s_ * 16)
                for hh in range(2):
                    for kk in range(nb):
                        blks.append({"s": s_, "hh": hh, "kk": kk})

            def emit_S(bl):
                s_, hh, kk = bl["s"], bl["hh"], bl["kk"]
                if s_ not in seg_slot:
                    seg_load(s_)
                if hh == 0 and kk == 0:
                    seg_load(s_ + 1)
                sl = seg_slot[s_]
                r0, r1 = hh * 64, (hh + 1) * 64
                kb = s_ * 16 + kk
                jd = kb - NBLK * i
                cc0 = 128 * jd if jd > 0 else 0
                sb_, sbkey = sbank()
                bl.update(sl=sl, kb=kb, jd=jd, cc0=cc0, sb=sb_, sbkey=sbkey)
                mm(sb_[:, cc0:TT], kseg[sl][r0:r1, kk * 128:(kk + 1) * 128], qT[r0:r1, pr, cc0:TT], True, jd < 0,
                   ["kseg%d" % sl, "qT%d" % pr], [sbkey])
                if jd >= 0:
                    mm(sb_[:, cc0:TT], identb[:], m0b[:, 0:TT - cc0], False, True, ["identb", "m0b"], [sbkey])

            def emit_rest(bl):
                hh, kk, sl, kb, cc0, sb_, sbkey = bl["hh"], bl["kk"], bl["sl"], bl["kb"], bl["cc0"], bl["sb"], bl["sbkey"]
                fb, fbkey = fbs[hh]
                oacc, oakey = oaccs[hh]
                rot["pt"] = (rot["pt"] + 1) % 3
                pt = PT[rot["pt"]]
                ptkey = "PT%d" % rot["pt"]
                add("act", lambda e: e.activation(out=pt[:, cc0:TT], in_=sb_[:, cc0:TT], func=AF.Exp, bias=fb[:, kb:kb + 1], scale=1.0),
                    reads=[sbkey, fbkey], writes=[ptkey])
                mm(oacc[0:65, cc0:TT], vseg[sl][:, kk, hh * 65:(hh + 1) * 65], pt[:, cc0:TT], kb == 0, kb == nkb - 1,
                   ["vseg%d" % sl, ptkey], [oakey])

            emit_S(blks[0])
            for n_ in range(len(blks)):
                if n_ + 1 < len(blks):
                    emit_S(blks[n_ + 1])
                emit_rest(blks[n_])
            for hh in range(2):
                r0, r1 = hh * 64, (hh + 1) * 64
                oacc, oakey = oaccs[hh]
                rot["osb"] = (rot["osb"] + 1) % 2
                os_ = osb[rot["osb"]]
                oskey = "osb%d" % rot["osb"]
                add("dve", lambda e, os_=os_, oacc=oacc: e.tensor_copy(out=os_[0:65, :], in_=oacc[0:65, :]), reads=[oakey], writes=[oskey])
                add("dve", lambda e, os_=os_: e.reciprocal(out=os_[64:65, :], in_=os_[64:65, :]), reads=[oskey], writes=[oskey])
                mm(aux[0:64, :], onesf[64:65, 0:64], os_[64:65, :], True, True, ["onesf", oskey], ["aux"])
                add("dve", lambda e, os_=os_, pr=pr, r0=r0, r1=r1: e.tensor_tensor(out=mixT[r0:r1, 4 + pr, :], in0=os_[0:64, :],
                                                                                  in1=aux[0:64, :], op=ALU.mult),
                    reads=[oskey, "aux"], writes=["mix%d" % (4 + pr)])


        for fn_, args_ in [(ip_gs, ()), (f_chain, ()), (ip_g1, ()), (gla_a, (0,)), (gla_a, (1,)), (ip_g2, ()), (ip_pool, ()),
                           (ip_fq, ()), (ip_fk, ()), (ip_fv, ()), (pool_elem, (0,)), (pool_elem, (1,)), (gla_b, (0,)), (gla_b, (1,)),
                           (fox, (0,)), (gla_c, (0,)), (fox, (1,)), (gla_c, (1,)), (fox, (2,)), (pool_mm, (0,)), (pool_mm, (1,)),
                           (fox, (3,))]:
            P.tag = fn_.__name__
            fn_(*args_)
        P.tag = "wo"

        for og in range(2):
            W, wk = wload(l, "wo%d" % og)
            for jj in range(4):
                j = og * 4 + jj
                bk, bkey = mbank()
                for kc in range(KC):
                    mm(bk[:, :], W[:, kc, jj * 128:(jj + 1) * 128], mixT[:, kc, :], kc == 0, kc == KC - 1, [wk, "mix%d" % kc], [bkey])
                add("dve", lambda e, j=j, bk=bk: e.tensor_tensor(out=xT[:, j, :], in0=xT[:, j, :], in1=bk[:, :], op=ALU.add),
                    reads=[bkey, "xT%d" % j], writes=["xT%d" % j])
        P.tag = "norm2"
        norm(vb + 8)
        P.tag = "ffn_gu"
        for g, ncol in enumerate(GU_SIZES):
            Wg, wkg = wload(l, "gate%d" % g)
            Wu, wku = wload(l, "up%d" % g)
            for jj in range(ncol // 128):
                f = g * 4 + jj
                bg, bgkey = mbank()
                for kc in range(KC):
                    mm(bg[:, :], Wg[:, kc, jj * 128:(jj + 1) * 128], hT[:, kc, :], kc == 0, kc == KC - 1, [wkg, "hT%d" % kc], [bgkey])
                bu, bukey = mbank()
                for kc in range(KC):
                    mm(bu[:, :], Wu[:, kc, jj * 128:(jj + 1) * 128], hT[:, kc, :], kc == 0, kc == KC - 1, [wku, "hT%d" % kc], [bukey])
                rot["sg"] = (rot["sg"] + 1) % 2
                sgt = sgate[rot["sg"]]
                sgkey = "Et%d" % rot["sg"]
                add("act", lambda e, sgt=sgt, bg=bg: e.activation(out=sgt[:], in_=bg[:, :], func=AF.Silu), reads=[bgkey], writes=[sgkey])
                add("dve", lambda e, sgt=sgt, bu=bu, f=f: e.tensor_tensor(out=actT[:, f, :], in0=sgt[:], in1=bu[:, :], op=ALU.mult),
                    reads=[sgkey, bukey], writes=["act%d" % f] + (stg_keys if f < 6 else []))
        P.tag = "ffn_dn"
        for j in range(KC):
            W, wk = wload(l, "dn%d" % j)
            bk, bkey = mbank()
            for fc in range(NFC):
                mm(bk[:, :], W[:, fc, :], actT[:, fc, :], fc == 0, fc == NFC - 1, [wk, "act%d" % fc], [bkey])
            add("dve", lambda e, j=j, bk=bk: e.tensor_tensor(out=xT[:, j, :], in0=xT[:, j, :], in1=bk[:, :], op=ALU.add),
                reads=[bkey, "xT%d" % j], writes=["xT%d" % j])
        P.tag = "out"
        if last_layer:
            norm(NVL * depth, final=True)
            add(ST_ENG, lambda e: e.dma_start(out=outT_d.rearrange("(c p) t -> p c t", p=128)[:, :, c0:c0 + TT], in_=yout),
                reads=["act%d" % c for c in range(2 * KC)], writes=["outd"], dma_slot="yout")
        else:
            add(ST_ENG, lambda e: e.dma_start(out=x1_d.rearrange("(c p) t -> p c t", p=128)[:, :, c0:c0 + TT], in_=xT[:]),
                reads=["xT%d" % c for c in range(KC)], writes=["x1d%d" % i], dma_slot="x1st")

    try:
        if stop == "conv":
            raise _Stop()
        for l in range(depth):
            for i in range(NT):
                tile(l, i)
    except _Stop:
        pass

    P.emit()
    P.close()
    return nc


def _regroup(w, c0, c1):
    K = w.shape[0]
    kc = K // 128
    return np.ascontiguousarray(w[:, c0:c1].reshape(kc, 128, c1 - c0).transpose(1, 0, 2)).reshape(128, kc * (c1 - c0))


def host_weights(w_in, w_o, w_gu, w_down):
    depth = w_in.shape[0]
    out = np.empty((depth, 128, WTOT), np.float32)
    for l in range(depth):
        wi = w_in[l]
        parts = [
            _regroup(wi, 0, 256),
            _regroup(wi, 256, 768),
            _regroup(wi, 768, 1280),
            _regroup(np.concatenate([wi[:, 1280:1296], wi[:, 2832:2840]], axis=1), 0, 24),
            _regroup(wi, 1296, 1808),
            _regroup(wi, 1808, 2320),
            _regroup(wi, 2320, 2832),
        ]
        for og in range(2):
            parts.append(_regroup(w_o[l], og * 512, (og + 1) * 512))
        o = 0
        for n in GU_SIZES:
            parts.append(_regroup(w_gu[l], o, o + n))
            o += n
        o = 0
        for n in GU_SIZES:
            parts.append(_regroup(w_gu[l], DFF + o, DFF + o + n))
            o += n
        for j in range(8):
            parts.append(_regroup(w_down[l], j * 128, (j + 1) * 128))
        out[l] = np.concatenate(parts, axis=1)
    return out


def host_consts():
    c = np.zeros((128, NCONST), np.float32)
    k = np.arange(128)[:, None]
    q = np.arange(512)[None, :]
    c[:, C_M0:C_M0 + 512] = np.where(q >= k, 0.0, NEG)
    s = np.arange(128)[:, None]
    cc = np.arange(128)[None, :]
    c[:, C_MASK2:C_MASK2 + 128] = ((s // 64 == cc // 64) & (s <= cc)).astype(np.float32)
    seg = np.ones(512, np.float32)
    seg[::64] = 0.0
    c[:, C_SEG:C_SEG + 512] = seg[None, :]
    c[:, C_IDENT:C_IDENT + 128] = np.eye(128, dtype=np.float32)
    c[:, C_TRIU:C_TRIU + 128] = (s <= cc).astype(np.float32)
    c[127, C_SEL127:C_SEL127 + 128] = 1.0
    c[:, C_BONES:C_BONES + 128] = (s // 64 == cc // 64).astype(np.float32)
    wins = [[2, 4], [8, 16]]
    for j in range(2):
        for hh in range(2):
            w = wins[j][hh]
            c[hh * 64:(hh + 1) * 64, C_INVW + j] = 1.0 / w
            t = np.arange(16)
            c[hh * 64:(hh + 1) * 64, C_INVCNT + 16 * j:C_INVCNT + 16 * j + 16] = (1.0 / np.minimum(t + 1, w))[None, :]
    return c


def host_vecs(ln1, ln2, pool_scale, b_a, gla_gn, ln_f):
    depth = ln1.shape[0]
    v = np.zeros((128, NVL * depth + 8), np.float32)
    for l in range(depth):
        b = NVL * l
        v[:, b:b + 8] = ln1[l].reshape(8, 128).T
        v[:, b + 8:b + 16] = ln2[l].reshape(8, 128).T
        v[:, b + 16:b + 18] = pool_scale[l].reshape(2, 128).T
        v[:, b + 18:b + 20] = b_a[l].reshape(2, 128).T
        v[:, b + 20:b + 22] = gla_gn[l].reshape(2, 128).T
    v[:, NVL * depth:] = ln_f.reshape(8, 128).T
    return v


_PROG_CACHE = {}


def kernel(x, ln1, w_in, w_pool, pool_scale, w_a_up, b_a, gla_gn, b_f, w_o, ln2, w_gu, w_down, ln_f):
    x = np.asarray(x, np.float32)
    B, T, _ = x.shape
    depth = np.asarray(w_in).shape[0]
    n = 8
    key = (T, depth)
    if key not in _PROG_CACHE:
        import os
        _PROG_CACHE[key] = build_program(T, depth, stop=os.environ.get("KSTOP"))
    nc = _PROG_CACHE[key]
    wall = host_weights(np.asarray(w_in, np.float32), np.asarray(w_o, np.float32), np.asarray(w_gu, np.float32),
                        np.asarray(w_down, np.float32))
    vecs = host_vecs(np.asarray(ln1, np.float32), np.asarray(ln2, np.float32), np.asarray(pool_scale, np.float32),
                     np.asarray(b_a, np.float32), np.asarray(gla_gn, np.float32), np.asarray(ln_f, np.float32))
    consts = host_consts()
    bfb = np.ascontiguousarray(np.broadcast_to(np.tile(np.asarray(b_f, np.float32), (1, 4))[:, None, :], (depth, 128, 32)))
    in_maps = []
    for c in range(n):
        b = c % B
        wall_c = np.concatenate([wall, np.full((depth, 128, 4), float(c), np.float32)], axis=2)
        in_maps.append({
            "xT": np.ascontiguousarray(x[b].T),
            "wall": wall_c,
            "vecs": vecs,
            "wpool": np.asarray(w_pool, np.float32),
            "waup": np.asarray(w_a_up, np.float32),
            "bfb": bfb,
            "consts": consts,
        })
    res = run_bass_kernel_spmd(nc, in_maps, core_ids=list(range(n)))
    out = np.empty((B, T, D), np.float32)
    for b in range(B):
        out[b] = np.asarray(res.results[b]["outT"], np.float32).T
    return out
```

```python
import contextlib
import numpy as np
import concourse.bass as bass
import concourse.mybir as mybir
from concourse.bass_utils import run_bass_kernel_spmd

F32 = mybir.dt.float32
BF16 = mybir.dt.bfloat16
AF = mybir.ActivationFunctionType
ALU = mybir.AluOpType

ENGS = ("pe", "act", "dve", "pool", "sp")

D = 1024
KC = 8
DEPTH = 2
TT = 512
NBLK = 4
DFF = 2816
NFC = 22
EPS = 1e-6
NEG = -30000.0
ST_ENG = "sp"
LN8 = float(np.log(8.0))

G_IN = [("pool", 256), ("g1", 512), ("g2", 512), ("gs", 24), ("fq", 512), ("fk", 512), ("fv", 512)]
GU_SIZES = [512, 512, 512, 512, 512, 256]


def _offsets():
    off = {}
    o = 0
    for nm, n in G_IN:
        off[nm] = (o, 8 * n, n)
        o += 8 * n
    for og in range(2):
        off["wo%d" % og] = (o, 8 * 512, 512)
        o += 8 * 512
    for g, n in enumerate(GU_SIZES):
        off["gate%d" % g] = (o, 8 * n, n)
        o += 8 * n
    for g, n in enumerate(GU_SIZES):
        off["up%d" % g] = (o, 8 * n, n)
        o += 8 * n
    for j in range(8):
        off["dn%d" % j] = (o, NFC * 128, 128)
        o += NFC * 128
    return off, o


WOFF, WTOT = _offsets()
PIECE = 2052
assert WTOT % PIECE == 0

NVL = 22
C_M0 = 0
C_MASK2 = 512
C_SEG = 640
C_IDENT = 1152
C_TRIU = 1280
C_SEL127 = 1408
C_BONES = 1536
C_INVW = 1664
C_INVCNT = 1666
NCONST = 1698


class Op:
    __slots__ = ("eng", "fn", "deps", "signal", "semval", "dma_sem", "dma_val", "is_dma", "tag")

    def __init__(self, eng, fn, is_dma):
        self.eng = eng
        self.fn = fn
        self.deps = []
        self.signal = False
        self.semval = None
        self.is_dma = is_dma
        self.dma_sem = None
        self.dma_val = None


class Prog:
    def __init__(self, nc):
        self.nc = nc
        self.ops = {e: [] for e in ENGS}
        self.last_writer = {}
        self.readers = {}
        self.dma_slots = {}
        self.n_dma_sems = 0
        self.stack = contextlib.ExitStack()
        self.rot = {}
        self.excl = set()
        self.tag = "setup"

    def sb(self, name, shape, dtype):
        return self.stack.enter_context(self.nc.sbuf_tensor(name, list(shape), dtype))

    def ps(self, name, shape, dtype=F32):
        return self.stack.enter_context(self.nc.psum_tensor(name, list(shape), dtype))

    def add(self, eng, fn, reads=(), writes=(), dma_slot=None):
        op = Op(eng, fn, dma_slot is not None)
        op.tag = self.tag
        deps = {}
        ex = [r for r in reads if r in self.excl]
        if ex:
            reads = [r for r in reads if r not in self.excl]
            writes = list(writes) + ex
        for r in reads:
            w = self.last_writer.get(r)
            if w is not None:
                deps[id(w)] = w
        for k in writes:
            w = self.last_writer.get(k)
            if w is not None:
                deps[id(w)] = w
            for rd in self.readers.get(k, ()):
                deps[id(rd)] = rd
        op.deps = list(deps.values())
        for r in reads:
            rl = self.readers.setdefault(r, [])
            if not op.is_dma:
                for idx in range(len(rl)):
                    if (not rl[idx].is_dma) and rl[idx].eng == eng:
                        rl.pop(idx)
                        break
            rl.append(op)
        for k in writes:
            self.last_writer[k] = op
            self.readers[k] = []
        if dma_slot is not None:
            if dma_slot not in self.dma_slots:
                self.dma_slots[dma_slot] = [self.n_dma_sems, 0]
                self.n_dma_sems += 1
            s = self.dma_slots[dma_slot]
            s[1] += 1
            op.dma_sem = s[0]
            op.dma_val = 16 * s[1]
        self.ops[eng].append(op)
        return op

    def emit(self):
        nc = self.nc
        for e in ENGS:
            for op in self.ops[e]:
                for d in op.deps:
                    if d.is_dma:
                        continue
                    if d.eng == "pe" and op.eng == "pe":
                        continue
                    d.signal = True
        for e in ENGS:
            c = 0
            for op in self.ops[e]:
                if op.signal and not op.is_dma:
                    c += 1
                    op.semval = c
        st = self.stack
        esem = {e: st.enter_context(nc.semaphore("s_" + e)) for e in ENGS if e != "sp"}
        dsem = [st.enter_context(nc.semaphore("d%d" % i)) for i in range(self.n_dma_sems)]
        block = st.enter_context(nc.Block())

        def run(ename, eng):
            waited = {}
            for op in self.ops[ename]:
                need = {}
                for d in op.deps:
                    if d.is_dma:
                        key = ("d", d.dma_sem)
                        val = d.dma_val
                    else:
                        if d.eng == "pe" and ename == "pe":
                            continue
                        key = ("e", d.eng)
                        val = d.semval
                    if val > need.get(key, 0):
                        need[key] = val
                for key, val in need.items():
                    if waited.get(key, 0) >= val:
                        continue
                    waited[key] = val
                    sem = dsem[key[1]] if key[0] == "d" else esem[key[1]]
                    eng.wait_ge(sem, val)
                ins = op.fn(eng)
                if op.is_dma:
                    ins.then_inc(dsem[op.dma_sem], 16)
                elif op.signal:
                    ins.then_inc(esem[ename], 1)
            fin = {}
            for op in self.ops[ename]:
                if op.is_dma:
                    fin[op.dma_sem] = max(fin.get(op.dma_sem, 0), op.dma_val)
            for s, v in fin.items():
                if waited.get(("d", s), 0) < v:
                    eng.wait_ge(dsem[s], v)

        @block.tensor
        def _(pe):
            run("pe", pe)

        @block.scalar
        def _(act):
            run("act", act)

        @block.vector
        def _(dve):
            run("dve", dve)

        @block.gpsimd
        def _(pool):
            run("pool", pool)

        @block.sync
        def _(sp):
            run("sp", sp)

    def close(self):
        self.stack.close()


class _Stop(Exception):
    pass


def build_program(T, depth=DEPTH, own_from=0, stop=None):
    NT = T // TT
    NBT = T // 128
    nc = bass.Bass("TRN2", target_bir_lowering=False)
    P = Prog(nc)
    add = P.add

    xT_d = nc.dram_tensor("xT", [D, T], F32, kind="ExternalInput").ap()
    wall_d = nc.dram_tensor("wall", [depth, 128, WTOT + 4], F32, kind="ExternalInput").ap()
    vecs_d = nc.dram_tensor("vecs", [128, NVL * depth + 8], F32, kind="ExternalInput").ap()
    wpool_d = nc.dram_tensor("wpool", [depth, 4, 64, 64], F32, kind="ExternalInput").ap()
    waup_d = nc.dram_tensor("waup", [depth, 16, 256], F32, kind="ExternalInput").ap()
    bf_d = nc.dram_tensor("bfb", [depth, 128, 32], F32, kind="ExternalInput").ap()
    const_d = nc.dram_tensor("consts", [128, NCONST], F32, kind="ExternalInput").ap()
    outT_d = nc.dram_tensor("outT", [D, T], F32, kind="ExternalOutput").ap()

    wbf_d = nc.dram_tensor("wbf", [depth, 128, WTOT], BF16).ap()
    kc_d = nc.dram_tensor("kcache", [4, 128, T], BF16).ap()
    vc_d = nc.dram_tensor("vcache", [4, 128, NBT, 130], BF16).ap()
    x1_d = nc.dram_tensor("x1T", [D, T], F32).ap()

    xT = P.sb("xTs", [128, KC, TT], F32)
    hT = P.sb("hT", [128, KC, TT], BF16)
    rstd = P.sb("rstd", [128, TT], F32)
    sq = [P.sb("sq%d" % i, [128, TT], BF16) for i in range(2)]
    wbuf = [P.sb("wbuf%d" % i, [128, 4096], BF16) for i in range(3)]
    mixT = P.sb("mixT", [128, KC, TT], BF16)
    actT = P.sb("actT", [128, NFC, TT], BF16)
    qT = P.sb("qT", [128, 4, TT], BF16)
    kT = P.sb("kT", [128, 4, TT], BF16)
    vtm = P.sb("vtm", [128, 4, NBLK, 130], BF16)
    kseg = [P.sb("kseg%d" % i, [128, 2048], BF16) for i in range(2)]
    vseg = [P.sb("vseg%d" % i, [128, 16, 130], BF16) for i in range(2)]
    PT = [P.sb("PT%d" % i, [128, TT], BF16) for i in range(4)]
    osb = [P.sb("osb%d" % i, [128, TT], F32) for i in range(2)]
    negF = P.sb("negF", [128, NBT * 8], F32)
    fbias = [P.sb("fbias%d" % i, [128, NBT], F32) for i in range(2)]
    nref = P.sb("nref", [128, 8], F32)
    fft = P.sb("fft", [128, 32], F32)
    spf = P.sb("spf", [128, 32], F32)
    carry = P.sb("carry", [1, 8], F32)
    bfb = P.sb("bfbs", [128, depth, 32], F32)
    gqT = P.sb("gqT", [128, 2, TT], F32)
    gkT = P.sb("gkT", [128, 2, TT], F32)
    sg = P.sb("sg", [128, 2, TT], BF16)
    gvraw = P.sb("gvraw", [128, NBLK, 256], BF16)
    gvpad = P.sb("gvpad", [128, NBLK, 2, 2, 128], BF16)
    gaT = P.sb("gaT", [16, TT], BF16)
    bp = [P.sb("bp%d" % i, [128, TT], F32) for i in range(2)]
    d1 = P.sb("d1", [128, TT], F32)
    d3 = P.sb("d3", [128, TT], F32)
    Et = [P.sb("Et%d" % i, [128, TT], F32) for i in range(2)]
    esb, spg = Et[0], Et[1]
    qin = [P.sb("qin%d" % i, [128, TT], BF16) for i in range(2)]
    kin = [P.sb("kin%d" % i, [128, 2, TT], BF16) for i in range(2)]
    kkv = [P.sb("kkv%d" % i, [128, TT], F32) for i in range(2)]
    qo = [P.sb("qo%d" % i, [128, TT], BF16) for i in range(2)]
    dec = [P.sb("dec%d" % i, [128, 8], F32) for i in range(2)]
    kktm = [P.sb("kktm%d" % i, [128, 2, NBLK, 128], BF16) for i in range(2)]
    attm = P.sb("attm", [128, 2, 128], BF16)
    S32 = P.sb("S32", [128, 2, 128], F32)
    Sbf = P.sb("Sbf", [128, 2, 2, 8, 128], BF16)
    osq = P.sb("osq", [128, TT], BF16)
    t1 = P.sb("t1", [128, TT], F32)
    rr = P.sb("rr", [128, TT], F32)
    sgate = Et
    wa_f = P.sb("wa_f", [16, depth, 256], F32)
    wa_b = P.sb("wa_b", [16, depth, 256], BF16)
    U = P.sb("U", [128, 2, 16 + TT], F32)
    s2 = P.sb("s2", [128, 16 + TT], F32)
    s4 = P.sb("s4", [128, 16 + TT], F32)
    s8 = s2
    s16 = s4
    dT = P.sb("dT", [128, 2, TT], BF16)
    wp_f = P.sb("wp_f", [128, depth, 2, 128], F32)
    wp_b = P.sb("wp_b", [128, depth, 2, 128], BF16)
    cst = P.sb("cst", [128, NCONST], F32)
    vec = P.sb("vec", [128, NVL * depth + 8], F32)
    negba = P.sb("negba", [128, depth, 2], F32)
    m0b = P.sb("m0b", [128, TT], BF16)
    mask2b = P.sb("mask2b", [128, 128], BF16)
    identb = P.sb("identb", [128, 128], BF16)
    onesb = P.sb("onesb", [128, 128], BF16)
    bonesb = P.sb("bonesb", [128, 128], BF16)
    onesf = P.sb("onesf", [128, 128], F32)

    banks = {}
    for nm in ["m0", "m1", "m2", "s0", "s1", "o0", "o1", "aux"]:
        banks[nm] = P.ps("ps_" + nm, [128, 512], F32)
    rot = {"m": 0, "s": 0, "o": 0, "w": 0, "pt": 0, "sq": 0, "sg": 0, "osb": 0, "et": 0, "kv": 0, "fb": 0}

    def mbank():
        rot["m"] = (rot["m"] + 1) % 3
        k = "m%d" % rot["m"]
        return banks[k], k

    def sbank():
        rot["s"] = (rot["s"] + 1) % 2
        k = "s%d" % rot["s"]
        return banks[k], k

    def obank():
        rot["o"] = (rot["o"] + 1) % 2
        k = "o%d" % rot["o"]
        return banks[k], k

    aux = banks["aux"]
    P.excl = set(banks.keys())

    def mm(out, lhsT, rhs, start, stop, reads, writes):
        add("pe", lambda e: e.matmul(out, lhsT=lhsT, rhs=rhs, start=start, stop=stop), reads, writes)

    def cvec(col):
        return vec[:, col:col + 1]

    add("sp", lambda e: e.dma_start(out=cst[:], in_=const_d[:, :]), writes=["cst"], dma_slot="cst")
    add("sp", lambda e: e.dma_start(out=vec[:], in_=vecs_d[:, :]), writes=["vec"], dma_slot="vec")
    add("sp", lambda e: e.dma_start(out=bfb[:], in_=bf_d.rearrange("l p c -> p l c")), writes=["bfb"], dma_slot="bfb")
    add("sp", lambda e: e.dma_start(out=wa_f[:], in_=waup_d.rearrange("l r c -> r l c")), writes=["wa_f"], dma_slot="wa_f")
    add("pool", lambda e: e.memset(wp_f[:], 0.0), writes=["wp_f"])
    for l in range(depth):
        for g in range(4):
            j, hh = g // 2, g % 2
            add("sp", lambda e, l=l, g=g, j=j, hh=hh: e.dma_start(
                out=wp_f[hh * 64:(hh + 1) * 64, l, j, hh * 64:(hh + 1) * 64], in_=wpool_d[l, g, :, :]),
                reads=[], writes=["wp_f"], dma_slot="wp%d_%d" % (l, g))
    add("dve", lambda e: e.tensor_copy(out=wp_b[:], in_=wp_f[:]), reads=["wp_f"], writes=["wp_b"])
    add("dve", lambda e: e.tensor_copy(out=wa_b[:], in_=wa_f[:]), reads=["wa_f"], writes=["wa_b"])
    add("dve", lambda e: e.tensor_copy(out=m0b[:], in_=cst[:, C_M0:C_M0 + 512]), reads=["cst"], writes=["m0b"])
    add("dve", lambda e: e.tensor_copy(out=mask2b[:], in_=cst[:, C_MASK2:C_MASK2 + 128]), reads=["cst"], writes=["mask2b"])
    add("dve", lambda e: e.tensor_copy(out=identb[:], in_=cst[:, C_IDENT:C_IDENT + 128]), reads=["cst"], writes=["identb"])
    add("dve", lambda e: e.tensor_copy(out=bonesb[:], in_=cst[:, C_BONES:C_BONES + 128]), reads=["cst"], writes=["bonesb"])
    add("pool", lambda e: e.memset(onesb[:], 1.0), writes=["onesb"])
    add("pool", lambda e: e.memset(onesf[:], 1.0), writes=["onesf"])
    add("pool", lambda e: e.memset(vtm[:], 1.0), writes=["vtm"])
    add("pool", lambda e: e.memset(gvpad[:], 0.0), writes=["gvpad"])
    for pc_ in range(2):
        add("pool", lambda e, pc_=pc_: e.memset(kin[pc_][:], 0.0), writes=["kin_%d" % pc_])
        add("pool", lambda e, pc_=pc_: e.memset(kktm[pc_][:], 0.0), writes=["kktm_%d" % pc_])
    for l in range(depth):
        add("dve", lambda e, l=l: e.tensor_scalar(out=negba[:, l, :], in0=vec[:, NVL * l + 18:NVL * l + 20],
                                                  scalar1=-1.0, scalar2=None, op0=ALU.mult),
            reads=["vec"], writes=["negba"])
    identf = cst[:, C_IDENT:C_IDENT + 128]
    triU = cst[:, C_TRIU:C_TRIU + 128]
    sel127 = cst[:, C_SEL127:C_SEL127 + 128]
    segm = cst[:, C_SEG:C_SEG + 512]

    actT_f = actT[:].rearrange("p a b -> p (a b)").bitcast(F32)
    yout = actT_f[:, 0:KC * TT].rearrange("p (c t) -> p c t", c=KC)
    npiece = WTOT // PIECE
    cv = 0
    for l in range(depth):
        for pi in range(npiece):
            s = cv % 2
            stg = actT_f[:, s * PIECE:(s + 1) * PIECE]
            ob = wbuf[s][:, 0:PIECE]
            add("sp", lambda e, l=l, pi=pi, stg=stg: e.dma_start(out=stg, in_=wall_d[l, :, pi * PIECE:(pi + 1) * PIECE]),
                writes=["stg%d" % s], dma_slot="stg%d" % s)
            eng = ("dve", "act", "pool")[cv % 3]
            if eng == "act":
                add("act", lambda e, stg=stg, ob=ob: e.copy(out=ob, in_=stg), reads=["stg%d" % s], writes=["wbuf%d" % s])
            else:
                add(eng, lambda e, stg=stg, ob=ob: e.tensor_copy(out=ob, in_=stg), reads=["stg%d" % s], writes=["wbuf%d" % s])
            add(ST_ENG, lambda e, l=l, pi=pi, ob=ob: e.dma_start(out=wbf_d[l, :, pi * PIECE:(pi + 1) * PIECE], in_=ob),
                reads=["wbuf%d" % s], writes=["wbf%d_%d" % (l, pi)], dma_slot="cvo%d" % s)
            cv += 1
    stg_keys = ["stg0", "stg1"]

    def wload(l, name):
        off, n, ncol = WOFF[name]
        rot["w"] = (rot["w"] + 1) % 3
        s = rot["w"]
        key = "wbuf%d" % s
        buf = wbuf[s]
        pkeys = ["wbf%d_%d" % (l, pi) for pi in range(off // PIECE, (off + n - 1) // PIECE + 1)]
        add("sp", lambda e: e.dma_start(out=buf[:, 0:n], in_=wbf_d[l, :, off:off + n]),
            reads=pkeys, writes=[key], dma_slot=key)
        if name.startswith("dn"):
            view = buf[:, 0:n].rearrange("p (k c) -> p k c", k=NFC)
        else:
            view = buf[:, 0:n].rearrange("p (k c) -> p k c", k=KC)
        return view, key

    def norm(gcol0, out_is_h=True, final=False):
        for c in range(KC):
            rot["sq"] = (rot["sq"] + 1) % 2
            s = rot["sq"]
            if c % 2 == 0:
                add("act", lambda e, c=c, s=s: e.activation(out=sq[s][:], in_=xT[:, c, :], func=AF.Square),
                    reads=["xT%d" % c], writes=["sq%d" % s])
            else:
                add("dve", lambda e, c=c, s=s: e.tensor_tensor(out=sq[s][:], in0=xT[:, c, :], in1=xT[:, c, :], op=ALU.mult),
                    reads=["xT%d" % c], writes=["sq%d" % s])
            mm(aux[:, :], onesb[:], sq[s][:], c == 0, c == KC - 1, ["onesb", "sq%d" % s], ["aux"])
        add("act", lambda e: e.activation(out=rstd[:], in_=aux[:, :], func=AF.Ln, bias=EPS, scale=1.0 / D),
            reads=["aux"], writes=["rstd"])
        add("act", lambda e: e.activation(out=rstd[:], in_=rstd[:], func=AF.Exp, scale=-0.5),
            reads=["rstd"], writes=["rstd"])
        for c in range(KC):
            if final:
                add("dve", lambda e, c=c: e.scalar_tensor_tensor(out=yout[:, c, :], in0=xT[:, c, :], scalar=cvec(gcol0 + c),
                                                                 in1=rstd[:], op0=ALU.mult, op1=ALU.mult),
                    reads=["xT%d" % c, "rstd", "vec"], writes=["act%d" % (2 * c), "act%d" % (2 * c + 1)])
            else:
                add("dve", lambda e, c=c: e.scalar_tensor_tensor(out=hT[:, c, :], in0=xT[:, c, :], scalar=cvec(gcol0 + c),
                                                                 in1=rstd[:], op0=ALU.mult, op1=ALU.mult),
                    reads=["xT%d" % c, "rstd", "vec"], writes=["hT%d" % c])

    hkeys = ["hT%d" % c for c in range(KC)]

    def proj_fm(W, wkey, j, out_ap_fn, ncols=TT):
        bk, bkey = mbank()
        for kc in range(KC):
            mm(bk[:, 0:ncols], W[:, kc, j * 128:(j + 1) * 128], hT[:, kc, :], kc == 0, kc == KC - 1,
               [wkey, "hT%d" % kc], [bkey])
        return bk, bkey

    def tile(l, i):
        vb = NVL * l
        par = i % 2
        first_layer = (l == 0)
        last_layer = (l == depth - 1)
        src = xT_d if first_layer else x1_d
        c0 = i * TT
        L_ = 16 + TT
        add("sp", lambda e: e.dma_start(out=xT[:], in_=src.rearrange("(c p) t -> p c t", p=128)[:, :, c0:c0 + TT]),
            reads=["x1d%d" % i] if not first_layer else [], writes=["xT%d" % c for c in range(KC)], dma_slot="xT")
        P.tag = "norm1"
        norm(vb + 0)

        def ip_pool():
            W, wk = wload(l, "pool")
            for j in range(2):
                bk, bkey = proj_fm(W, wk, j, None)
                add("act", lambda e, j=j, bk=bk: e.copy(out=U[:, j, 16:16 + TT], in_=bk[:, :]), reads=[bkey], writes=["U%d" % j])

        def ip_g1():
            W, wk = wload(l, "g1")
            for j in range(4):
                bk, bkey = proj_fm(W, wk, j, None)
                dst = gqT if j < 2 else gkT
                dkey = ("gq%d" if j < 2 else "gk%d") % (j % 2)
                add("dve", lambda e, j=j, bk=bk, dst=dst: e.tensor_copy(out=dst[:, j % 2, :], in_=bk[:, :]), reads=[bkey], writes=[dkey])

        def ip_g2():
            W, wk = wload(l, "g2")
            for blk in range(NBLK):
                bk, bkey = mbank()
                for kc in range(KC):
                    mm(bk[:, 0:256], hT[:, kc, blk * 128:(blk + 1) * 128], W[:, kc, 0:256], kc == 0, kc == KC - 1,
                       [wk, "hT%d" % kc], [bkey])
                add("dve", lambda e, blk=blk, bk=bk: e.tensor_copy(out=gvraw[:, blk, :], in_=bk[:, 0:256]), reads=[bkey], writes=["gvraw"])
                for hh in range(2):
                    add("act", lambda e, blk=blk, bk=bk, hh=hh: e.copy(
                        out=gvpad[:, blk, :, hh, hh * 64:(hh + 1) * 64],
                        in_=bk[:, 0:256].rearrange("p (a h d) -> p a h d", a=2, h=2)[:, :, hh, :]),
                        reads=[bkey], writes=["gvpad"])
            for j in range(2):
                bk, bkey = proj_fm(W, wk, 2 + j, None)
                add("act", lambda e, j=j, bk=bk: e.activation(out=sg[:, j, :], in_=bk[:, :], func=AF.Silu), reads=[bkey], writes=["sg%d" % j])

        def ip_gs():
            W, wk = wload(l, "gs")
            bk, bkey = mbank()
            for kc in range(KC):
                mm(bk[0:16, :], W[:, kc, 0:16], hT[:, kc, :], kc == 0, kc == KC - 1, [wk, "hT%d" % kc], [bkey])
            add("dve", lambda e, bk=bk: e.tensor_copy(out=gaT[:], in_=bk[0:16, :]), reads=[bkey], writes=["gaT"])
            bk, bkey = mbank()
            for blk in range(NBLK):
                for kc in range(KC):
                    mm(bk[:, blk * 8:(blk + 1) * 8], hT[:, kc, blk * 128:(blk + 1) * 128], W[:, kc, 16:24], kc == 0, kc == KC - 1,
                       [wk, "hT%d" % kc], [bkey])
            add("dve", lambda e, bk=bk: e.tensor_tensor(out=fft[:], in0=bk[:, 0:32], in1=bfb[:, l, :], op=ALU.add),
                reads=[bkey, "bfb"], writes=["fft"])

        def ip_fq():
            W, wk = wload(l, "fq")
            for j in range(4):
                bk, bkey = proj_fm(W, wk, j, None)
                add("act", lambda e, j=j, bk=bk: e.mul(out=qT[:, j, :], in_=bk[:, :], mul=0.125), reads=[bkey], writes=["qT%d" % j])

        def ip_fk():
            W, wk = wload(l, "fk")
            for j in range(4):
                bk, bkey = proj_fm(W, wk, j, None)
                add("dve", lambda e, j=j, bk=bk: e.tensor_copy(out=kT[:, j, :], in_=bk[:, :]), reads=[bkey], writes=["kT%d" % j])
                add(ST_ENG, lambda e, j=j: e.dma_start(out=kc_d[j, :, c0:c0 + TT], in_=kT[:, j, :]),
                    reads=["kT%d" % j], writes=["kc%d_%d" % (j, i)], dma_slot="kst%d" % j)

        def ip_fv():
            W, wk = wload(l, "fv")
            for blk in range(NBLK):
                bk, bkey = mbank()
                for kc in range(KC):
                    mm(bk[:, :], hT[:, kc, blk * 128:(blk + 1) * 128], W[:, kc, :], kc == 0, kc == KC - 1, [wk, "hT%d" % kc], [bkey])
                add("dve", lambda e, blk=blk, bk=bk: e.tensor_copy(
                    out=vtm[:, :, blk, :].rearrange("p a (h d) -> p a h d", h=2)[:, :, :, 0:64],
                    in_=bk[:, :].rearrange("p (a h d) -> p a h d", a=4, h=2)), reads=[bkey], writes=["vtm"])
            for pr in range(4):
                add(ST_ENG, lambda e, pr=pr: e.dma_start(out=vc_d[pr, :, i * NBLK:(i + 1) * NBLK, :], in_=vtm[:, pr, :, :]),
                    reads=["vtm"], writes=["vc%d_%d" % (pr, i)], dma_slot="vst%d" % pr)

        def f_chain():
            add("act", lambda e: e.activation(out=spf[:], in_=fft[:], func=AF.Exp, scale=-1.0), reads=["fft"], writes=["spf"])
            add("act", lambda e: e.activation(out=spf[:], in_=spf[:], func=AF.Ln, bias=1.0, scale=1.0), reads=["spf"], writes=["spf"])
            if i == 0:
                add("dve", lambda e: e.memset(carry[:], 0.0), writes=["carry"])
            for blk in range(NBLK):
                o = aux[:, blk * 8:(blk + 1) * 8]
                mm(o, triU, spf[:, blk * 8:(blk + 1) * 8], True, False, ["cst", "spf"], ["aux"])
                for b2 in range(blk):
                    mm(o, onesf[:], spf[:, b2 * 8:(b2 + 1) * 8], False, False, ["onesf", "spf"], ["aux"])
                mm(o, onesf[0:1, :], carry[0:1, :], False, True, ["onesf", "carry"], ["aux"])
            add("dve", lambda e: e.tensor_copy(out=negF[:, i * 32:(i + 1) * 32], in_=aux[:, 0:32]), reads=["aux"], writes=["negF"])
            mm(aux[:, 64:72], sel127, negF[:, i * 32 + 8:i * 32 + 16], True, True, ["cst", "negF"], ["aux"])
            mm(aux[0:1, 80:88], sel127[:, 0:1], negF[:, i * 32 + 24:i * 32 + 32], True, True, ["cst", "negF"], ["aux"])
            add("dve", lambda e: e.tensor_copy(out=nref[:], in_=aux[:, 64:72]), reads=["aux"], writes=["nref"])
            add("dve", lambda e: e.tensor_copy(out=carry[:], in_=aux[0:1, 80:88]), reads=["aux"], writes=["carry"])

        def pool_elem(j):
            if i == 0:
                add("pool", lambda e, j=j: e.memset(U[:, j, 0:16], 0.0), writes=["Uh%d" % j])
            Uj = U[:, j, :]
            add("pool", lambda e, Uj=Uj: e.tensor_tensor(out=s2[:, 1:L_], in0=Uj[:, 1:L_], in1=Uj[:, 0:L_ - 1], op=ALU.add),
                reads=["U%d" % j, "Uh%d" % j], writes=["s2"])
            add("pool", lambda e: e.tensor_tensor(out=s4[:, 3:L_], in0=s2[:, 3:L_], in1=s2[:, 1:L_ - 2], op=ALU.add),
                reads=["s2"], writes=["s4"])
            if j == 1:
                add("pool", lambda e: e.tensor_tensor(out=s8[:, 7:L_], in0=s4[:, 7:L_], in1=s4[:, 3:L_ - 4], op=ALU.add),
                    reads=["s4"], writes=["s2"])
                add("pool", lambda e: e.tensor_tensor(out=s16[:, 15:L_], in0=s8[:, 15:L_], in1=s8[:, 7:L_ - 8], op=ALU.add),
                    reads=["s2"], writes=["s4"])
                wins = [(s8, "s2", 0.125), (s16, "s4", 0.0625)]
            else:
                wins = [(s2, "s2", 0.5), (s4, "s4", 0.25)]
            for hh in range(2):
                wt, wkey_, invw = wins[hh]
                r0, r1 = hh * 64, (hh + 1) * 64
                add("dve", lambda e, j=j, wt=wt, invw=invw, r0=r0, r1=r1, Uj=Uj: e.scalar_tensor_tensor(
                    out=dT[r0:r1, j, :], in0=wt[r0:r1, 16:L_], scalar=invw, in1=Uj[r0:r1, 16:L_],
                    op0=ALU.mult, op1=ALU.subtract),
                    reads=[wkey_, "U%d" % j], writes=["dT%d" % j])
                if i == 0:
                    add("dve", lambda e, j=j, wt=wt, r0=r0, r1=r1: e.tensor_tensor(
                        out=t1[r0:r1, 0:16], in0=wt[r0:r1, 16:32], in1=cst[r0:r1, C_INVCNT + 16 * j:C_INVCNT + 16 * j + 16],
                        op=ALU.mult), reads=[wkey_, "cst"], writes=["t1"])
                    add("dve", lambda e, j=j, r0=r0, r1=r1, Uj=Uj: e.tensor_tensor(
                        out=dT[r0:r1, j, 0:16], in0=t1[r0:r1, 0:16], in1=Uj[r0:r1, 16:32], op=ALU.subtract),
                        reads=["t1", "U%d" % j, "dT%d" % j], writes=["dT%d" % j])
            add("pool", lambda e, j=j: e.tensor_copy(out=U[:, j, 0:16], in_=U[:, j, TT:TT + 16]),
                reads=["U%d" % j, "s2"], writes=["Uh%d" % j])

        def pool_mm(j):
            bk, bkey = mbank()
            mm(bk[:, :], wp_b[:, l, j, :], dT[:, j, :], True, True, ["wp_b", "dT%d" % j], [bkey])
            add("dve", lambda e, j=j, bk=bk: e.tensor_scalar(out=mixT[:, j, :], in0=bk[:, :], scalar1=cvec(vb + 16 + j), scalar2=None,
                                                              op0=ALU.mult), reads=[bkey, "vec"], writes=["mix%d" % j])

        def gla_a(pc):
            bpc, qinc, kinc, kkvc, qoc, decc = bp[pc], qin[pc], kin[pc], kkv[pc], qo[pc], dec[pc]
            sfx = "_%d" % pc
            bk, bkey = mbank()
            mm(bk[:, :], wa_b[0:16, l, pc * 128:(pc + 1) * 128], gaT[0:16, :], True, True, ["wa_b", "gaT"], [bkey])
            add("act", lambda e, bk=bk: e.activation(out=esb[:], in_=bk[:, :], func=AF.Exp, bias=negba[:, l, pc:pc + 1], scale=-1.0),
                reads=[bkey, "negba"], writes=["Et0"])
            add("act", lambda e: e.activation(out=spg[:], in_=esb[:], func=AF.Ln, bias=1.0, scale=1.0), reads=["Et0"], writes=["Et1"])
            add("dve", lambda e: e.tensor_tensor_scan(out=bpc[:], data0=segm, data1=spg[:], initial=0.0, op0=ALU.mult, op1=ALU.add),
                reads=["cst", "Et1"], writes=["bp" + sfx])
            bpv = bpc[:].rearrange("p (n c) -> p n c", c=64)
            add("dve", lambda e: e.tensor_tensor(out=d1[:].rearrange("p (n c) -> p n c", c=64), in0=bpv,
                                                 in1=bpv[:, :, 31:32].to_broadcast([128, 8, 64]), op=ALU.subtract),
                reads=["bp" + sfx], writes=["d1"])
            add("dve", lambda e: e.tensor_tensor(out=d3[:].rearrange("p (n c) -> p n c", c=64), in0=bpv,
                                                 in1=bpv[:, :, 63:64].to_broadcast([128, 8, 64]), op=ALU.subtract),
                reads=["bp" + sfx], writes=["d3"])
            specs = [(d1, "d1", -1.0 / 16, -LN8, gqT, "gq%d" % pc, qinc, "qin" + sfx),
                     (d1, "d1", 1.0 / 16, 0.0, gkT, "gk%d" % pc, None, "kin" + sfx),
                     (d3, "d3", 1.0 / 16, 0.0, gkT, "gk%d" % pc, kkvc, "kkv" + sfx),
                     (bpc, "bp" + sfx, -1.0 / 16, -LN8, gqT, "gq%d" % pc, qoc, "qo" + sfx)]
            for (srcT, skey, scl, bia, opT, okey, dst, dkey) in specs:
                rot["et"] = (rot["et"] + 1) % 2
                et = Et[rot["et"]]
                ekey = "Et%d" % rot["et"]
                add("act", lambda e, et=et, srcT=srcT, scl=scl, bia=bia: e.activation(out=et[:], in_=srcT[:], func=AF.Exp, bias=bia, scale=scl),
                    reads=[skey], writes=[ekey])
                if dst is None:
                    for hh in range(2):
                        r0, r1 = hh * 64, (hh + 1) * 64
                        add("dve", lambda e, et=et, opT=opT, hh=hh, r0=r0, r1=r1: e.tensor_tensor(
                            out=kinc[r0:r1, hh, :], in0=opT[r0:r1, pc, :], in1=et[r0:r1, :], op=ALU.mult),
                            reads=[ekey, okey], writes=[dkey])
                else:
                    add("dve", lambda e, et=et, opT=opT, dst=dst: e.tensor_tensor(out=dst[:], in0=opT[:, pc, :], in1=et[:], op=ALU.mult),
                        reads=[ekey, okey], writes=[dkey])
            add("act", lambda e: e.activation(out=decc[:], in_=bpv[:, :, 63], func=AF.Exp, scale=-1.0 / 16),
                reads=["bp" + sfx], writes=["dec" + sfx])

        def gla_b(pc):
            kkvc, decc, kktmc = kkv[pc], dec[pc], kktm[pc]
            sfx = "_%d" % pc
            bk, bkey = mbank()
            for blk in range(NBLK):
                add("pe", lambda e, bk=bk, blk=blk: e.transpose(bk[:, blk * 128:(blk + 1) * 128], kkvc[:, blk * 128:(blk + 1) * 128], identf),
                    reads=["kkv" + sfx, "cst"], writes=[bkey])
            for ch in range(2):
                t0, t1_ = ch * 64, (ch + 1) * 64
                add("dve", lambda e, bk=bk, ch=ch, t0=t0, t1_=t1_: e.tensor_copy(
                    out=kktmc[t0:t1_, ch, :, :], in_=bk[t0:t1_, :].rearrange("p (b c) -> p b c", b=NBLK)), reads=[bkey], writes=["kktm" + sfx])
            if i == 0:
                add("dve", lambda e: e.memset(S32[:, pc, :], 0.0), writes=["S32_%d" % pc])
                add("dve", lambda e: e.memset(Sbf[:, pc, 0, 0, :], 0.0), writes=["Sbf%d_0_0" % pc])
            kvb = []
            for half in range(2):
                kb_, kbkey = sbank()
                kvb.append((kb_, kbkey))
                for q4 in range(4):
                    n = half * 4 + q4
                    blk, ch = n // 2, n % 2
                    mm(kb_[:, q4 * 128:(q4 + 1) * 128], kktmc[:, ch, blk, :], gvraw[:, blk, pc * 128:(pc + 1) * 128], True, True,
                       ["kktm" + sfx, "gvraw"], [kbkey])
            for n in range(8):
                kb_, kbkey = kvb[n // 4]
                q4 = n % 4
                for hh in range(2):
                    r0, r1 = hh * 64, (hh + 1) * 64
                    add("dve", lambda e, n=n, hh=hh, r0=r0, r1=r1, kb_=kb_, q4=q4: e.scalar_tensor_tensor(
                        out=S32[r0:r1, pc, hh * 64:(hh + 1) * 64], in0=S32[r0:r1, pc, hh * 64:(hh + 1) * 64],
                        scalar=decc[r0:r1, n:n + 1], in1=kb_[r0:r1, q4 * 128 + hh * 64:q4 * 128 + (hh + 1) * 64],
                        op0=ALU.mult, op1=ALU.add), reads=["S32_%d" % pc, "dec" + sfx, kbkey], writes=["S32_%d" % pc])
                pn, vn = (par, n + 1) if n < 7 else (1 - par, 0)
                add("act", lambda e, pn=pn, vn=vn: e.copy(out=Sbf[:, pc, pn, vn, :], in_=S32[:, pc, :]),
                    reads=["S32_%d" % pc], writes=["Sbf%d_%d_%d" % (pc, pn, vn)])

        def gla_c(pc):
            qinc, kinc, qoc = qin[pc], kin[pc], qo[pc]
            sfx = "_%d" % pc
            ob, okey = obank()
            for blk in range(NBLK):
                ab, abkey = sbank()
                for hh in range(2):
                    mm(ab[:, hh * 128:(hh + 1) * 128], kinc[:, hh, blk * 128:(blk + 1) * 128], qinc[:, blk * 128:(blk + 1) * 128],
                       True, True, ["kin" + sfx, "qin" + sfx], [abkey])
                add("dve", lambda e, ab=ab: e.tensor_tensor(out=attm[:], in0=ab[:, 0:256].rearrange("p (h c) -> p h c", h=2),
                                                            in1=mask2b[:].unsqueeze(1).to_broadcast([128, 2, 128]), op=ALU.mult),
                    reads=[abkey, "mask2b"], writes=["attm"])
                oc = ob[:, blk * 128:(blk + 1) * 128]
                mm(oc, gvpad[:, blk, pc, 0, :], attm[:, 0, :], True, False, ["gvpad", "attm"], [okey])
                mm(oc, gvpad[:, blk, pc, 1, :], attm[:, 1, :], False, False, ["gvpad", "attm"], [okey])
                for ch in range(2):
                    n = blk * 2 + ch
                    occ = ob[:, n * 64:(n + 1) * 64]
                    mm(occ, Sbf[:, pc, par, n, :], qoc[:, n * 64:(n + 1) * 64], False, ch == 1,
                       ["Sbf%d_%d_%d" % (pc, par, n), "qo" + sfx], [okey])
            add("act", lambda e, ob=ob: e.activation(out=osq[:], in_=ob[:, :], func=AF.Square), reads=[okey], writes=["osq"])
            mm(aux[:, :], bonesb[:], osq[:], True, True, ["bonesb", "osq"], ["aux"])
            add("act", lambda e: e.activation(out=rr[:], in_=aux[:, :], func=AF.Ln, bias=EPS, scale=1.0 / 64), reads=["aux"], writes=["rr"])
            add("act", lambda e: e.activation(out=rr[:], in_=rr[:], func=AF.Exp, scale=-0.5), reads=["rr"], writes=["rr"])
            add("dve", lambda e, ob=ob: e.scalar_tensor_tensor(out=t1[:], in0=ob[:, :], scalar=cvec(vb + 20 + pc), in1=rr[:],
                                                               op0=ALU.mult, op1=ALU.mult), reads=[okey, "rr", "vec"], writes=["t1"])
            add("dve", lambda e: e.tensor_tensor(out=mixT[:, 2 + pc, :], in0=t1[:], in1=sg[:, pc, :], op=ALU.mult),
                reads=["t1", "sg%d" % pc], writes=["mix%d" % (2 + pc)])

        nkb = NBLK * (i + 1)
        nseg = (nkb + 15) // 16

        def fox(pr):
            fbs, oaccs = [], []
            for hh in range(2):
                h = 2 * pr + hh
                rot["fb"] = (rot["fb"] + 1) % 2
                fb = fbias[rot["fb"]]
                fbkey = "fbias%d" % rot["fb"]
                add("dve", lambda e, fb=fb, h=h: e.tensor_scalar(
                    out=fb[:, 0:nkb], in0=negF[:, 0:nkb * 8].rearrange("p (b h) -> p b h", h=8)[:, :, h],
                    scalar1=nref[:, h:h + 1], scalar2=None, op0=ALU.subtract),
                    reads=["negF", "nref"], writes=[fbkey])
                fbs.append((fb, fbkey))
                oaccs.append(obank())
            seg_slot = {}

            def seg_load(s_):
                if s_ >= nseg or s_ in seg_slot:
                    return
                nb = min(16, nkb - s_ * 16)
                rot["kv"] = (rot["kv"] + 1) % 2
                sl = rot["kv"]
                seg_slot[s_] = sl
                tiles_in = list(range(s_ * 4, min(s_ * 4 + 4, i + 1)))
                add("sp", lambda e, s_=s_, nb=nb, sl=sl: e.dma_start(out=kseg[sl][:, 0:nb * 128],
                                                                    in_=kc_d[pr, :, s_ * 2048:s_ * 2048 + nb * 128]),
                    reads=["kc%d_%d" % (pr, t) for t in tiles_in], writes=["kseg%d" % sl], dma_slot="kseg%d" % sl)
                add("sp", lambda e, s_=s_, nb=nb, sl=sl: e.dma_start(out=vseg[sl][:, 0:nb, :],
                                                                    in_=vc_d[pr, :, s_ * 16:s_ * 16 + nb, :]),
                    reads=["vc%d_%d" % (pr, t) for t in tiles_in], writes=["vseg%d" % sl], dma_slot="vseg%d" % sl)

            steps = []
            for s_ in range(nseg):
                nb = min(16, nkb - s_ * 16)
                for kk in range(nb):
                    steps.append({"s": s_, "kk": kk})
            spairs = [("s0", "s1"), ("m1", "m2")]

            def emit_S(st, n_):
                s_, kk = st["s"], st["kk"]
                if s_ not in seg_slot:
                    seg_load(s_)
                if kk == 0:
                    seg_load(s_ + 1)
                sl = seg_slot[s_]
                kb = s_ * 16 + kk
                jd = kb - NBLK * i
                cc0 = 128 * jd if jd > 0 else 0
                bks = spairs[n_ % 2]
                st.update(sl=sl, kb=kb, jd=jd, cc0=cc0, bks=bks)
                for hh in range(2):
                    r0, r1 = hh * 64, (hh + 1) * 64
                    mm(banks[bks[hh]][:, cc0:TT], kseg[sl][r0:r1, kk * 128:(kk + 1) * 128], qT[r0:r1, pr, cc0:TT], True, jd < 0,
                       ["kseg%d" % sl, "qT%d" % pr], [bks[hh]])
                if jd >= 0:
                    for hh in range(2):
                        mm(banks[bks[hh]][:, cc0:TT], identb[:], m0b[:, 0:TT - cc0], False, True, ["identb", "m0b"], [bks[hh]])

            def emit_rest(st):
                kk, sl, kb, cc0, bks = st["kk"], st["sl"], st["kb"], st["cc0"], st["bks"]
                pts = []
                for hh in range(2):
                    fb, fbkey = fbs[hh]
                    rot["pt"] = (rot["pt"] + 1) % 4
                    pt = PT[rot["pt"]]
                    ptkey = "PT%d" % rot["pt"]
                    pts.append((pt, ptkey))
                    sb_ = banks[bks[hh]]
                    add("act", lambda e, pt=pt, sb_=sb_, fb=fb: e.activation(out=pt[:, cc0:TT], in_=sb_[:, cc0:TT], func=AF.Exp,
                                                                            bias=fb[:, kb:kb + 1], scale=1.0),
                        reads=[bks[hh], fbkey], writes=[ptkey])
                for hh in range(2):
                    oacc, oakey = oaccs[hh]
                    pt, ptkey = pts[hh]
                    mm(oacc[0:65, cc0:TT], vseg[sl][:, kk, hh * 65:(hh + 1) * 65], pt[:, cc0:TT], kb == 0, kb == nkb - 1,
                       ["vseg%d" % sl, ptkey], [oakey])

            emit_S(steps[0], 0)
            for n_ in range(len(steps)):
                if n_ + 1 < len(steps):
                    emit_S(steps[n_ + 1], n_ + 1)
                emit_rest(steps[n_])
            for hh in range(2):
                r0, r1 = hh * 64, (hh + 1) * 64
                oacc, oakey = oaccs[hh]
                rot["osb"] = (rot["osb"] + 1) % 2
                os_ = osb[rot["osb"]]
                oskey = "osb%d" % rot["osb"]
                add("dve", lambda e, os_=os_, oacc=oacc: e.tensor_copy(out=os_[0:65, :], in_=oacc[0:65, :]), reads=[oakey], writes=[oskey])
                add("dve", lambda e, os_=os_: e.reciprocal(out=os_[64:65, :], in_=os_[64:65, :]), reads=[oskey], writes=[oskey])
                mm(aux[0:64, :], onesf[64:65, 0:64], os_[64:65, :], True, True, ["onesf", oskey], ["aux"])
                add("dve", lambda e, os_=os_, pr=pr, r0=r0, r1=r1: e.tensor_tensor(out=mixT[r0:r1, 4 + pr, :], in0=os_[0:64, :],
                                                                                  in1=aux[0:64, :], op=ALU.mult),
                    reads=[oskey, "aux"], writes=["mix%d" % (4 + pr)])


        for fn_, args_ in [(ip_gs, ()), (f_chain, ()), (ip_g1, ()), (gla_a, (0,)), (gla_a, (1,)), (ip_g2, ()), (ip_pool, ()),
                           (ip_fq, ()), (ip_fk, ()), (ip_fv, ()), (pool_elem, (0,)), (pool_elem, (1,)), (gla_b, (0,)), (gla_b, (1,)),
                           (fox, (0,)), (gla_c, (0,)), (fox, (1,)), (gla_c, (1,)), (fox, (2,)), (pool_mm, (0,)), (pool_mm, (1,)),
                           (fox, (3,))]:
            P.tag = fn_.__name__
            fn_(*args_)
        P.tag = "wo"

        for og in range(2):
            W, wk = wload(l, "wo%d" % og)
            for jj in range(4):
                j = og * 4 + jj
                bk, bkey = mbank()
                for kc in range(KC):
                    mm(bk[:, :], W[:, kc, jj * 128:(jj + 1) * 128], mixT[:, kc, :], kc == 0, kc == KC - 1, [wk, "mix%d" % kc], [bkey])
                add("dve", lambda e, j=j, bk=bk: e.tensor_tensor(out=xT[:, j, :], in0=xT[:, j, :], in1=bk[:, :], op=ALU.add),
                    reads=[bkey, "xT%d" % j], writes=["xT%d" % j])
        P.tag = "norm2"
        norm(vb + 8)
        P.tag = "ffn_gu"
        for g, ncol in enumerate(GU_SIZES):
            Wg, wkg = wload(l, "gate%d" % g)
            Wu, wku = wload(l, "up%d" % g)
            for jj in range(ncol // 128):
                f = g * 4 + jj
                bg, bgkey = mbank()
                for kc in range(KC):
                    mm(bg[:, :], Wg[:, kc, jj * 128:(jj + 1) * 128], hT[:, kc, :], kc == 0, kc == KC - 1, [wkg, "hT%d" % kc], [bgkey])
                bu, bukey = mbank()
                for kc in range(KC):
                    mm(bu[:, :], Wu[:, kc, jj * 128:(jj + 1) * 128], hT[:, kc, :], kc == 0, kc == KC - 1, [wku, "hT%d" % kc], [bukey])
                rot["sg"] = (rot["sg"] + 1) % 2
                sgt = sgate[rot["sg"]]
                sgkey = "Et%d" % rot["sg"]
                add("act", lambda e, sgt=sgt, bg=bg: e.activation(out=sgt[:], in_=bg[:, :], func=AF.Silu), reads=[bgkey], writes=[sgkey])
                add("dve", lambda e, sgt=sgt, bu=bu, f=f: e.tensor_tensor(out=actT[:, f, :], in0=sgt[:], in1=bu[:, :], op=ALU.mult),
                    reads=[sgkey, bukey], writes=["act%d" % f] + (stg_keys if f < 6 else []))
        P.tag = "ffn_dn"
        for j in range(KC):
            W, wk = wload(l, "dn%d" % j)
            bk, bkey = mbank()
            for fc in range(NFC):
                mm(bk[:, :], W[:, fc, :], actT[:, fc, :], fc == 0, fc == NFC - 1, [wk, "act%d" % fc], [bkey])
            add("dve", lambda e, j=j, bk=bk: e.tensor_tensor(out=xT[:, j, :], in0=xT[:, j, :], in1=bk[:, :], op=ALU.add),
                reads=[bkey, "xT%d" % j], writes=["xT%d" % j])
        P.tag = "out"
        if last_layer:
            norm(NVL * depth, final=True)
            add(ST_ENG, lambda e: e.dma_start(out=outT_d.rearrange("(c p) t -> p c t", p=128)[:, :, c0:c0 + TT], in_=yout),
                reads=["act%d" % c for c in range(2 * KC)], writes=["outd"], dma_slot="yout")
        else:
            add(ST_ENG, lambda e: e.dma_start(out=x1_d.rearrange("(c p) t -> p c t", p=128)[:, :, c0:c0 + TT], in_=xT[:]),
                reads=["xT%d" % c for c in range(KC)], writes=["x1d%d" % i], dma_slot="x1st")

    try:
        if stop == "conv":
            raise _Stop()
        for l in range(depth):
            for i in range(NT):
                tile(l, i)
    except _Stop:
        pass

    P.emit()
    P.close()
    return nc


def _regroup(w, c0, c1):
    K = w.shape[0]
    kc = K // 128
    return np.ascontiguousarray(w[:, c0:c1].reshape(kc, 128, c1 - c0).transpose(1, 0, 2)).reshape(128, kc * (c1 - c0))


def host_weights(w_in, w_o, w_gu, w_down):
    depth = w_in.shape[0]
    out = np.empty((depth, 128, WTOT), np.float32)
    for l in range(depth):
        wi = w_in[l]
        parts = [
            _regroup(wi, 0, 256),
            _regroup(wi, 256, 768),
            _regroup(wi, 768, 1280),
            _regroup(np.concatenate([wi[:, 1280:1296], wi[:, 2832:2840]], axis=1), 0, 24),
            _regroup(wi, 1296, 1808),
            _regroup(wi, 1808, 2320),
            _regroup(wi, 2320, 2832),
        ]
        for og in range(2):
            parts.append(_regroup(w_o[l], og * 512, (og + 1) * 512))
        o = 0
        for n in GU_SIZES:
            parts.append(_regroup(w_gu[l], o, o + n))
            o += n
        o = 0
        for n in GU_SIZES:
            parts.append(_regroup(w_gu[l], DFF + o, DFF + o + n))
            o += n
        for j in range(8):
            parts.append(_regroup(w_down[l], j * 128, (j + 1) * 128))
        out[l] = np.concatenate(parts, axis=1)
    return out


def host_consts():
    c = np.zeros((128, NCONST), np.float32)
    k = np.arange(128)[:, None]
    q = np.arange(512)[None, :]
    c[:, C_M0:C_M0 + 512] = np.where(q >= k, 0.0, NEG)
    s = np.arange(128)[:, None]
    cc = np.arange(128)[None, :]
    c[:, C_MASK2:C_MASK2 + 128] = ((s // 64 == cc // 64) & (s <= cc)).astype(np.float32)
    seg = np.ones(512, np.float32)
    seg[::64] = 0.0
    c[:, C_SEG:C_SEG + 512] = seg[None, :]
    c[:, C_IDENT:C_IDENT + 128] = np.eye(128, dtype=np.float32)
    c[:, C_TRIU:C_TRIU + 128] = (s <= cc).astype(np.float32)
    c[127, C_SEL127:C_SEL127 + 128] = 1.0
    c[:, C_BONES:C_BONES + 128] = (s // 64 == cc // 64).astype(np.float32)
    wins = [[2, 4], [8, 16]]
    for j in range(2):
        for hh in range(2):
            w = wins[j][hh]
            c[hh * 64:(hh + 1) * 64, C_INVW + j] = 1.0 / w
            t = np.arange(16)
            c[hh * 64:(hh + 1) * 64, C_INVCNT + 16 * j:C_INVCNT + 16 * j + 16] = (1.0 / np.minimum(t + 1, w))[None, :]
    return c


def host_vecs(ln1, ln2, pool_scale, b_a, gla_gn, ln_f):
    depth = ln1.shape[0]
    v = np.zeros((128, NVL * depth + 8), np.float32)
    for l in range(depth):
        b = NVL * l
        v[:, b:b + 8] = ln1[l].reshape(8, 128).T
        v[:, b + 8:b + 16] = ln2[l].reshape(8, 128).T
        v[:, b + 16:b + 18] = pool_scale[l].reshape(2, 128).T
        v[:, b + 18:b + 20] = b_a[l].reshape(2, 128).T
        v[:, b + 20:b + 22] = gla_gn[l].reshape(2, 128).T
    v[:, NVL * depth:] = ln_f.reshape(8, 128).T
    return v


_PROG_CACHE = {}


def kernel(x, ln1, w_in, w_pool, pool_scale, w_a_up, b_a, gla_gn, b_f, w_o, ln2, w_gu, w_down, ln_f):
    x = np.asarray(x, np.float32)
    B, T, _ = x.shape
    depth = np.asarray(w_in).shape[0]
    n = 8
    key = (T, depth)
    if key not in _PROG_CACHE:
        import os
        _PROG_CACHE[key] = build_program(T, depth, stop=os.environ.get("KSTOP"))
    nc = _PROG_CACHE[key]
    wall = host_weights(np.asarray(w_in, np.float32), np.asarray(w_o, np.float32), np.asarray(w_gu, np.float32),
                        np.asarray(w_down, np.float32))
    vecs = host_vecs(np.asarray(ln1, np.float32), np.asarray(ln2, np.float32), np.asarray(pool_scale, np.float32),
                     np.asarray(b_a, np.float32), np.asarray(gla_gn, np.float32), np.asarray(ln_f, np.float32))
    consts = host_consts()
    bfb = np.ascontiguousarray(np.broadcast_to(np.tile(np.asarray(b_f, np.float32), (1, 4))[:, None, :], (depth, 128, 32)))
    in_maps = []
    for c in range(n):
        b = c % B
        wall_c = np.concatenate([wall, np.full((depth, 128, 4), float(c), np.float32)], axis=2)
        in_maps.append({
            "xT": np.ascontiguousarray(x[b].T),
            "wall": wall_c,
            "vecs": vecs,
            "wpool": np.asarray(w_pool, np.float32),
            "waup": np.asarray(w_a_up, np.float32),
            "bfb": bfb,
            "consts": consts,
        })
    res = run_bass_kernel_spmd(nc, in_maps, core_ids=list(range(n)))
    out = np.empty((B, T, D), np.float32)
    for b in range(B):
        out[b] = np.asarray(res.results[b]["outT"], np.float32).T
    return out
```

```python
import contextlib
import numpy as np
import concourse.bass as bass
import concourse.mybir as mybir
from concourse.bass_utils import run_bass_kernel_spmd

F32 = mybir.dt.float32
BF16 = mybir.dt.bfloat16
AF = mybir.ActivationFunctionType
ALU = mybir.AluOpType

ENGS = ("pe", "act", "dve", "pool", "sp")

D = 1024
KC = 8
DEPTH = 2
TT = 512
NBLK = 4
DFF = 2816
NFC = 22
EPS = 1e-6
NEG = -30000.0
ST_ENG = "sp"
LN8 = float(np.log(8.0))

G_IN = [("pool", 256), ("g1", 512), ("g2", 512), ("gs", 24), ("fq", 512), ("fk", 512), ("fv", 512)]
GU_SIZES = [512, 512, 512, 512, 512, 256]


def _offsets():
    off = {}
    o = 0
    for nm, n in G_IN:
        off[nm] = (o, 8 * n, n)
        o += 8 * n
    for og in range(2):
        off["wo%d" % og] = (o, 8 * 512, 512)
        o += 8 * 512
    for g, n in enumerate(GU_SIZES):
        off["gate%d" % g] = (o, 8 * n, n)
        o += 8 * n
    for g, n in enumerate(GU_SIZES):
        off["up%d" % g] = (o, 8 * n, n)
        o += 8 * n
    for j in range(8):
        off["dn%d" % j] = (o, NFC * 128, 128)
        o += NFC * 128
    return off, o


WOFF, WTOT = _offsets()
PIECE = 2052
assert WTOT % PIECE == 0

NVL = 22
C_M0 = 0
C_MASK2 = 512
C_SEG = 640
C_IDENT = 1152
C_TRIU = 1280
C_SEL127 = 1408
C_BONES = 1536
C_INVW = 1664
C_INVCNT = 1666
NCONST = 1698


class Op:
    __slots__ = ("eng", "fn", "deps", "signal", "semval", "dma_sem", "dma_val", "is_dma", "tag")

    def __init__(self, eng, fn, is_dma):
        self.eng = eng
        self.fn = fn
        self.deps = []
        self.signal = False
        self.semval = None
        self.is_dma = is_dma
        self.dma_sem = None
        self.dma_val = None


class Prog:
    def __init__(self, nc):
        self.nc = nc
        self.ops = {e: [] for e in ENGS}
        self.last_writer = {}
        self.readers = {}
        self.dma_slots = {}
        self.n_dma_sems = 0
        self.stack = contextlib.ExitStack()
        self.rot = {}
        self.excl = set()
        self.tag = "setup"

    def sb(self, name, shape, dtype):
        return self.stack.enter_context(self.nc.sbuf_tensor(name, list(shape), dtype))

    def ps(self, name, shape, dtype=F32):
        return self.stack.enter_context(self.nc.psum_tensor(name, list(shape), dtype))

    def add(self, eng, fn, reads=(), writes=(), dma_slot=None):
        op = Op(eng, fn, dma_slot is not None)
        op.tag = self.tag
        deps = {}
        ex = [r for r in reads if r in self.excl]
        if ex:
            reads = [r for r in reads if r not in self.excl]
            writes = list(writes) + ex
        for r in reads:
            w = self.last_writer.get(r)
            if w is not None:
                deps[id(w)] = w
        for k in writes:
            w = self.last_writer.get(k)
            if w is not None:
                deps[id(w)] = w
            for rd in self.readers.get(k, ()):
                deps[id(rd)] = rd
        op.deps = list(deps.values())
        for r in reads:
            rl = self.readers.setdefault(r, [])
            if not op.is_dma:
                for idx in range(len(rl)):
                    if (not rl[idx].is_dma) and rl[idx].eng == eng:
                        rl.pop(idx)
                        break
            rl.append(op)
        for k in writes:
            self.last_writer[k] = op
            self.readers[k] = []
        if dma_slot is not None:
            if dma_slot not in self.dma_slots:
                self.dma_slots[dma_slot] = [self.n_dma_sems, 0]
                self.n_dma_sems += 1
            s = self.dma_slots[dma_slot]
            s[1] += 1
            op.dma_sem = s[0]
            op.dma_val = 16 * s[1]
        self.ops[eng].append(op)
        return op

    def emit(self):
        nc = self.nc
        for e in ENGS:
            for op in self.ops[e]:
                for d in op.deps:
                    if d.is_dma:
                        continue
                    if d.eng == "pe" and op.eng == "pe":
                        continue
                    d.signal = True
        for e in ENGS:
            c = 0
            for op in self.ops[e]:
                if op.signal and not op.is_dma:
                    c += 1
                    op.semval = c
        st = self.stack
        esem = {e: st.enter_context(nc.semaphore("s_" + e)) for e in ENGS if e != "sp"}
        dsem = [st.enter_context(nc.semaphore("d%d" % i)) for i in range(self.n_dma_sems)]
        block = st.enter_context(nc.Block())

        def run(ename, eng):
            waited = {}
            for op in self.ops[ename]:
                need = {}
                for d in op.deps:
                    if d.is_dma:
                        key = ("d", d.dma_sem)
                        val = d.dma_val
                    else:
                        if d.eng == "pe" and ename == "pe":
                            continue
                        key = ("e", d.eng)
                        val = d.semval
                    if val > need.get(key, 0):
                        need[key] = val
                for key, val in need.items():
                    if waited.get(key, 0) >= val:
                        continue
                    waited[key] = val
                    sem = dsem[key[1]] if key[0] == "d" else esem[key[1]]
                    eng.wait_ge(sem, val)
                ins = op.fn(eng)
                if op.is_dma:
                    ins.then_inc(dsem[op.dma_sem], 16)
                elif op.signal:
                    ins.then_inc(esem[ename], 1)
            fin = {}
            for op in self.ops[ename]:
                if op.is_dma:
                    fin[op.dma_sem] = max(fin.get(op.dma_sem, 0), op.dma_val)
            for s, v in fin.items():
                if waited.get(("d", s), 0) < v:
                    eng.wait_ge(dsem[s], v)

        @block.tensor
        def _(pe):
            run("pe", pe)

        @block.scalar
        def _(act):
            run("act", act)

        @block.vector
        def _(dve):
            run("dve", dve)

        @block.gpsimd
        def _(pool):
            run("pool", pool)

        @block.sync
        def _(sp):
            run("sp", sp)

    def close(self):
        self.stack.close()


class _Stop(Exception):
    pass


def build_program(T, depth=DEPTH, own_from=0, stop=None, split=True):
    NT = T // TT
    NBT = T // 128
    NP = NT // 2 if split else 0
    TO = T - NP * TT
    nc = bass.Bass("TRN2", target_bir_lowering=False)
    P = Prog(nc)
    add = P.add

    xT_d = nc.dram_tensor("xT", [D, T], F32, kind="ExternalInput").ap()
    wall_d = nc.dram_tensor("wall", [depth, 128, WTOT + 4], F32, kind="ExternalInput").ap()
    vecs_d = nc.dram_tensor("vecs", [128, NVL * depth + 8], F32, kind="ExternalInput").ap()
    wpool_d = nc.dram_tensor("wpool", [depth, 4, 64, 64], F32, kind="ExternalInput").ap()
    waup_d = nc.dram_tensor("waup", [depth, 16, 256], F32, kind="ExternalInput").ap()
    bf_d = nc.dram_tensor("bfb", [depth, 128, 32], F32, kind="ExternalInput").ap()
    const_d = nc.dram_tensor("consts", [128, NCONST], F32, kind="ExternalInput").ap()
    outT_d = nc.dram_tensor("outT", [D, TO], F32, kind="ExternalOutput").ap()
    pc_d = nc.dram_tensor("percore", [128, NBT + 64], F32, kind="ExternalInput").ap()

    wbf_d = nc.dram_tensor("wbf", [depth, 128, WTOT], BF16).ap()
    kc_d = nc.dram_tensor("kcache", [4, 128, T], BF16).ap()
    vc_d = nc.dram_tensor("vcache", [4, 128, NBT, 130], BF16).ap()
    x1_d = nc.dram_tensor("x1T", [D, T], F32).ap()

    xT = P.sb("xTs", [128, KC, TT], F32)
    hT = P.sb("hT", [128, KC, TT], BF16)
    rstd = P.sb("rstd", [128, TT], F32)
    sq = [P.sb("sq%d" % i, [128, TT], BF16) for i in range(2)]
    wbuf = [P.sb("wbuf%d" % i, [128, 4096], BF16) for i in range(3)]
    mixT = P.sb("mixT", [128, KC, TT], BF16)
    actT = P.sb("actT", [128, NFC, TT], BF16)
    qT = P.sb("qT", [128, 4, TT], BF16)
    kT = P.sb("kT", [128, 4, TT], BF16)
    vtm = P.sb("vtm", [128, 4, NBLK, 130], BF16)
    kseg = [P.sb("kseg%d" % i, [128, 2048], BF16) for i in range(2)]
    vseg = [P.sb("vseg%d" % i, [128, 16, 130], BF16) for i in range(2)]
    PT = [P.sb("PT%d" % i, [128, TT], BF16) for i in range(4)]
    osb = [P.sb("osb%d" % i, [128, TT], F32) for i in range(2)]
    negF = P.sb("negF", [128, NBT * 8], F32)
    fbias = [P.sb("fbias%d" % i, [128, NBT], F32) for i in range(2)]
    nref = P.sb("nref", [128, 8], F32)
    fft = P.sb("fft", [128, 32], F32)
    spf = P.sb("spf", [128, 32], F32)
    carry = P.sb("carry", [1, 8], F32)
    bfb = P.sb("bfbs", [128, depth, 32], F32)
    gqT = P.sb("gqT", [128, 2, TT], F32)
    gkT = P.sb("gkT", [128, 2, TT], F32)
    sg = P.sb("sg", [128, 2, TT], BF16)
    gvraw = P.sb("gvraw", [128, NBLK, 256], BF16)
    gvpad = P.sb("gvpad", [128, NBLK, 2, 2, 128], BF16)
    gaT = P.sb("gaT", [16, TT], BF16)
    bp = [P.sb("bp%d" % i, [128, TT], F32) for i in range(2)]
    d1 = P.sb("d1", [128, TT], F32)
    d3 = P.sb("d3", [128, TT], F32)
    Et = [P.sb("Et%d" % i, [128, TT], F32) for i in range(2)]
    esb, spg = Et[0], Et[1]
    qin = [P.sb("qin%d" % i, [128, TT], BF16) for i in range(2)]
    kin = [P.sb("kin%d" % i, [128, 2, TT], BF16) for i in range(2)]
    kkv = [P.sb("kkv%d" % i, [128, TT], F32) for i in range(2)]
    qo = [P.sb("qo%d" % i, [128, TT], BF16) for i in range(2)]
    dec = [P.sb("dec%d" % i, [128, 8], F32) for i in range(2)]
    kktm = [P.sb("kktm%d" % i, [128, 2, NBLK, 128], BF16) for i in range(2)]
    attm = P.sb("attm", [128, 2, 128], BF16)
    S32 = P.sb("S32", [128, 2, 128], F32)
    Sbf = P.sb("Sbf", [128, 2, 2, 8, 128], BF16)
    osq = P.sb("osq", [128, TT], BF16)
    t1 = P.sb("t1", [128, TT], F32)
    rr = P.sb("rr", [128, TT], F32)
    sgate = Et
    wa_f = P.sb("wa_f", [16, depth, 256], F32)
    wa_b = P.sb("wa_b", [16, depth, 256], BF16)
    U = P.sb("U", [128, 2, 16 + TT], F32)
    s2 = P.sb("s2", [128, 16 + TT], F32)
    s4 = P.sb("s4", [128, 16 + TT], F32)
    s8 = s2
    s16 = s4
    dT = P.sb("dT", [128, 2, TT], BF16)
    wp_f = P.sb("wp_f", [128, depth, 2, 128], F32)
    wp_b = P.sb("wp_b", [128, depth, 2, 128], BF16)
    cst = P.sb("cst", [128, NCONST], F32)
    pcs = P.sb("pcs", [128, NBT + 64], F32)
    vec = P.sb("vec", [128, NVL * depth + 8], F32)
    negba = P.sb("negba", [128, depth, 2], F32)
    m0b = P.sb("m0b", [128, TT], BF16)
    mask2b = P.sb("mask2b", [128, 128], BF16)
    identb = P.sb("identb", [128, 128], BF16)
    onesb = P.sb("onesb", [128, 128], BF16)
    bonesb = P.sb("bonesb", [128, 128], BF16)
    onesf = P.sb("onesf", [128, 128], F32)

    banks = {}
    for nm in ["m0", "m1", "m2", "s0", "s1", "o0", "o1", "aux"]:
        banks[nm] = P.ps("ps_" + nm, [128, 512], F32)
    rot = {"m": 0, "s": 0, "o": 0, "w": 0, "pt": 0, "sq": 0, "sg": 0, "osb": 0, "et": 0, "kv": 0, "fb": 0}

    def mbank():
        rot["m"] = (rot["m"] + 1) % 3
        k = "m%d" % rot["m"]
        return banks[k], k

    def sbank():
        rot["s"] = (rot["s"] + 1) % 2
        k = "s%d" % rot["s"]
        return banks[k], k

    def obank():
        rot["o"] = (rot["o"] + 1) % 2
        k = "o%d" % rot["o"]
        return banks[k], k

    aux = banks["aux"]
    P.excl = set(banks.keys())

    def mm(out, lhsT, rhs, start, stop, reads, writes):
        add("pe", lambda e: e.matmul(out, lhsT=lhsT, rhs=rhs, start=start, stop=stop), reads, writes)

    def cvec(col):
        return vec[:, col:col + 1]

    add("sp", lambda e: e.dma_start(out=cst[:], in_=const_d[:, :]), writes=["cst"], dma_slot="cst")
    add("sp", lambda e: e.dma_start(out=vec[:], in_=vecs_d[:, :]), writes=["vec"], dma_slot="vec")
    add("sp", lambda e: e.dma_start(out=pcs[:], in_=pc_d[:, :]), writes=["pcs"], dma_slot="pcs")
    add("sp", lambda e: e.dma_start(out=bfb[:], in_=bf_d.rearrange("l p c -> p l c")), writes=["bfb"], dma_slot="bfb")
    add("sp", lambda e: e.dma_start(out=wa_f[:], in_=waup_d.rearrange("l r c -> r l c")), writes=["wa_f"], dma_slot="wa_f")
    add("pool", lambda e: e.memset(wp_f[:], 0.0), writes=["wp_f"])
    for l in range(depth):
        for g in range(4):
            j, hh = g // 2, g % 2
            add("sp", lambda e, l=l, g=g, j=j, hh=hh: e.dma_start(
                out=wp_f[hh * 64:(hh + 1) * 64, l, j, hh * 64:(hh + 1) * 64], in_=wpool_d[l, g, :, :]),
                reads=[], writes=["wp_f"], dma_slot="wp%d_%d" % (l, g))
    add("dve", lambda e: e.tensor_copy(out=wp_b[:], in_=wp_f[:]), reads=["wp_f"], writes=["wp_b"])
    add("dve", lambda e: e.tensor_copy(out=wa_b[:], in_=wa_f[:]), reads=["wa_f"], writes=["wa_b"])
    add("dve", lambda e: e.tensor_copy(out=m0b[:], in_=cst[:, C_M0:C_M0 + 512]), reads=["cst"], writes=["m0b"])
    add("dve", lambda e: e.tensor_copy(out=mask2b[:], in_=cst[:, C_MASK2:C_MASK2 + 128]), reads=["cst"], writes=["mask2b"])
    add("dve", lambda e: e.tensor_copy(out=identb[:], in_=cst[:, C_IDENT:C_IDENT + 128]), reads=["cst"], writes=["identb"])
    add("dve", lambda e: e.tensor_copy(out=bonesb[:], in_=cst[:, C_BONES:C_BONES + 128]), reads=["cst"], writes=["bonesb"])
    add("pool", lambda e: e.memset(onesb[:], 1.0), writes=["onesb"])
    add("pool", lambda e: e.memset(onesf[:], 1.0), writes=["onesf"])
    add("pool", lambda e: e.memset(vtm[:], 1.0), writes=["vtm"])
    add("pool", lambda e: e.memset(gvpad[:], 0.0), writes=["gvpad"])
    for pc_ in range(2):
        add("pool", lambda e, pc_=pc_: e.memset(kin[pc_][:], 0.0), writes=["kin_%d" % pc_])
        add("pool", lambda e, pc_=pc_: e.memset(kktm[pc_][:], 0.0), writes=["kktm_%d" % pc_])
    for l in range(depth):
        add("dve", lambda e, l=l: e.tensor_scalar(out=negba[:, l, :], in0=vec[:, NVL * l + 18:NVL * l + 20],
                                                  scalar1=-1.0, scalar2=None, op0=ALU.mult),
            reads=["vec"], writes=["negba"])
    identf = cst[:, C_IDENT:C_IDENT + 128]
    triU = cst[:, C_TRIU:C_TRIU + 128]
    sel127 = cst[:, C_SEL127:C_SEL127 + 128]
    segm = cst[:, C_SEG:C_SEG + 512]

    actT_f = actT[:].rearrange("p a b -> p (a b)").bitcast(F32)
    yout = actT_f[:, 0:KC * TT].rearrange("p (c t) -> p c t", c=KC)
    npiece = WTOT // PIECE
    cv = 0
    for l in range(depth):
        for pi in range(npiece):
            s = cv % 2
            stg = actT_f[:, s * PIECE:(s + 1) * PIECE]
            ob = wbuf[s][:, 0:PIECE]
            add("sp", lambda e, l=l, pi=pi, stg=stg: e.dma_start(out=stg, in_=wall_d[l, :, pi * PIECE:(pi + 1) * PIECE]),
                writes=["stg%d" % s], dma_slot="stg%d" % s)
            eng = ("dve", "act", "pool")[cv % 3]
            if eng == "act":
                add("act", lambda e, stg=stg, ob=ob: e.copy(out=ob, in_=stg), reads=["stg%d" % s], writes=["wbuf%d" % s])
            else:
                add(eng, lambda e, stg=stg, ob=ob: e.tensor_copy(out=ob, in_=stg), reads=["stg%d" % s], writes=["wbuf%d" % s])
            add(ST_ENG, lambda e, l=l, pi=pi, ob=ob: e.dma_start(out=wbf_d[l, :, pi * PIECE:(pi + 1) * PIECE], in_=ob),
                reads=["wbuf%d" % s], writes=["wbf%d_%d" % (l, pi)], dma_slot="cvo%d" % s)
            cv += 1
    stg_keys = ["stg0", "stg1"]

    def wload(l, name):
        off, n, ncol = WOFF[name]
        rot["w"] = (rot["w"] + 1) % 3
        s = rot["w"]
        key = "wbuf%d" % s
        buf = wbuf[s]
        pkeys = ["wbf%d_%d" % (l, pi) for pi in range(off // PIECE, (off + n - 1) // PIECE + 1)]
        add("sp", lambda e: e.dma_start(out=buf[:, 0:n], in_=wbf_d[l, :, off:off + n]),
            reads=pkeys, writes=[key], dma_slot=key)
        if name.startswith("dn"):
            view = buf[:, 0:n].rearrange("p (k c) -> p k c", k=NFC)
        else:
            view = buf[:, 0:n].rearrange("p (k c) -> p k c", k=KC)
        return view, key

    def norm(gcol0, out_is_h=True, final=False):
        for c in range(KC):
            rot["sq"] = (rot["sq"] + 1) % 2
            s = rot["sq"]
            if c % 2 == 0:
                add("act", lambda e, c=c, s=s: e.activation(out=sq[s][:], in_=xT[:, c, :], func=AF.Square),
                    reads=["xT%d" % c], writes=["sq%d" % s])
            else:
                add("dve", lambda e, c=c, s=s: e.tensor_tensor(out=sq[s][:], in0=xT[:, c, :], in1=xT[:, c, :], op=ALU.mult),
                    reads=["xT%d" % c], writes=["sq%d" % s])
            mm(aux[:, :], onesb[:], sq[s][:], c == 0, c == KC - 1, ["onesb", "sq%d" % s], ["aux"])
        add("act", lambda e: e.activation(out=rstd[:], in_=aux[:, :], func=AF.Ln, bias=EPS, scale=1.0 / D),
            reads=["aux"], writes=["rstd"])
        add("act", lambda e: e.activation(out=rstd[:], in_=rstd[:], func=AF.Exp, scale=-0.5),
            reads=["rstd"], writes=["rstd"])
        for c in range(KC):
            if final:
                add("dve", lambda e, c=c: e.scalar_tensor_tensor(out=yout[:, c, :], in0=xT[:, c, :], scalar=cvec(gcol0 + c),
                                                                 in1=rstd[:], op0=ALU.mult, op1=ALU.mult),
                    reads=["xT%d" % c, "rstd", "vec"], writes=["act%d" % (2 * c), "act%d" % (2 * c + 1)])
            else:
                add("dve", lambda e, c=c: e.scalar_tensor_tensor(out=hT[:, c, :], in0=xT[:, c, :], scalar=cvec(gcol0 + c),
                                                                 in1=rstd[:], op0=ALU.mult, op1=ALU.mult),
                    reads=["xT%d" % c, "rstd", "vec"], writes=["hT%d" % c])

    hkeys = ["hT%d" % c for c in range(KC)]

    def proj_fm(W, wkey, j, out_ap_fn, ncols=TT):
        bk, bkey = mbank()
        for kc in range(KC):
            mm(bk[:, 0:ncols], W[:, kc, j * 128:(j + 1) * 128], hT[:, kc, :], kc == 0, kc == KC - 1,
               [wkey, "hT%d" % kc], [bkey])
        return bk, bkey

    def tile(l, i):
        vb = NVL * l
        par = i % 2
        first_layer = (l == 0)
        last_layer = (l == depth - 1)
        src = xT_d if first_layer else x1_d
        c0 = i * TT
        L_ = 16 + TT
        lite = last_layer and i < NP
        add("sp", lambda e: e.dma_start(out=xT[:], in_=src.rearrange("(c p) t -> p c t", p=128)[:, :, c0:c0 + TT]),
            reads=["x1d%d" % i] if not first_layer else [], writes=["xT%d" % c for c in range(KC)], dma_slot="xT")
        P.tag = "norm1"
        norm(vb + 0)

        def ip_pool():
            W, wk = wload(l, "pool")
            for j in range(2):
                bk, bkey = proj_fm(W, wk, j, None)
                add("act", lambda e, j=j, bk=bk: e.copy(out=U[:, j, 16:16 + TT], in_=bk[:, :]), reads=[bkey], writes=["U%d" % j])

        def ip_g1():
            W, wk = wload(l, "g1")
            for j in ((2, 3) if lite else range(4)):
                bk, bkey = proj_fm(W, wk, j, None)
                dst = gqT if j < 2 else gkT
                dkey = ("gq%d" if j < 2 else "gk%d") % (j % 2)
                add("dve", lambda e, j=j, bk=bk, dst=dst: e.tensor_copy(out=dst[:, j % 2, :], in_=bk[:, :]), reads=[bkey], writes=[dkey])

        def ip_g2():
            W, wk = wload(l, "g2")
            for blk in range(NBLK):
                bk, bkey = mbank()
                for kc in range(KC):
                    mm(bk[:, 0:256], hT[:, kc, blk * 128:(blk + 1) * 128], W[:, kc, 0:256], kc == 0, kc == KC - 1,
                       [wk, "hT%d" % kc], [bkey])
                add("dve", lambda e, blk=blk, bk=bk: e.tensor_copy(out=gvraw[:, blk, :], in_=bk[:, 0:256]), reads=[bkey], writes=["gvraw"])
                for hh in (() if lite else range(2)):
                    add("act", lambda e, blk=blk, bk=bk, hh=hh: e.copy(
                        out=gvpad[:, blk, :, hh, hh * 64:(hh + 1) * 64],
                        in_=bk[:, 0:256].rearrange("p (a h d) -> p a h d", a=2, h=2)[:, :, hh, :]),
                        reads=[bkey], writes=["gvpad"])
            for j in (() if lite else range(2)):
                bk, bkey = proj_fm(W, wk, 2 + j, None)
                add("act", lambda e, j=j, bk=bk: e.activation(out=sg[:, j, :], in_=bk[:, :], func=AF.Silu), reads=[bkey], writes=["sg%d" % j])

        def ip_gs():
            W, wk = wload(l, "gs")
            bk, bkey = mbank()
            for kc in range(KC):
                mm(bk[0:16, :], W[:, kc, 0:16], hT[:, kc, :], kc == 0, kc == KC - 1, [wk, "hT%d" % kc], [bkey])
            add("dve", lambda e, bk=bk: e.tensor_copy(out=gaT[:], in_=bk[0:16, :]), reads=[bkey], writes=["gaT"])
            bk, bkey = mbank()
            for blk in range(NBLK):
                for kc in range(KC):
                    mm(bk[:, blk * 8:(blk + 1) * 8], hT[:, kc, blk * 128:(blk + 1) * 128], W[:, kc, 16:24], kc == 0, kc == KC - 1,
                       [wk, "hT%d" % kc], [bkey])
            add("dve", lambda e, bk=bk: e.tensor_tensor(out=fft[:], in0=bk[:, 0:32], in1=bfb[:, l, :], op=ALU.add),
                reads=[bkey, "bfb"], writes=["fft"])

        def ip_fq():
            W, wk = wload(l, "fq")
            for j in range(4):
                bk, bkey = proj_fm(W, wk, j, None)
                add("act", lambda e, j=j, bk=bk: e.mul(out=qT[:, j, :], in_=bk[:, :], mul=0.125), reads=[bkey], writes=["qT%d" % j])

        def ip_fk():
            W, wk = wload(l, "fk")
            for j in range(4):
                bk, bkey = proj_fm(W, wk, j, None)
                add("dve", lambda e, j=j, bk=bk: e.tensor_copy(out=kT[:, j, :], in_=bk[:, :]), reads=[bkey], writes=["kT%d" % j])
                add(ST_ENG, lambda e, j=j: e.dma_start(out=kc_d[j, :, c0:c0 + TT], in_=kT[:, j, :]),
                    reads=["kT%d" % j], writes=["kc%d_%d" % (j, i)], dma_slot="kst%d" % j)

        def ip_fv():
            W, wk = wload(l, "fv")
            for blk in range(NBLK):
                bk, bkey = mbank()
                for kc in range(KC):
                    mm(bk[:, :], hT[:, kc, blk * 128:(blk + 1) * 128], W[:, kc, :], kc == 0, kc == KC - 1, [wk, "hT%d" % kc], [bkey])
                add("dve", lambda e, blk=blk, bk=bk: e.tensor_copy(
                    out=vtm[:, :, blk, :].rearrange("p a (h d) -> p a h d", h=2)[:, :, :, 0:64],
                    in_=bk[:, :].rearrange("p (a h d) -> p a h d", a=4, h=2)), reads=[bkey], writes=["vtm"])
            for pr in range(4):
                add(ST_ENG, lambda e, pr=pr: e.dma_start(out=vc_d[pr, :, i * NBLK:(i + 1) * NBLK, :], in_=vtm[:, pr, :, :]),
                    reads=["vtm"], writes=["vc%d_%d" % (pr, i)], dma_slot="vst%d" % pr)

        def f_chain():
            add("act", lambda e: e.activation(out=spf[:], in_=fft[:], func=AF.Exp, scale=-1.0), reads=["fft"], writes=["spf"])
            add("act", lambda e: e.activation(out=spf[:], in_=spf[:], func=AF.Ln, bias=1.0, scale=1.0), reads=["spf"], writes=["spf"])
            if i == 0:
                add("dve", lambda e: e.memset(carry[:], 0.0), writes=["carry"])
            for blk in range(NBLK):
                o = aux[:, blk * 8:(blk + 1) * 8]
                mm(o, triU, spf[:, blk * 8:(blk + 1) * 8], True, False, ["cst", "spf"], ["aux"])
                for b2 in range(blk):
                    mm(o, onesf[:], spf[:, b2 * 8:(b2 + 1) * 8], False, False, ["onesf", "spf"], ["aux"])
                mm(o, onesf[0:1, :], carry[0:1, :], False, True, ["onesf", "carry"], ["aux"])
            add("dve", lambda e: e.tensor_copy(out=negF[:, i * 32:(i + 1) * 32], in_=aux[:, 0:32]), reads=["aux"], writes=["negF"])
            mm(aux[:, 64:72], sel127, negF[:, i * 32 + 8:i * 32 + 16], True, True, ["cst", "negF"], ["aux"])
            mm(aux[0:1, 80:88], sel127[:, 0:1], negF[:, i * 32 + 24:i * 32 + 32], True, True, ["cst", "negF"], ["aux"])
            add("dve", lambda e: e.tensor_copy(out=nref[:], in_=aux[:, 64:72]), reads=["aux"], writes=["nref"])
            add("dve", lambda e: e.tensor_copy(out=carry[:], in_=aux[0:1, 80:88]), reads=["aux"], writes=["carry"])

        def pool_elem(j):
            if i == 0:
                add("pool", lambda e, j=j: e.memset(U[:, j, 0:16], 0.0), writes=["Uh%d" % j])
            Uj = U[:, j, :]
            add("pool", lambda e, Uj=Uj: e.tensor_tensor(out=s2[:, 1:L_], in0=Uj[:, 1:L_], in1=Uj[:, 0:L_ - 1], op=ALU.add),
                reads=["U%d" % j, "Uh%d" % j], writes=["s2"])
            add("pool", lambda e: e.tensor_tensor(out=s4[:, 3:L_], in0=s2[:, 3:L_], in1=s2[:, 1:L_ - 2], op=ALU.add),
                reads=["s2"], writes=["s4"])
            if j == 1:
                add("pool", lambda e: e.tensor_tensor(out=s8[:, 7:L_], in0=s4[:, 7:L_], in1=s4[:, 3:L_ - 4], op=ALU.add),
                    reads=["s4"], writes=["s2"])
                add("pool", lambda e: e.tensor_tensor(out=s16[:, 15:L_], in0=s8[:, 15:L_], in1=s8[:, 7:L_ - 8], op=ALU.add),
                    reads=["s2"], writes=["s4"])
                wins = [(s8, "s2", 0.125), (s16, "s4", 0.0625)]
            else:
                wins = [(s2, "s2", 0.5), (s4, "s4", 0.25)]
            for hh in range(2):
                wt, wkey_, invw = wins[hh]
                r0, r1 = hh * 64, (hh + 1) * 64
                add("dve", lambda e, j=j, wt=wt, invw=invw, r0=r0, r1=r1, Uj=Uj: e.scalar_tensor_tensor(
                    out=dT[r0:r1, j, :], in0=wt[r0:r1, 16:L_], scalar=invw, in1=Uj[r0:r1, 16:L_],
                    op0=ALU.mult, op1=ALU.subtract),
                    reads=[wkey_, "U%d" % j], writes=["dT%d" % j])
                if i == 0 or (split and i == NP):
                    icol = NBT + (0 if i == 0 else 32) + 16 * j
                    add("dve", lambda e, j=j, wt=wt, r0=r0, r1=r1, icol=icol: e.tensor_tensor(
                        out=t1[r0:r1, 0:16], in0=wt[r0:r1, 16:32], in1=pcs[r0:r1, icol:icol + 16],
                        op=ALU.mult), reads=[wkey_, "pcs"], writes=["t1"])
                    add("dve", lambda e, j=j, r0=r0, r1=r1, Uj=Uj: e.tensor_tensor(
                        out=dT[r0:r1, j, 0:16], in0=t1[r0:r1, 0:16], in1=Uj[r0:r1, 16:32], op=ALU.subtract),
                        reads=["t1", "U%d" % j, "dT%d" % j], writes=["dT%d" % j])
            add("pool", lambda e, j=j: e.tensor_copy(out=U[:, j, 0:16], in_=U[:, j, TT:TT + 16]),
                reads=["U%d" % j, "s2"], writes=["Uh%d" % j])

        def pool_mm(j):
            bk, bkey = mbank()
            mm(bk[:, :], wp_b[:, l, j, :], dT[:, j, :], True, True, ["wp_b", "dT%d" % j], [bkey])
            add("dve", lambda e, j=j, bk=bk: e.tensor_scalar(out=mixT[:, j, :], in0=bk[:, :], scalar1=cvec(vb + 16 + j), scalar2=None,
                                                              op0=ALU.mult), reads=[bkey, "vec"], writes=["mix%d" % j])

        def gla_a(pc):
            bpc, qinc, kinc, kkvc, qoc, decc = bp[pc], qin[pc], kin[pc], kkv[pc], qo[pc], dec[pc]
            sfx = "_%d" % pc
            bk, bkey = mbank()
            mm(bk[:, :], wa_b[0:16, l, pc * 128:(pc + 1) * 128], gaT[0:16, :], True, True, ["wa_b", "gaT"], [bkey])
            add("act", lambda e, bk=bk: e.activation(out=esb[:], in_=bk[:, :], func=AF.Exp, bias=negba[:, l, pc:pc + 1], scale=-1.0),
                reads=[bkey, "negba"], writes=["Et0"])
            add("act", lambda e: e.activation(out=spg[:], in_=esb[:], func=AF.Ln, bias=1.0, scale=1.0), reads=["Et0"], writes=["Et1"])
            add("dve", lambda e: e.tensor_tensor_scan(out=bpc[:], data0=segm, data1=spg[:], initial=0.0, op0=ALU.mult, op1=ALU.add),
                reads=["cst", "Et1"], writes=["bp" + sfx])
            bpv = bpc[:].rearrange("p (n c) -> p n c", c=64)
            add("dve", lambda e: e.tensor_tensor(out=d1[:].rearrange("p (n c) -> p n c", c=64), in0=bpv,
                                                 in1=bpv[:, :, 31:32].to_broadcast([128, 8, 64]), op=ALU.subtract),
                reads=["bp" + sfx], writes=["d1"])
            add("dve", lambda e: e.tensor_tensor(out=d3[:].rearrange("p (n c) -> p n c", c=64), in0=bpv,
                                                 in1=bpv[:, :, 63:64].to_broadcast([128, 8, 64]), op=ALU.subtract),
                reads=["bp" + sfx], writes=["d3"])
            specs = [(d1, "d1", -1.0 / 16, -LN8, gqT, "gq%d" % pc, qinc, "qin" + sfx),
                     (d1, "d1", 1.0 / 16, 0.0, gkT, "gk%d" % pc, None, "kin" + sfx),
                     (d3, "d3", 1.0 / 16, 0.0, gkT, "gk%d" % pc, kkvc, "kkv" + sfx),
                     (bpc, "bp" + sfx, -1.0 / 16, -LN8, gqT, "gq%d" % pc, qoc, "qo" + sfx)]
            if lite:
                specs = specs[2:3]
            for (srcT, skey, scl, bia, opT, okey, dst, dkey) in specs:
                rot["et"] = (rot["et"] + 1) % 2
                et = Et[rot["et"]]
                ekey = "Et%d" % rot["et"]
                add("act", lambda e, et=et, srcT=srcT, scl=scl, bia=bia: e.activation(out=et[:], in_=srcT[:], func=AF.Exp, bias=bia, scale=scl),
                    reads=[skey], writes=[ekey])
                if dst is None:
                    for hh in range(2):
                        r0, r1 = hh * 64, (hh + 1) * 64
                        add("dve", lambda e, et=et, opT=opT, hh=hh, r0=r0, r1=r1: e.tensor_tensor(
                            out=kinc[r0:r1, hh, :], in0=opT[r0:r1, pc, :], in1=et[r0:r1, :], op=ALU.mult),
                            reads=[ekey, okey], writes=[dkey])
                else:
                    add("dve", lambda e, et=et, opT=opT, dst=dst: e.tensor_tensor(out=dst[:], in0=opT[:, pc, :], in1=et[:], op=ALU.mult),
                        reads=[ekey, okey], writes=[dkey])
            add("act", lambda e: e.activation(out=decc[:], in_=bpv[:, :, 63], func=AF.Exp, scale=-1.0 / 16),
                reads=["bp" + sfx], writes=["dec" + sfx])

        def gla_b(pc):
            kkvc, decc, kktmc = kkv[pc], dec[pc], kktm[pc]
            sfx = "_%d" % pc
            bk, bkey = mbank()
            for blk in range(NBLK):
                add("pe", lambda e, bk=bk, blk=blk: e.transpose(bk[:, blk * 128:(blk + 1) * 128], kkvc[:, blk * 128:(blk + 1) * 128], identf),
                    reads=["kkv" + sfx, "cst"], writes=[bkey])
            for ch in range(2):
                t0, t1_ = ch * 64, (ch + 1) * 64
                add("dve", lambda e, bk=bk, ch=ch, t0=t0, t1_=t1_: e.tensor_copy(
                    out=kktmc[t0:t1_, ch, :, :], in_=bk[t0:t1_, :].rearrange("p (b c) -> p b c", b=NBLK)), reads=[bkey], writes=["kktm" + sfx])
            if i == 0:
                add("dve", lambda e: e.memset(S32[:, pc, :], 0.0), writes=["S32_%d" % pc])
                add("dve", lambda e: e.memset(Sbf[:, pc, 0, 0, :], 0.0), writes=["Sbf%d_0_0" % pc])
            kvb = []
            for half in range(2):
                kb_, kbkey = sbank()
                kvb.append((kb_, kbkey))
                for q4 in range(4):
                    n = half * 4 + q4
                    blk, ch = n // 2, n % 2
                    mm(kb_[:, q4 * 128:(q4 + 1) * 128], kktmc[:, ch, blk, :], gvraw[:, blk, pc * 128:(pc + 1) * 128], True, True,
                       ["kktm" + sfx, "gvraw"], [kbkey])
            for n in range(8):
                kb_, kbkey = kvb[n // 4]
                q4 = n % 4
                for hh in range(2):
                    r0, r1 = hh * 64, (hh + 1) * 64
                    add("dve", lambda e, n=n, hh=hh, r0=r0, r1=r1, kb_=kb_, q4=q4: e.scalar_tensor_tensor(
                        out=S32[r0:r1, pc, hh * 64:(hh + 1) * 64], in0=S32[r0:r1, pc, hh * 64:(hh + 1) * 64],
                        scalar=decc[r0:r1, n:n + 1], in1=kb_[r0:r1, q4 * 128 + hh * 64:q4 * 128 + (hh + 1) * 64],
                        op0=ALU.mult, op1=ALU.add), reads=["S32_%d" % pc, "dec" + sfx, kbkey], writes=["S32_%d" % pc])
                pn, vn = (par, n + 1) if n < 7 else (1 - par, 0)
                add("act", lambda e, pn=pn, vn=vn: e.copy(out=Sbf[:, pc, pn, vn, :], in_=S32[:, pc, :]),
                    reads=["S32_%d" % pc], writes=["Sbf%d_%d_%d" % (pc, pn, vn)])

        def gla_c(pc):
            qinc, kinc, qoc = qin[pc], kin[pc], qo[pc]
            sfx = "_%d" % pc
            ob, okey = obank()
            for blk in range(NBLK):
                ab, abkey = sbank()
                for hh in range(2):
                    mm(ab[:, hh * 128:(hh + 1) * 128], kinc[:, hh, blk * 128:(blk + 1) * 128], qinc[:, blk * 128:(blk + 1) * 128],
                       True, True, ["kin" + sfx, "qin" + sfx], [abkey])
                add("dve", lambda e, ab=ab: e.tensor_tensor(out=attm[:], in0=ab[:, 0:256].rearrange("p (h c) -> p h c", h=2),
                                                            in1=mask2b[:].unsqueeze(1).to_broadcast([128, 2, 128]), op=ALU.mult),
                    reads=[abkey, "mask2b"], writes=["attm"])
                oc = ob[:, blk * 128:(blk + 1) * 128]
                mm(oc, gvpad[:, blk, pc, 0, :], attm[:, 0, :], True, False, ["gvpad", "attm"], [okey])
                mm(oc, gvpad[:, blk, pc, 1, :], attm[:, 1, :], False, False, ["gvpad", "attm"], [okey])
                for ch in range(2):
                    n = blk * 2 + ch
                    occ = ob[:, n * 64:(n + 1) * 64]
                    mm(occ, Sbf[:, pc, par, n, :], qoc[:, n * 64:(n + 1) * 64], False, ch == 1,
                       ["Sbf%d_%d_%d" % (pc, par, n), "qo" + sfx], [okey])
            add("act", lambda e, ob=ob: e.activation(out=osq[:], in_=ob[:, :], func=AF.Square), reads=[okey], writes=["osq"])
            mm(aux[:, :], bonesb[:], osq[:], True, True, ["bonesb", "osq"], ["aux"])
            add("act", lambda e: e.activation(out=rr[:], in_=aux[:, :], func=AF.Ln, bias=EPS, scale=1.0 / 64), reads=["aux"], writes=["rr"])
            add("act", lambda e: e.activation(out=rr[:], in_=rr[:], func=AF.Exp, scale=-0.5), reads=["rr"], writes=["rr"])
            add("dve", lambda e, ob=ob: e.scalar_tensor_tensor(out=t1[:], in0=ob[:, :], scalar=cvec(vb + 20 + pc), in1=rr[:],
                                                               op0=ALU.mult, op1=ALU.mult), reads=[okey, "rr", "vec"], writes=["t1"])
            add("dve", lambda e: e.tensor_tensor(out=mixT[:, 2 + pc, :], in0=t1[:], in1=sg[:, pc, :], op=ALU.mult),
                reads=["t1", "sg%d" % pc], writes=["mix%d" % (2 + pc)])

        nkb = NBLK * (i + 1)
        nseg = (nkb + 15) // 16

        def fox(pr):
            fbs, oaccs = [], []
            for hh in range(2):
                h = 2 * pr + hh
                rot["fb"] = (rot["fb"] + 1) % 2
                fb = fbias[rot["fb"]]
                fbkey = "fbias%d" % rot["fb"]
                add("dve", lambda e, fb=fb, h=h: e.tensor_scalar(
                    out=fb[:, 0:nkb], in0=negF[:, 0:nkb * 8].rearrange("p (b h) -> p b h", h=8)[:, :, h],
                    scalar1=nref[:, h:h + 1], scalar2=None, op0=ALU.subtract),
                    reads=["negF", "nref"], writes=[fbkey])
                if split and i >= NP:
                    add("dve", lambda e, fb=fb: e.tensor_tensor(out=fb[:, 0:nkb], in0=fb[:, 0:nkb], in1=pcs[:, 0:nkb], op=ALU.add),
                        reads=[fbkey, "pcs"], writes=[fbkey])
                add("dve", lambda e, fb=fb: e.tensor_scalar(out=fb[:, 0:nkb], in0=fb[:, 0:nkb], scalar1=78.0, scalar2=None, op0=ALU.min),
                    reads=[fbkey], writes=[fbkey])
                fbs.append((fb, fbkey))
                oaccs.append(obank())
            seg_slot = {}

            def seg_load(s_):
                if s_ >= nseg or s_ in seg_slot:
                    return
                nb = min(16, nkb - s_ * 16)
                rot["kv"] = (rot["kv"] + 1) % 2
                sl = rot["kv"]
                seg_slot[s_] = sl
                tiles_in = list(range(s_ * 4, min(s_ * 4 + 4, i + 1)))
                add("sp", lambda e, s_=s_, nb=nb, sl=sl: e.dma_start(out=kseg[sl][:, 0:nb * 128],
                                                                    in_=kc_d[pr, :, s_ * 2048:s_ * 2048 + nb * 128]),
                    reads=["kc%d_%d" % (pr, t) for t in tiles_in], writes=["kseg%d" % sl], dma_slot="kseg%d" % sl)
                add("sp", lambda e, s_=s_, nb=nb, sl=sl: e.dma_start(out=vseg[sl][:, 0:nb, :],
                                                                    in_=vc_d[pr, :, s_ * 16:s_ * 16 + nb, :]),
                    reads=["vc%d_%d" % (pr, t) for t in tiles_in], writes=["vseg%d" % sl], dma_slot="vseg%d" % sl)

            steps = []
            for s_ in range(nseg):
                nb = min(16, nkb - s_ * 16)
                for kk in range(nb):
                    steps.append({"s": s_, "kk": kk})
            spairs = [("s0", "s1"), ("m1", "m2")]

            def emit_S(st, n_):
                s_, kk = st["s"], st["kk"]
                if s_ not in seg_slot:
                    seg_load(s_)
                if kk == 0:
                    seg_load(s_ + 1)
                sl = seg_slot[s_]
                kb = s_ * 16 + kk
                jd = kb - NBLK * i
                cc0 = 128 * jd if jd > 0 else 0
                bks = spairs[n_ % 2]
                st.update(sl=sl, kb=kb, jd=jd, cc0=cc0, bks=bks)
                for hh in range(2):
                    r0, r1 = hh * 64, (hh + 1) * 64
                    mm(banks[bks[hh]][:, cc0:TT], kseg[sl][r0:r1, kk * 128:(kk + 1) * 128], qT[r0:r1, pr, cc0:TT], True, jd < 0,
                       ["kseg%d" % sl, "qT%d" % pr], [bks[hh]])
                if jd >= 0:
                    for hh in range(2):
                        mm(banks[bks[hh]][:, cc0:TT], identb[:], m0b[:, 0:TT - cc0], False, True, ["identb", "m0b"], [bks[hh]])

            def emit_rest(st):
                kk, sl, kb, cc0, bks = st["kk"], st["sl"], st["kb"], st["cc0"], st["bks"]
                pts = []
                for hh in range(2):
                    fb, fbkey = fbs[hh]
                    rot["pt"] = (rot["pt"] + 1) % 4
                    pt = PT[rot["pt"]]
                    ptkey = "PT%d" % rot["pt"]
                    pts.append((pt, ptkey))
                    sb_ = banks[bks[hh]]
                    add("act", lambda e, pt=pt, sb_=sb_, fb=fb: e.activation(out=pt[:, cc0:TT], in_=sb_[:, cc0:TT], func=AF.Exp,
                                                                            bias=fb[:, kb:kb + 1], scale=1.0),
                        reads=[bks[hh], fbkey], writes=[ptkey])
                for hh in range(2):
                    oacc, oakey = oaccs[hh]
                    pt, ptkey = pts[hh]
                    mm(oacc[0:65, cc0:TT], vseg[sl][:, kk, hh * 65:(hh + 1) * 65], pt[:, cc0:TT], kb == 0, kb == nkb - 1,
                       ["vseg%d" % sl, ptkey], [oakey])

            emit_S(steps[0], 0)
            for n_ in range(len(steps)):
                if n_ + 1 < len(steps):
                    emit_S(steps[n_ + 1], n_ + 1)
                emit_rest(steps[n_])
            for hh in range(2):
                r0, r1 = hh * 64, (hh + 1) * 64
                oacc, oakey = oaccs[hh]
                rot["osb"] = (rot["osb"] + 1) % 2
                os_ = osb[rot["osb"]]
                oskey = "osb%d" % rot["osb"]
                add("dve", lambda e, os_=os_, oacc=oacc: e.tensor_copy(out=os_[0:65, :], in_=oacc[0:65, :]), reads=[oakey], writes=[oskey])
                add("dve", lambda e, os_=os_: e.tensor_scalar(out=os_[64:65, :], in0=os_[64:65, :], scalar1=1e-37, scalar2=None, op0=ALU.max),
                    reads=[oskey], writes=[oskey])
                add("dve", lambda e, os_=os_: e.reciprocal(out=os_[64:65, :], in_=os_[64:65, :]), reads=[oskey], writes=[oskey])
                mm(aux[0:64, :], onesf[64:65, 0:64], os_[64:65, :], True, True, ["onesf", oskey], ["aux"])
                add("dve", lambda e, os_=os_, pr=pr, r0=r0, r1=r1: e.tensor_tensor(out=mixT[r0:r1, 4 + pr, :], in0=os_[0:64, :],
                                                                                  in1=aux[0:64, :], op=ALU.mult),
                    reads=[oskey, "aux"], writes=["mix%d" % (4 + pr)])


        def pool_hist(j):
            add("pool", lambda e, j=j: e.tensor_copy(out=U[:, j, 0:16], in_=U[:, j, TT:TT + 16]),
                reads=["U%d" % j], writes=["Uh%d" % j])

        if lite:
            order = [(ip_gs, ()), (f_chain, ()), (ip_g1, ()), (gla_a, (0,)), (gla_a, (1,)), (ip_g2, ()), (ip_pool, ()),
                     (ip_fk, ()), (ip_fv, ()), (pool_hist, (0,)), (pool_hist, (1,)), (gla_b, (0,)), (gla_b, (1,))]
        else:
            order = [(ip_gs, ()), (f_chain, ()), (ip_g1, ()), (gla_a, (0,)), (gla_a, (1,)), (ip_g2, ()), (ip_pool, ()),
                     (ip_fq, ()), (ip_fk, ()), (ip_fv, ()), (pool_elem, (0,)), (pool_elem, (1,)), (gla_b, (0,)), (gla_b, (1,)),
                     (fox, (0,)), (gla_c, (0,)), (fox, (1,)), (gla_c, (1,)), (fox, (2,)), (pool_mm, (0,)), (pool_mm, (1,)),
                     (fox, (3,))]
        for fn_, args_ in order:
            P.tag = fn_.__name__
            fn_(*args_)
        if lite:
            return
        P.tag = "wo"

        for og in range(2):
            W, wk = wload(l, "wo%d" % og)
            for jj in range(4):
                j = og * 4 + jj
                bk, bkey = mbank()
                for kc in range(KC):
                    mm(bk[:, :], W[:, kc, jj * 128:(jj + 1) * 128], mixT[:, kc, :], kc == 0, kc == KC - 1, [wk, "mix%d" % kc], [bkey])
                add("dve", lambda e, j=j, bk=bk: e.tensor_tensor(out=xT[:, j, :], in0=xT[:, j, :], in1=bk[:, :], op=ALU.add),
                    reads=[bkey, "xT%d" % j], writes=["xT%d" % j])
        P.tag = "norm2"
        norm(vb + 8)
        P.tag = "ffn_gu"
        for g, ncol in enumerate(GU_SIZES):
            Wg, wkg = wload(l, "gate%d" % g)
            Wu, wku = wload(l, "up%d" % g)
            for jj in range(ncol // 128):
                f = g * 4 + jj
                bg, bgkey = mbank()
                for kc in range(KC):
                    mm(bg[:, :], Wg[:, kc, jj * 128:(jj + 1) * 128], hT[:, kc, :], kc == 0, kc == KC - 1, [wkg, "hT%d" % kc], [bgkey])
                bu, bukey = mbank()
                for kc in range(KC):
                    mm(bu[:, :], Wu[:, kc, jj * 128:(jj + 1) * 128], hT[:, kc, :], kc == 0, kc == KC - 1, [wku, "hT%d" % kc], [bukey])
                rot["sg"] = (rot["sg"] + 1) % 2
                sgt = sgate[rot["sg"]]
                sgkey = "Et%d" % rot["sg"]
                add("act", lambda e, sgt=sgt, bg=bg: e.activation(out=sgt[:], in_=bg[:, :], func=AF.Silu), reads=[bgkey], writes=[sgkey])
                add("dve", lambda e, sgt=sgt, bu=bu, f=f: e.tensor_tensor(out=actT[:, f, :], in0=sgt[:], in1=bu[:, :], op=ALU.mult),
                    reads=[sgkey, bukey], writes=["act%d" % f] + (stg_keys if f < 6 else []))
        P.tag = "ffn_dn"
        for j in range(KC):
            W, wk = wload(l, "dn%d" % j)
            bk, bkey = mbank()
            for fc in range(NFC):
                mm(bk[:, :], W[:, fc, :], actT[:, fc, :], fc == 0, fc == NFC - 1, [wk, "act%d" % fc], [bkey])
            add("dve", lambda e, j=j, bk=bk: e.tensor_tensor(out=xT[:, j, :], in0=xT[:, j, :], in1=bk[:, :], op=ALU.add),
                reads=[bkey, "xT%d" % j], writes=["xT%d" % j])
        P.tag = "out"
        if last_layer:
            norm(NVL * depth, final=True)
            co = c0 - NP * TT
            add(ST_ENG, lambda e: e.dma_start(out=outT_d.rearrange("(c p) t -> p c t", p=128)[:, :, co:co + TT], in_=yout),
                reads=["act%d" % c for c in range(2 * KC)], writes=["outd"], dma_slot="yout")
        else:
            add(ST_ENG, lambda e: e.dma_start(out=x1_d.rearrange("(c p) t -> p c t", p=128)[:, :, c0:c0 + TT], in_=xT[:]),
                reads=["xT%d" % c for c in range(KC)], writes=["x1d%d" % i], dma_slot="x1st")

    try:
        if stop == "conv":
            raise _Stop()
        for l in range(depth):
            for i in range(NT):
                tile(l, i)
    except _Stop:
        pass

    P.emit()
    P.close()
    return nc


def _regroup(w, c0, c1):
    K = w.shape[0]
    kc = K // 128
    return np.ascontiguousarray(w[:, c0:c1].reshape(kc, 128, c1 - c0).transpose(1, 0, 2)).reshape(128, kc * (c1 - c0))


def host_weights(w_in, w_o, w_gu, w_down):
    depth = w_in.shape[0]
    out = np.empty((depth, 128, WTOT), np.float32)
    for l in range(depth):
        wi = w_in[l]
        parts = [
            _regroup(wi, 0, 256),
            _regroup(wi, 256, 768),
            _regroup(wi, 768, 1280),
            _regroup(np.concatenate([wi[:, 1280:1296], wi[:, 2832:2840]], axis=1), 0, 24),
            _regroup(wi, 1296, 1808),
            _regroup(wi, 1808, 2320),
            _regroup(wi, 2320, 2832),
        ]
        for og in range(2):
            parts.append(_regroup(w_o[l], og * 512, (og + 1) * 512))
        o = 0
        for n in GU_SIZES:
            parts.append(_regroup(w_gu[l], o, o + n))
            o += n
        o = 0
        for n in GU_SIZES:
            parts.append(_regroup(w_gu[l], DFF + o, DFF + o + n))
            o += n
        for j in range(8):
            parts.append(_regroup(w_down[l], j * 128, (j + 1) * 128))
        out[l] = np.concatenate(parts, axis=1)
    return out


def host_consts():
    c = np.zeros((128, NCONST), np.float32)
    k = np.arange(128)[:, None]
    q = np.arange(512)[None, :]
    c[:, C_M0:C_M0 + 512] = np.where(q >= k, 0.0, NEG)
    s = np.arange(128)[:, None]
    cc = np.arange(128)[None, :]
    c[:, C_MASK2:C_MASK2 + 128] = ((s // 64 == cc // 64) & (s <= cc)).astype(np.float32)
    seg = np.ones(512, np.float32)
    seg[::64] = 0.0
    c[:, C_SEG:C_SEG + 512] = seg[None, :]
    c[:, C_IDENT:C_IDENT + 128] = np.eye(128, dtype=np.float32)
    c[:, C_TRIU:C_TRIU + 128] = (s <= cc).astype(np.float32)
    c[127, C_SEL127:C_SEL127 + 128] = 1.0
    c[:, C_BONES:C_BONES + 128] = (s // 64 == cc // 64).astype(np.float32)
    wins = [[2, 4], [8, 16]]
    for j in range(2):
        for hh in range(2):
            w = wins[j][hh]
            c[hh * 64:(hh + 1) * 64, C_INVW + j] = 1.0 / w
            t = np.arange(16)
            c[hh * 64:(hh + 1) * 64, C_INVCNT + 16 * j:C_INVCNT + 16 * j + 16] = (1.0 / np.minimum(t + 1, w))[None, :]
    return c


def host_vecs(ln1, ln2, pool_scale, b_a, gla_gn, ln_f):
    depth = ln1.shape[0]
    v = np.zeros((128, NVL * depth + 8), np.float32)
    for l in range(depth):
        b = NVL * l
        v[:, b:b + 8] = ln1[l].reshape(8, 128).T
        v[:, b + 8:b + 16] = ln2[l].reshape(8, 128).T
        v[:, b + 16:b + 18] = pool_scale[l].reshape(2, 128).T
        v[:, b + 18:b + 20] = b_a[l].reshape(2, 128).T
        v[:, b + 20:b + 22] = gla_gn[l].reshape(2, 128).T
    v[:, NVL * depth:] = ln_f.reshape(8, 128).T
    return v


_PROG_CACHE = {}


def kernel(x, ln1, w_in, w_pool, pool_scale, w_a_up, b_a, gla_gn, b_f, w_o, ln2, w_gu, w_down, ln_f):
    x = np.asarray(x, np.float32)
    B, T, _ = x.shape
    depth = np.asarray(w_in).shape[0]
    n = 8
    key = (T, depth)
    if key not in _PROG_CACHE:
        import os
        _PROG_CACHE[key] = build_program(T, depth, stop=os.environ.get("KSTOP"))
    nc = _PROG_CACHE[key]
    wall = host_weights(np.asarray(w_in, np.float32), np.asarray(w_o, np.float32), np.asarray(w_gu, np.float32),
                        np.asarray(w_down, np.float32))
    vecs = host_vecs(np.asarray(ln1, np.float32), np.asarray(ln2, np.float32), np.asarray(pool_scale, np.float32),
                     np.asarray(b_a, np.float32), np.asarray(gla_gn, np.float32), np.asarray(ln_f, np.float32))
    consts = host_consts()
    bfb = np.ascontiguousarray(np.broadcast_to(np.tile(np.asarray(b_f, np.float32), (1, 4))[:, None, :], (depth, 128, 32)))
    in_maps = []
    NBT = T // 128
    half = T // 2
    wins = [[2, 4], [8, 16]]
    for c in range(n):
        b, h = (c // 2) % B, c % 2
        wall_c = np.concatenate([wall, np.full((depth, 128, 4), float(c), np.float32)], axis=2)
        if h == 1:
            xT_c = np.ascontiguousarray(x[b].T)
        else:
            xT_c = np.concatenate([np.zeros((D, half), np.float32), x[b, :half].T], axis=1)
        pc = np.zeros((128, NBT + 64), np.float32)
        if h == 0:
            pc[:, :NBT // 2] = -60.0
        for which in range(2):
            start = (which == 0) or (h == 0)
            for j in range(2):
                for hh in range(2):
                    w = wins[j][hh]
                    t = np.arange(16)
                    cnt = np.minimum(t + 1, w) if start else np.full(16, w)
                    pc[hh * 64:(hh + 1) * 64, NBT + which * 32 + 16 * j:NBT + which * 32 + 16 * j + 16] = (1.0 / cnt)[None, :]
        in_maps.append({
            "xT": xT_c,
            "wall": wall_c,
            "vecs": vecs,
            "wpool": np.asarray(w_pool, np.float32),
            "waup": np.asarray(w_a_up, np.float32),
            "bfb": bfb,
            "consts": consts,
            "percore": pc,
        })
    res = run_bass_kernel_spmd(nc, in_maps, core_ids=list(range(n)))
    out = np.empty((B, T, D), np.float32)
    for c in range(n):
        b, h = (c // 2) % B, c % 2
        out[b, h * half:(h + 1) * half] = np.asarray(res.results[c]["outT"], np.float32).T
    return out
```
